# Optimizing a Trainium2 kernel written in Bass

```python
import math
import jax, jax.numpy as jnp
from jax import lax
import numpy as np

D_MODEL = 4096
BATCH = 4
SEQ = 4096
DEPTH = 1

D_MIX = D_MODEL
D_SSM = D_MIX // 2
SSM_HEAD_DIM = 64
SSM_HEADS = D_SSM // SSM_HEAD_DIM
SSM_GROUPS = 8
SSM_STATE = 128
SSM_CONV = 4
SSM_CHUNK = 128
DT_MIN = 0.001
DT_MAX = 0.1
SSM_NORM_EPS = 1e-5
D_RWKV = D_MIX - D_SSM
RWKV_HEAD_DIM = 64
RWKV_HEADS = D_RWKV // RWKV_HEAD_DIM
DECAY_LORA = 96
AAA_LORA = 96
GATE_LORA = 256
RWKV_GN_EPS = 64e-5
L2_EPS = 1e-12
D_FF = 4 * D_MODEL
D_PLE = 256
LN_EPS = 1e-5
DEEPNORM_ALPHA = (2 * DEPTH) ** 0.25
DEEPNORM_BETA = (8 * DEPTH) ** -0.25

D_XBC = D_SSM + 2 * SSM_GROUPS * SSM_STATE
D_IN_SSM = D_SSM + D_XBC + SSM_HEADS
D_IN_RWKV = 3 * D_RWKV + DECAY_LORA + AAA_LORA + GATE_LORA
D_IN = D_IN_SSM + D_IN_RWKV

kernel_name = "hymba_ssd_rwkv7_deepnorm_block"


def _layer_norm(x, g, b):
    xf = x.astype(jnp.float32)
    mu = jnp.mean(xf, axis=-1, keepdims=True)
    var = jnp.mean(jnp.square(xf - mu), axis=-1, keepdims=True)
    y = (xf - mu) * lax.rsqrt(var + LN_EPS) * g.astype(jnp.float32) + b.astype(jnp.float32)
    return y.astype(x.dtype)


def _causal_depthwise_conv(u, w, b):
    C = u.shape[-1]
    out = lax.conv_general_dilated(
        u, w.astype(u.dtype)[:, None, :], window_strides=(1,),
        padding=[(w.shape[0] - 1, 0)], dimension_numbers=('NWC', 'WIO', 'NWC'),
        feature_group_count=C)
    return out + b.astype(u.dtype)


def _token_shift(u, mu):
    prev = jnp.pad(u, ((0, 0), (1, 0), (0, 0)))[:, :-1, :]
    return u + (prev - u) * mu.astype(u.dtype)


def _ssd_chunked(xs, dt, A, Bm, Cm):
    Bsz, L, H, P = xs.shape
    G, N = Bm.shape[-2:]
    E = H // G
    Q = SSM_CHUNK
    NC = L // Q
    x = (xs * dt[..., None]).reshape(Bsz, NC, Q, G, E, P)
    a = (dt * A).reshape(Bsz, NC, Q, G, E)
    Bc = Bm.reshape(Bsz, NC, Q, G, N)
    Cc = Cm.reshape(Bsz, NC, Q, G, N)
    a_cs = jnp.cumsum(a, axis=2)
    causal = jnp.tril(jnp.ones((Q, Q), dtype=bool))
    seg = a_cs[:, :, :, None] - a_cs[:, :, None, :]
    decay_ls = jnp.exp(jnp.where(causal[:, :, None, None], seg, -jnp.inf))
    scores = jnp.einsum('bclgn,bcsgn->bclsg', Cc, Bc)
    y_diag = jnp.einsum('bclsge,bcsgep->bclgep', scores[..., None] * decay_ls, x)
    decay_to_end = jnp.exp(a_cs[:, :, -1:] - a_cs)
    chunk_states = jnp.einsum('bcsgn,bcsgep->bcgepn', Bc, x * decay_to_end[..., None])
    chunk_decay = jnp.exp(a_cs[:, :, -1])

    def chunk_step(state, inp):
        s_c, d_c = inp
        return state * d_c[..., None, None] + s_c, state

    init = jnp.zeros((Bsz, G, E, P, N), jnp.float32)
    _, states_in = lax.scan(chunk_step, init,
                            (jnp.moveaxis(chunk_states, 1, 0), jnp.moveaxis(chunk_decay, 1, 0)))
    states_in = jnp.moveaxis(states_in, 0, 1)
    y_off = jnp.einsum('bclgn,bcgepn->bclgep', Cc, states_in) * jnp.exp(a_cs)[..., None]
    return (y_diag + y_off).reshape(Bsz, L, H, P)


def _mamba2_group(u, conv_w, conv_b, dt_bias, A_log, D_skip, norm_g):
    Bsz, L, _ = u.shape
    z, xbc, dt_raw = jnp.split(u, [D_SSM, D_SSM + D_XBC], axis=-1)
    xbc = jax.nn.silu(_causal_depthwise_conv(xbc, conv_w, conv_b)).astype(jnp.float32)
    xs, Bm, Cm = jnp.split(xbc, [D_SSM, D_SSM + SSM_GROUPS * SSM_STATE], axis=-1)
    xs = xs.reshape(Bsz, L, SSM_HEADS, SSM_HEAD_DIM)
    Bm = Bm.reshape(Bsz, L, SSM_GROUPS, SSM_STATE)
    Cm = Cm.reshape(Bsz, L, SSM_GROUPS, SSM_STATE)
    dt = jax.nn.softplus(dt_raw.astype(jnp.float32) + dt_bias.astype(jnp.float32))
    A = -jnp.exp(A_log.astype(jnp.float32))
    y = _ssd_chunked(xs, dt, A, Bm, Cm)
    y = y + D_skip.astype(jnp.float32)[:, None] * xs
    v = y.reshape(Bsz, L, D_SSM) * jax.nn.silu(z.astype(jnp.float32))
    v = v.reshape(Bsz, L, SSM_GROUPS, D_SSM // SSM_GROUPS)
    v = v * lax.rsqrt(jnp.mean(jnp.square(v), axis=-1, keepdims=True) + SSM_NORM_EPS)
    v = v.reshape(Bsz, L, D_SSM) * norm_g.astype(jnp.float32)
    return v.astype(u.dtype)


def _wkv7_scan(r, decay, k, v, kk, a):
    Bsz, L, H, N = r.shape

    def step(S, inp):
        r_t, w_t, k_t, v_t, kk_t, a_t = inp
        sa = jnp.einsum('bhij,bhj->bhi', S, kk_t)
        S = (S * w_t[:, :, None, :]
             - sa[..., None] * (kk_t * a_t)[:, :, None, :]
             + v_t[..., None] * k_t[:, :, None, :])
        return S, jnp.einsum('bhij,bhj->bhi', S, r_t)

    seq = tuple(jnp.moveaxis(t, 1, 0) for t in (r, decay, k, v, kk, a))
    S0 = jnp.zeros((Bsz, H, N, N), jnp.float32)
    _, y = lax.scan(step, S0, seq)
    return jnp.moveaxis(y, 0, 1)


def _rwkv7_group(u, mu, w0, w_decay_b, a0, w_aaa_b, w_gate_b, k_k, k_a, r_k, gn_g, gn_b):
    Bsz, L, _ = u.shape
    H, N = RWKV_HEADS, RWKV_HEAD_DIM
    f32 = jnp.float32
    s = _token_shift(u, mu).astype(f32)
    r, k, v, xw, xa, xg = jnp.split(
        s, [D_RWKV, 2 * D_RWKV, 3 * D_RWKV, 3 * D_RWKV + DECAY_LORA,
            3 * D_RWKV + DECAY_LORA + AAA_LORA], axis=-1)
    w_log = -jax.nn.softplus(-(w0.astype(f32) + jnp.tanh(xw) @ w_decay_b.astype(f32))) - 0.5
    decay = jnp.exp(-jnp.exp(w_log))
    a = jax.nn.sigmoid(a0.astype(f32) + xa @ w_aaa_b.astype(f32))
    g = jax.nn.sigmoid(xg) @ w_gate_b.astype(f32)
    kk = (k * k_k.astype(f32)).reshape(Bsz, L, H, N)
    kk = kk / jnp.maximum(jnp.linalg.norm(kk, axis=-1, keepdims=True), L2_EPS)
    k = k * (1.0 + (a - 1.0) * k_a.astype(f32))
    hd = lambda t: t.reshape(Bsz, L, H, N)
    r, k, v, decay, a = hd(r), hd(k), hd(v), hd(decay), hd(a)
    y = _wkv7_scan(r, decay, k, v, kk, a)
    m = jnp.mean(y, axis=-1, keepdims=True)
    var = jnp.mean(jnp.square(y - m), axis=-1, keepdims=True)
    y = (y - m) * lax.rsqrt(var + RWKV_GN_EPS)
    y = y * gn_g.astype(f32).reshape(H, N) + gn_b.astype(f32).reshape(H, N)
    y = y + jnp.sum(r * k * r_k.astype(f32), axis=-1, keepdims=True) * v
    y = y.reshape(Bsz, L, D_RWKV) * g
    return y.astype(u.dtype)


def setup_inputs(seed: int = 0) -> dict:
    key = jax.random.key(seed)
    ks = jax.random.split(key, 32)
    f32 = jnp.float32
    nrm = lambda k, shape, scale: jax.random.normal(k, shape, f32) * scale
    beta = DEEPNORM_BETA

    x = nrm(ks[0], (BATCH, SEQ, D_MODEL), 1.0)
    p = nrm(ks[1], (DEPTH, BATCH, SEQ, D_PLE), 1.0)

    col_scale = jnp.ones((D_IN,), f32)
    col_scale = col_scale.at[D_SSM:2 * D_SSM].set(beta)
    col_scale = col_scale.at[D_IN_SSM + 2 * D_RWKV:D_IN_SSM + 3 * D_RWKV].set(beta)
    w_in = nrm(ks[2], (DEPTH, D_MODEL, D_IN), D_MODEL ** -0.5) * col_scale

    conv_w = nrm(ks[3], (DEPTH, SSM_CONV, D_XBC), SSM_CONV ** -0.5)
    conv_b = nrm(ks[4], (DEPTH, D_XBC), 0.01)
    u = jax.random.uniform(ks[5], (DEPTH, SSM_HEADS), f32)
    dt0 = jnp.exp(u * (math.log(DT_MAX) - math.log(DT_MIN)) + math.log(DT_MIN))
    dt_bias = dt0 + jnp.log(-jnp.expm1(-dt0))
    A_log = jnp.log(jax.random.uniform(ks[6], (DEPTH, SSM_HEADS), f32, 1.0, 16.0))
    D_skip = 1.0 + nrm(ks[7], (DEPTH, SSM_HEADS), 0.1)
    ssm_norm_g = 1.0 + nrm(ks[8], (DEPTH, D_SSM), 0.05)

    rwkv_mu = jax.random.uniform(ks[9], (DEPTH, D_IN_RWKV), f32)
    w0 = jax.random.uniform(ks[10], (DEPTH, D_RWKV), f32, -6.0, 0.0)
    w_decay_b = nrm(ks[11], (DEPTH, DECAY_LORA, D_RWKV), 0.5 * DECAY_LORA ** -0.5)
    a0 = nrm(ks[12], (DEPTH, D_RWKV), 0.1)
    w_aaa_b = nrm(ks[13], (DEPTH, AAA_LORA, D_RWKV), AAA_LORA ** -0.5)
    w_gate_b = nrm(ks[14], (DEPTH, GATE_LORA, D_RWKV), GATE_LORA ** -0.5)
    k_k = 0.85 + nrm(ks[15], (DEPTH, D_RWKV), 0.05)
    k_a = 1.0 + nrm(ks[16], (DEPTH, D_RWKV), 0.05)
    r_k = -0.04 + nrm(ks[17], (DEPTH, RWKV_HEADS, RWKV_HEAD_DIM), 0.02)
    gn_g = 1.0 + nrm(ks[18], (DEPTH, D_RWKV), 0.05)
    gn_b = nrm(ks[19], (DEPTH, D_RWKV), 0.01)

    w_out = nrm(ks[20], (DEPTH, D_MIX, D_MODEL), D_MIX ** -0.5 * beta)
    ln1_g = 1.0 + nrm(ks[21], (DEPTH, D_MODEL), 0.05)
    ln1_b = nrm(ks[22], (DEPTH, D_MODEL), 0.01)
    w_up = nrm(ks[23], (DEPTH, D_MODEL, D_FF), D_MODEL ** -0.5 * beta)
    w_down = nrm(ks[24], (DEPTH, D_FF, D_MODEL), D_FF ** -0.5 * beta)
    ln2_g = 1.0 + nrm(ks[25], (DEPTH, D_MODEL), 0.05)
    ln2_b = nrm(ks[26], (DEPTH, D_MODEL), 0.01)
    w_ple = nrm(ks[27], (DEPTH, D_PLE, D_MODEL), D_PLE ** -0.5 * beta)
    w_ple_gate = nrm(ks[28], (DEPTH, D_MODEL, D_MODEL), D_MODEL ** -0.5)
    ln3_g = 1.0 + nrm(ks[29], (DEPTH, D_MODEL), 0.05)
    ln3_b = nrm(ks[30], (DEPTH, D_MODEL), 0.01)
    return {
        "x": x, "p": p, "w_in": w_in,
        "conv_w": conv_w, "conv_b": conv_b, "dt_bias": dt_bias, "A_log": A_log,
        "D_skip": D_skip, "ssm_norm_g": ssm_norm_g,
        "rwkv_mu": rwkv_mu, "w0": w0, "w_decay_b": w_decay_b, "a0": a0,
        "w_aaa_b": w_aaa_b, "w_gate_b": w_gate_b, "k_k": k_k, "k_a": k_a,
        "r_k": r_k, "gn_g": gn_g, "gn_b": gn_b,
        "w_out": w_out, "ln1_g": ln1_g, "ln1_b": ln1_b,
        "w_up": w_up, "w_down": w_down, "ln2_g": ln2_g, "ln2_b": ln2_b,
        "w_ple": w_ple, "w_ple_gate": w_ple_gate, "ln3_g": ln3_g, "ln3_b": ln3_b,
    }


def reference(x, p, w_in, conv_w, conv_b, dt_bias, A_log, D_skip, ssm_norm_g,
              rwkv_mu, w0, w_decay_b, a0, w_aaa_b, w_gate_b, k_k, k_a, r_k, gn_g, gn_b,
              w_out, ln1_g, ln1_b, w_up, w_down, ln2_g, ln2_b,
              w_ple, w_ple_gate, ln3_g, ln3_b):
    h = x
    for i in range(DEPTH):
        u = h @ w_in[i]
        y_ssm = _mamba2_group(u[..., :D_IN_SSM], conv_w[i], conv_b[i], dt_bias[i],
                              A_log[i], D_skip[i], ssm_norm_g[i])
        y_rwkv = _rwkv7_group(u[..., D_IN_SSM:], rwkv_mu[i], w0[i], w_decay_b[i], a0[i],
                              w_aaa_b[i], w_gate_b[i], k_k[i], k_a[i], r_k[i], gn_g[i], gn_b[i])
        mix = jnp.concatenate([y_ssm, y_rwkv], axis=-1) @ w_out[i]
        h = _layer_norm(DEEPNORM_ALPHA * h + mix, ln1_g[i], ln1_b[i])
        ff = jnp.square(jax.nn.relu(h @ w_up[i])) @ w_down[i]
        h = _layer_norm(DEEPNORM_ALPHA * h + ff, ln2_g[i], ln2_b[i])
        e = (p[i] @ w_ple[i]) * jax.nn.sigmoid(h @ w_ple_gate[i])
        h = _layer_norm(DEEPNORM_ALPHA * h + e, ln3_g[i], ln3_b[i])
    return h
```

```python
import math
import numpy as np
import concourse.bass as bass
import concourse.mybir as mybir
from concourse.bass_utils import run_bass_kernel_spmd

F32, BF16 = mybir.dt.float32, mybir.dt.bfloat16
AF = mybir.ActivationFunctionType
ALU = mybir.AluOpType
AX = mybir.AxisListType

D = 4096
SEQ = 4096
NT = 8
TT = 512
NCH = 32
ALPHA = 2.0 ** 0.25
CDEC = math.exp(-0.5)
NBLK = 53
NFM = 44
B_XS, B_B, B_C, B_R, B_K, B_V, B_XW, B_XA, B_XG = 0, 8, 12, 16, 24, 32, 40, 41, 42
BLK_Z, BLK_DT = 44, 52
FMW = [128] * 40 + [96, 96, 128, 128]
UPAD = 4
NTM = 1040


class Sched:
    ENG = ['pe', 'act', 'dve', 'pool', 'sp']

    def __init__(self, nc):
        self.nc = nc
        self.ops = []
        self.buf = {}
        self.emitted = 0
        self.cnt = {}
        self.waited = {e: {} for e in self.ENG}
        self.sems = {}
        self.lastq = {}

    def op(self, eng, fn, reads=(), writes=(), stream=None, inc=16):
        oid = len(self.ops)
        deps = set()
        for b in reads:
            st = self.buf.get(b)
            if st and st[0] is not None:
                deps.add(st[0])
        for b in writes:
            st = self.buf.get(b)
            if st:
                if st[0] is not None:
                    deps.add(st[0])
                deps.update(st[1])
        for b in reads:
            self.buf.setdefault(b, [None, []])[1].append(oid)
        for b in writes:
            self.buf[b] = [oid, []]
        if stream:
            prev = self.lastq.get(stream)
            if prev is not None:
                deps.add(prev)
            self.lastq[stream] = oid
        deps.discard(oid)
        self.ops.append(dict(eng=eng, fn=fn, deps=deps, stream=stream, inc=inc, sig=None))
        return oid

    def sem(self, k):
        if k not in self.sems:
            self.sems[k] = self.nc.alloc_semaphore(name="sem_%s_%s" % k)
        return self.sems[k]

    def flush(self, final_streams=()):
        ops = self.ops[self.emitted:]
        base = self.emitted
        n = len(self.ops)
        dependents = [False] * n
        for o in ops:
            for d in o['deps']:
                dependents[d] = True
        for i, o in enumerate(ops):
            if o['stream']:
                k = ('s', o['stream'])
                self.cnt[k] = self.cnt.get(k, 0) + o['inc']
                o['sig'] = (k, self.cnt[k])
            elif dependents[base + i] or i == len(ops) - 1:
                k = ('e', o['eng'])
                self.cnt[k] = self.cnt.get(k, 0) + 1
                o['sig'] = (k, self.cnt[k])
        final_streams = [k[1] for k in self.cnt if k[0] == 's' and (not k[1].startswith('cast') or 'cast' in final_streams)]
        streams_final = [(('s', s), self.cnt.get(('s', s), 0)) for s in final_streams]
        with self.nc.Block() as block:
            engs = {'pe': block.tensor, 'act': block.scalar, 'dve': block.vector,
                    'pool': block.gpsimd, 'sp': block.sync}
            for e in self.ENG:
                mine = [o for o in ops if o['eng'] == e]
                if not mine and e != 'sp':
                    continue

                def body(engine, mine=mine, e=e):
                    waited = self.waited[e]
                    for o in mine:
                        for d in sorted(o['deps']):
                            if self.ops[d]['sig'] is None:
                                continue
                            k, v = self.ops[d]['sig']
                            if waited.get(k, 0) < v:
                                engine.wait_ge(self.sem(k), v)
                                waited[k] = v
                        inst = o['fn'](engine)
                        if o['sig'] is not None:
                            inst.then_inc(self.sem(o['sig'][0]), o['inc'] if o['stream'] else 1)
                    if e == 'sp':
                        for k, v in streams_final:
                            if v and waited.get(k, 0) < v:
                                engine.wait_ge(self.sem(k), v)
                                waited[k] = v
                engs[e](body)
        self.emitted = n


DEBUG_STAGE = 0


def build_nc():
    nc = bass.Bass("TRN2", target_bir_lowering=False)

    def din(name, shape):
        return nc.dram_tensor(name, list(shape), F32, kind="ExternalInput").ap()

    xT = din("xT", [D, SEQ])
    w_in = din("w_in", [NBLK * 128, 4096])
    w_out = din("w_out", [32 * 128, 4096])
    w_up = din("w_up", [128 * 128, 4096])
    w_down = din("w_down", [8 * 32 * 128 // 2, 4096])
    w_gate = din("w_gate", [32 * 128, 4096])
    w_ple = din("w_ple", [256, 4096])
    pT = din("pT", [128, 2, 2048])
    cw = din("cw", [128, 16, 4])
    cb = din("cb", [128, 16])
    mu = din("mu", [128, 28])
    chp = din("chp", [128, 5, 8])
    tmb = din("tmb", [128, 4, 1024])
    dtp = din("dtp", [128, 2, 16])
    wdb = din("wdb", [96, 1024])
    wab = din("wab", [96, 1024])
    wgb = din("wgb", [128, 2, 1024])
    lnp = din("lnp", [128, 6, 32])
    xTr = din("xTr", [D, 2048])
    sel = din("sel", [128, 2])
    outT = nc.dram_tensor("outT", [D, 2048], F32, kind="ExternalOutput").ap()

    xT_bf = nc.dram_tensor("xT_bf", [D, SEQ], BF16).ap()
    w_in_bf = nc.dram_tensor("w_in_bf", [NBLK * 128, 4096], BF16).ap()
    w_out_bf = nc.dram_tensor("w_out_bf", [32 * 128, 4096], BF16).ap()
    w_up_bf = nc.dram_tensor("w_up_bf", [128 * 128, 4096], BF16).ap()
    w_down_bf = nc.dram_tensor("w_down_bf", [8 * 32 * 64, 4096], BF16).ap()
    w_gate_bf = nc.dram_tensor("w_gate_bf", [32 * 128, 4096], BF16).ap()
    w_ple_bf = nc.dram_tensor("w_ple_bf", [256, 4096], BF16).ap()
    uF = nc.dram_tensor("uF", [NFM, 128, UPAD + SEQ], F32).ap()
    uT = nc.dram_tensor("uT", [SEQ, NTM], F32).ap()
    agin = [nc.dram_tensor("agin%d" % t, [2048, 256], F32) for t in range(NT)]
    agout = [nc.dram_tensor("agout%d" % t, [4096, 256], F32) for t in range(NT)]

    S = Sched(nc)

    def sb(name, shape, dt):
        return nc.sbuf_tensor(name, list(shape), dt)

    ncast = [0]

    castq = []

    def cast(*args):
        castq.append(args)

    def issue_casts(n):
        for _ in range(min(n, len(castq))):
            cast_now(*castq.pop(0))

    def cast_now(dst, src, r0, r1, key, c0=None, c1=None):
        slot = ('castslot', ncast[0] % 2)
        cstream = 'cast%d' % (ncast[0] % 2)
        ncast[0] += 1
        if c0 is None:
            S.op('pool', lambda e: e.dma_start(out=dst[r0:r1, :], in_=src[r0:r1, :], max_dma_last_dim=4096),
                 writes=[key, slot], stream=cstream)
        else:
            S.op('pool', lambda e: e.dma_start(out=dst[r0:r1, c0:c1], in_=src[r0:r1, c0:c1]),
                 writes=[key, slot], stream=cstream)

    def cast_rows(dst, src, nrows, name, step=1024):
        for r0 in range(0, nrows, 512):
            cast(dst, src, r0, min(nrows, r0 + 512), (name, r0 // step))

    cast(xT_bf, xT, 0, 2048, ('xTbf', 0), 0, TT)
    cast(xT_bf, xT, 2048, D, ('xTbf', 0), 0, TT)
    cast_rows(w_in_bf, w_in, NBLK * 128, 'winbf')
    for t in range(1, NT):
        cast(xT_bf, xT, 0, 2048, ('xTbf', t), t * TT, (t + 1) * TT)
        cast(xT_bf, xT, 2048, D, ('xTbf', t), t * TT, (t + 1) * TT)
    cast_rows(w_out_bf, w_out, 4096, 'woutbf')
    cast_rows(w_up_bf, w_up, 16384, 'wupbf')
    cast_rows(w_down_bf, w_down, 16384, 'wdownbf')
    cast_rows(w_gate_bf, w_gate, 4096, 'wgatebf')
    cast_rows(w_ple_bf, w_ple, 256, 'wplebf')
    issue_casts(57)

    with (nc.psum_tensor("pb0", [128, 512], F32) as pb0, nc.psum_tensor("pb1", [128, 512], F32) as pb1,
          nc.psum_tensor("pb2", [128, 512], F32) as pb2, nc.psum_tensor("pb3", [128, 512], F32) as pb3,
          nc.psum_tensor("pb4", [128, 512], F32) as pb4, nc.psum_tensor("pb5", [128, 512], F32) as pb5,
          nc.psum_tensor("pb6", [128, 512], F32) as pb6, nc.psum_tensor("pb7", [128, 512], F32) as pb7):
        PB = [pb0, pb1, pb2, pb3, pb4, pb5, pb6, pb7]

        def pk(i):
            return ('psum', i)

        if phase_a2(nc, S, PB, pk, sb, locals()):
            return nc
        phase_b(nc, S, PB, pk, sb, locals())
    return nc


def phase_a2(nc, S, PB, pk, sb, G):
    for which in (0, 1):
        _mixer_pass(nc, S, PB, pk, sb, G, which)
        if DEBUG_STAGE == 2 + which:
            dbgA = nc.dram_tensor("dbgA", [NT, 2048, 256], F32, kind="ExternalOutput").ap()
            dbgO = nc.dram_tensor("dbgO", [NT, 4096, 256], F32, kind="ExternalOutput").ap()
            for t in range(NT):
                S.op('sp', lambda e, t=t: e.dma_start(out=dbgA[t, :, :], in_=G['agin'][t].ap()[:, :]), writes=[('dbgA', t)], stream='ld')
                if which == 1:
                    S.op('sp', lambda e, t=t: e.dma_start(out=dbgO[t, :, :], in_=G['agout'][t].ap()[:, :]), writes=[('dbgO', t)], stream='ld')
            S.flush(final_streams=['ld', 'st', 'cast', 'cc'])
            return True
    return False


def _mixer_pass(nc, S, PB, pk, sb, G, which):
    SSD = which == 0
    YB = 0 if SSD else 7
    uF, uT, agin, agout = G['uF'], G['uT'], G['agin'], G['agout']
    cw_d, cb_d, mu_d, chp_d, tmb_d, dtp_d = G['cw'], G['cb'], G['mu'], G['chp'], G['tmb'], G['dtp']
    wdb_d, wab_d, wgb_d = G['wdb'], G['wab'], G['wgb']
    from contextlib import ExitStack
    with ExitStack() as es:
        grp = ['both']

        def T(name, shape, dt=F32):
            if grp[0] == 'ssd' and not SSD:
                return None
            if grp[0] == 'rwkv' and SSD:
                return None
            return es.enter_context(sb("a2_%d_" % which + name, shape, dt))
        cwt = T("cw", [128, 16, 4]); cbt = T("cb", [128, 16]); mut = T("mu", [128, 28]); omm = T("omm", [128, 28])
        chp = T("chp", [128, 5, 8]); omka = T("omka", [128, 8]); tmb = T("tmb", [128, 2, 1024])
        dtp = T("dtp", [128, 2, 16]); At = T("At", [128, 16])
        wdb = T("wdb", [96, 8, 128], BF16); wab = T("wab", [96, 8, 128], BF16); wgb = T("wgb", [128, 2, 1024], BF16)
        ones_f = T("ones_f", [128, 128]); mUi = T("mUi", [128, 128]); mUs = T("mUs", [128, 128]); mLs = T("mLs", [128, 128])
        idf = T("idf", [128, 128]); idb = T("idb", [128, 128], BF16); blk1 = T("blk1", [128, 128], BF16)
        sel2 = T("sel2", [128, 2], BF16)
        ssm_st = T("ssm_st", [128, 1024]); ssm_stb = T("ssm_stb", [128, 1024], BF16)
        rst = T("rst", [128, 8, 128]); rstb = T("rstb", [128, 8, 128], BF16)
        pm = T("pm", [128, 2]); blkf = T("blkf", [128, 128])
        ytm = T("ytm", [128, 1024], BF16)
        yT = T("yT", [128, 8, 512], BF16)
        t2 = T("t2", [128, 1024])
        grp[0] = 'ssd'
        usb = T("usb", [128, 16, 131]); cacc = T("cacc", [128, 16, 128]); cfm = T("cfm", [128, 16, 128])
        BCb = T("BCb", [128, 8, 128], BF16)
        ztm = T("ztm", [128, NTM]); zs = T("zs", [128, 1024])
        dt_t = T("dt_t", [128, 16]); a_t = T("a_t", [128, 16]); ea = T("ea", [128, 16]); cd = T("cd", [128, 16])
        dtd = T("dtd", [128, 16])
        xs32 = T("xs32", [128, 1024]); xdt = T("xdt", [128, 1024], BF16); xdtd = T("xdtd", [128, 1024], BF16)
        Btm = T("Btm", [128, 512], BF16)
        Rm = T("Rm", [128, 16, 128]); Em = T("Em", [128, 16, 128]); scm = T("scm", [128, 4, 128])
        MT = T("MT", [128, 16, 128], BF16)
        t1 = T("t1", [128, 1024]); vsq = T("vsq", [128, 1024])
        ss4 = T("ss4", [128, 4]); rs4 = T("rs4", [128, 4])
        grp[0] = 'rwkv'
        BhTz = T("BhTz", [128, 16, 128], BF16); KhTz = T("KhTz", [128, 16, 128], BF16); KtTz = T("KtTz", [128, 16, 128], BF16)
        ur = T("ur", [128, 28, 129]); sr = T("sr", [128, 28, 128])
        lob = T("lob", [128, 4, 128], BF16)
        dsig = T("dsig", [128, 8, 128]); asig = T("asig", [128, 8, 128]); Lr = T("Lr", [128, 8, 128])
        w1 = T("w1", [128, 8, 128]); w2 = T("w2", [128, 8, 128]); w3 = T("w3", [128, 8, 128])
        kap = T("kap", [128, 8, 128]); kpr = T("kpr", [128, 8, 128]); bb = T("bb", [128, 8, 128])
        kk2b = T("kk2b", [128, 8, 128], BF16)
        KR = T("KR", [128, 8, 256], BF16); BhT = T("BhT", [128, 8, 128], BF16); KhT = T("KhT", [128, 8, 128], BF16)
        BGT = T("BGT", [128, 8, 128], BF16); KGT = T("KGT", [128, 8, 128], BF16); rkrb = T("rkrb", [128, 8, 128], BF16)
        gC = T("gC", [128, 8])
        gtm = T("gtm", [128, 1024]); Vtm = T("Vtm", [128, 1024]); Vb = T("Vb", [128, 1024], BF16)
        BGtm = T("BGtm", [128, 1024], BF16); KGtm = T("KGtm", [128, 1024], BF16)
        X0 = T("X0", [128, 16, 128], BF16); X1 = T("X1", [128, 16, 128], BF16)
        XT0 = T("XT0", [128, 16, 128], BF16); XT1 = T("XT1", [128, 16, 128], BF16)
        Gm = T("Gm", [128, 16, 128], BF16); PTm = T("PTm", [128, 16, 128], BF16)
        BmT = T("BmT", [128, 16, 128], BF16); QTm = T("QTm", [128, 16, 128], BF16)
        Zb = T("Zb", [128, 1024], BF16); Ub = T("Ub", [128, 1024], BF16)
        yr = T("yr", [128, 1024]); s16a = T("s16a", [128, 16]); s16b = T("s16b", [128, 16])
        s16c = T("s16c", [128, 16]); s16d = T("s16d", [128, 16]); bs16 = T("bs16", [128, 16])

        def ld(dst, src, key):
            S.op('sp', lambda e: e.dma_start(out=dst, in_=src), writes=[key], stream='ldc')

        ld(cwt[:], cw_d, 'cwt'); ld(cbt[:], cb_d, 'cbt'); ld(mut[:], mu_d, 'mut'); ld(chp[:], chp_d, 'chp')
        ld(tmb[:], tmb_d[:, 0:2, :] if SSD else tmb_d[:, 2:4, :], 'tmb'); ld(dtp[:], dtp_d, 'dtp');
        S.op('pool', lambda e: e.dma_start(out=wdb[:].rearrange("p a b -> p (a b)"), in_=wdb_d), writes=['wdb'], stream='ld2')
        S.op('pool', lambda e: e.dma_start(out=wab[:].rearrange("p a b -> p (a b)"), in_=wab_d), writes=['wab'], stream='ld2')
        S.op('pool', lambda e: e.dma_start(out=wgb[:], in_=wgb_d), writes=['wgb'], stream='ld2')
        V = 'dve'
        S.op(V, lambda e: e.tensor_scalar(out=omm[:], in0=mut[:], scalar1=-1.0, scalar2=1.0, op0=ALU.mult, op1=ALU.add),
             reads=['mut'], writes=['omm'])
        S.op(V, lambda e: e.tensor_scalar(out=omka[:], in0=chp[:, 3, :], scalar1=-1.0, scalar2=1.0, op0=ALU.mult, op1=ALU.add),
             reads=['chp'], writes=['omka'])
        S.op('act', lambda e: e.activation(out=At[:], in_=dtp[:, 1, :], func=AF.Exp), reads=['dtp'], writes=['At0'])
        S.op(V, lambda e: e.tensor_scalar(out=At[:], in0=At[:], scalar1=-1.0, scalar2=None, op0=ALU.mult),
             reads=['At0'], writes=['At'])
        S.op(V, lambda e: e.memset(ones_f[:], 1.0), writes=['ones_f'])

        def mk_mask(dst, key, pat, cm, cmp, dt_is_bf=False):
            S.op('pool', lambda e: e.memset(dst[:], 1.0), writes=[key + '0'])
            S.op('pool', lambda e: e.affine_select(out=dst[:], in_=dst[:], pattern=[[pat, 128]], compare_op=cmp,
                                                     fill=0.0, base=0, channel_multiplier=cm),
                 reads=[key + '0'], writes=[key])
        mk_mask(mUi, 'mUi', 1, -1, ALU.is_ge)
        mk_mask(mUs, 'mUs', 1, -1, ALU.is_gt)
        mk_mask(mLs, 'mLs', -1, 1, ALU.is_gt)
        mk_mask(idf, 'idf', 1, -1, ALU.is_equal)
        S.op(V, lambda e: e.tensor_copy(out=idb[:], in_=idf[:]), reads=['idf'], writes=['idb'])
        S.op('pool', lambda e: e.memset(blkf[:], 0.0), writes=['blk1a'])
        S.op('pool', lambda e: e.memset(blkf[0:64, 0:64], 1.0), reads=['blk1a'], writes=['blk1b'])
        S.op('pool', lambda e: e.memset(blkf[64:128, 64:128], 1.0), reads=['blk1b'], writes=['blkf'])
        S.op(V, lambda e: e.tensor_copy(out=blk1[:], in_=blkf[:]), reads=['blkf'], writes=['blk1'])
        S.op('pool', lambda e: e.memset(pm[:], 0.0), writes=['pma'])
        S.op('pool', lambda e: e.memset(pm[0:64, 0:1], 1.0), reads=['pma'], writes=['pmb'])
        S.op('pool', lambda e: e.memset(pm[64:128, 1:2], 1.0), reads=['pmb'], writes=['pm'])
        S.op('pool', lambda e: e.memset(sel2[:], 0.0), writes=['sel2a'])
        S.op('pool', lambda e: e.memset(sel2[0:64, 0:1], 1.0), reads=['sel2a'], writes=['sel2b'])
        S.op('pool', lambda e: e.memset(sel2[64:128, 1:2], 1.0), reads=['sel2b'], writes=['sel2'])
        S.op(V, lambda e: e.memset(ssm_st[:], 0.0), writes=['ssm_st'])
        S.op(V, lambda e: e.memset(ssm_stb[:], 0.0), writes=['ssm_stb'])
        S.op(V, lambda e: e.memset(rst[:], 0.0), writes=['rst'])
        S.op(V, lambda e: e.memset(rstb[:], 0.0), writes=['rstb'])

        def bc(ap, shape):
            return ap.broadcast_to(shape)

        evi = [0]

        def evac_copy(out, in_, reads, writes):
            evi[0] += 1
            if evi[0] % 2:
                S.op('act', lambda e: e.activation(out=out, in_=in_, func=AF.Copy), reads=reads, writes=writes)
            else:
                S.op('dve', lambda e: e.tensor_copy(out=out, in_=in_), reads=reads, writes=writes)

        def p1a_gen(c):
            tt, q = c // 4, c % 4
            t0 = c * 128
            S.op('sp', lambda e, t0=t0: e.dma_start(
                out=ur[:], in_=uF[16:44, :, UPAD + t0 - 1:UPAD + t0 + 128].rearrange("j p t -> p j t")),
                reads=[('uF', j, tt) for j in range(16, 44)] + ([('uF', j, tt - 1) for j in range(16, 44)] if tt > 0 else [('uFpad', j) for j in range(16, 44)]),
                writes=['ur'], stream='ur')
            S.op(V, lambda e: e.tensor_tensor(out=sr[:], in0=ur[:, :, 1:129], in1=bc(omm[:].unsqueeze(2), [128, 28, 128]), op=ALU.mult),
                 reads=['ur', 'omm'], writes=['sr'])
            S.op('pool', lambda e: e.tensor_tensor(out=ur[:, :, 0:128], in0=ur[:, :, 0:128], in1=bc(mut[:].unsqueeze(2), [128, 28, 128]), op=ALU.mult),
                 reads=['mut'], writes=['ur'])
            S.op(V, lambda e: e.tensor_tensor(out=sr[:], in0=sr[:], in1=ur[:, :, 0:128], op=ALU.add), reads=['ur'], writes=['sr'])
            S.op('act', lambda e: e.activation(out=lob[0:96, 0, :], in_=sr[0:96, 24, :], func=AF.Tanh), reads=['sr'], writes=[('lob', 0)])
            S.op('act', lambda e: e.activation(out=lob[0:96, 1, :], in_=sr[0:96, 25, :], func=AF.Copy), reads=['sr'], writes=[('lob', 1)])
            S.op('act', lambda e: e.activation(out=lob[:, 2:4, :], in_=sr[:, 26:28, :], func=AF.Sigmoid), reads=['sr'], writes=[('lob', 2)])
            yield
            for wh, (wt, li, key) in enumerate([(wdb, 0, 'wdb'), (wab, 1, 'wab')]):
                for half in range(2):
                    bank = wh * 2 + half

                    def mm_l(e, wt=wt, li=li, half=half, bank=bank):
                        for i in range(4):
                            inst = e.matmul(PB[bank][:, i * 128:(i + 1) * 128], wt[0:96, half * 4 + i, :], lob[0:96, li, :],
                                            start=True, stop=True)
                        return inst
                    S.op('pe', mm_l, reads=[key, ('lob', li)], writes=[pk(bank)])
                    dst = dsig if wh == 0 else asig
                    S.op(V, lambda e, dst=dst, half=half, bank=bank, wh=wh: e.tensor_tensor(
                        out=dst[:, half * 4:(half + 1) * 4, :], in0=PB[bank][:].rearrange("p (a t) -> p a t", a=4),
                        in1=bc(chp[:, wh, half * 4:(half + 1) * 4].unsqueeze(2), [128, 4, 128]), op=ALU.add),
                        reads=['chp'], writes=[pk(bank), ('sig', wh, half)])
            S.op('act', lambda e: e.activation(out=dsig[:], in_=dsig[:], func=AF.Sigmoid),
                 reads=[('sig', 0, 0), ('sig', 0, 1)], writes=['dsig'])
            S.op('act', lambda e: e.activation(out=asig[:], in_=asig[:], func=AF.Sigmoid),
                 reads=[('sig', 1, 0), ('sig', 1, 1)], writes=['asig'])
            yield
            for fc in range(8):
                S.op(V, lambda e, fc=fc: e.tensor_tensor_scan(out=Lr[:, fc, :], data0=ones_f[:], data1=dsig[:, fc, :],
                                                               initial=0.0, op0=ALU.mult, op1=ALU.add),
                     reads=['dsig', 'ones_f'], writes=[('Lr', fc)])
            LrK = [('Lr', fc) for fc in range(8)]
            yield
            S.op(V, lambda e: e.tensor_tensor(out=kap[:], in0=sr[:, 8:16, :], in1=bc(chp[:, 2, :].unsqueeze(2), [128, 8, 128]), op=ALU.mult),
                 reads=['sr', 'chp'], writes=['kap'])
            S.op('pool', lambda e: e.tensor_tensor(out=kk2b[:], in0=kap[:], in1=kap[:], op=ALU.mult), reads=['kap'], writes=['kk2b'])
            yield
            for half in range(2):
                bank = 6 + half
                S.op('pe', lambda e, half=half, bank=bank: e.matmul(PB[bank][:], blk1[:], kk2b[:, half * 4:(half + 1) * 4, :], start=True, stop=True),
                     reads=['kk2b', 'blk1'], writes=[pk(bank)])
                S.op('act', lambda e, half=half, bank=bank: e.activation(out=w1[:, half * 4:(half + 1) * 4, :],
                                                                         in_=PB[bank][:].rearrange("p (a t) -> p a t", a=4), func=AF.Sqrt),
                     writes=[pk(bank), ('w1', half)])
            S.op(V, lambda e: e.tensor_scalar(out=w1[:], in0=w1[:], scalar1=1e-12, scalar2=None, op0=ALU.max),
                 reads=[('w1', 0), ('w1', 1)], writes=['w1'])
            S.op(V, lambda e: e.reciprocal(out=w1[:], in_=w1[:]), writes=['w1'])
            S.op(V, lambda e: e.tensor_tensor(out=kap[:], in0=kap[:], in1=w1[:], op=ALU.mult), reads=['w1', 'kk2b'], writes=['kap'])
            yield
            S.op('pool', lambda e: e.tensor_tensor(out=kpr[:], in0=asig[:], in1=bc(chp[:, 3, :].unsqueeze(2), [128, 8, 128]), op=ALU.mult),
                 reads=['asig', 'chp'], writes=['kpr'])
            S.op('pool', lambda e: e.tensor_tensor(out=kpr[:], in0=kpr[:], in1=bc(omka[:].unsqueeze(2), [128, 8, 128]), op=ALU.add),
                 reads=['omka'], writes=['kpr'])
            S.op(V, lambda e: e.tensor_tensor(out=kpr[:], in0=kpr[:], in1=sr[:, 8:16, :], op=ALU.mult), reads=['sr'], writes=['kpr'])
            S.op(V, lambda e: e.tensor_tensor(out=bb[:], in0=kap[:], in1=asig[:], op=ALU.mult), reads=['kap', 'asig'], writes=['bb'])
            yield

        nxt = None
        LrK = [('Lr', fc) for fc in range(8)]

        if SSD:
            xT_bf, w_in_bf = G['xT_bf'], G['w_in_bf']
            grp[0] = 'both'
            xts = T("a1_xT", [128, 32, TT], BF16)
            ring = T("a1_ring", [128, 4, 32, 128], BF16)
            stg = T("a1_st", [128, 4, 512], F32)
            zpad = T("a1_z", [128, UPAD], F32)
            S.op('dve', lambda e: e.memset(zpad[:], 0.0), writes=['zpad'])
            for j in range(NFM):
                S.op('act', lambda e, j=j: e.dma_start(out=uF[j, :, 0:UPAD], in_=zpad[:]),
                     reads=['zpad'], writes=[('uFpad', j)], stream='pad')
            ctr = {'psi': 0, 'sgi': 0, 'rsi': 0}

            def load_blk(blk, slot):
                S.op('sp', lambda e: e.dma_start(
                    out=ring[:, slot, :, :].rearrange("p k c -> p (k c)"),
                    in_=w_in_bf[blk * 128:(blk + 1) * 128, :]),
                    reads=[('winbf', (blk * 128) // 1024)], writes=[('ring', slot)], stream='ring%d' % slot)

            def a1_tile(tt):
                S.op('sp', lambda e: e.dma_start(
                    out=xts[:], in_=xT_bf[:, tt * TT:(tt + 1) * TT].rearrange("(kc p) t -> p kc t", p=128)),
                    reads=[('xTbf', tt)], writes=['xts'], stream='xts')
                for j in range(NFM):
                    slot = ctr['rsi'] % 4
                    ctr['rsi'] += 1
                    load_blk(j, slot)
                    m = FMW[j]
                    bank = 6 + ctr['psi'] % 2
                    ctr['psi'] += 1

                    def mm(e, slot=slot, m=m, bank=bank):
                        for kc in range(32):
                            inst = e.matmul(PB[bank][0:m, :], ring[:, slot, kc, 0:m], xts[:, kc, :],
                                            start=(kc == 0), stop=(kc == 31))
                        return inst
                    S.op('pe', mm, reads=[('ring', slot), 'xts'], writes=[pk(bank)])
                    sg = ctr['sgi'] % 4
                    ctr['sgi'] += 1
                    S.op('act', lambda e, sg=sg, m=m, bank=bank: e.activation(
                        out=stg[0:m, sg, :], in_=PB[bank][0:m, :], func=AF.Copy),
                        writes=[pk(bank), ('stg', sg)])
                    S.op('act', lambda e, sg=sg, m=m, j=j: e.dma_start(
                        out=uF[j, 0:m, UPAD + tt * TT:UPAD + (tt + 1) * TT], in_=stg[0:m, sg, :]),
                        reads=[('stg', sg)], writes=[('uF', j, tt)], stream='stg%d' % sg)
                    yield
                for g5 in range(5):
                    if ctr['rsi'] % 2:
                        ctr['rsi'] += 1
                    s0 = ctr['rsi'] % 4
                    nb = 2 if g5 < 4 else 1
                    for i in range(nb):
                        load_blk(BLK_Z + g5 * 2 + i, s0 + i)
                    ctr['rsi'] += 2
                    ncol = 256 if g5 < 4 else 16
                    for q in range(4):
                        bank = 6 + ctr['psi'] % 2
                        ctr['psi'] += 1

                        def mmt(e, s0=s0, nb=nb, ncol=ncol, bank=bank, q=q):
                            for kc in range(32):
                                if nb == 2:
                                    rhs = ring[:, s0:s0 + 2, kc, :]
                                else:
                                    rhs = ring[:, s0, kc, 0:16]
                                inst = e.matmul(PB[bank][:, 0:ncol], xts[:, kc, q * 128:(q + 1) * 128], rhs,
                                                start=(kc == 0), stop=(kc == 31))
                            return inst
                        S.op('pe', mmt, reads=[('ring', s0 + i) for i in range(nb)] + ['xts'], writes=[pk(bank)])
                        sg = ctr['sgi'] % 4
                        ctr['sgi'] += 1
                        S.op('act', lambda e, sg=sg, ncol=ncol, bank=bank: e.activation(
                            out=stg[:, sg, 0:ncol], in_=PB[bank][:, 0:ncol], func=AF.Copy), writes=[pk(bank), ('stg', sg)])
                        c0 = g5 * 256
                        S.op('act', lambda e, sg=sg, ncol=ncol, c0=c0, q=q: e.dma_start(
                            out=uT[tt * TT + q * 128:tt * TT + (q + 1) * 128, c0:c0 + ncol],
                            in_=stg[:, sg, 0:ncol]),
                            reads=[('stg', sg)], writes=[('uT', tt, q, g5)], stream='stg%d' % sg)
                        yield

        def chunk_gen(c):
            G['issue_casts'](1)
            tt, q = c // 4, c % 4
            t0 = c * 128
            yb = tt % 2
            if SSD:
                S.op('sp', lambda e, t0=t0: e.dma_start(
                    out=usb[:], in_=uF[0:16, :, UPAD + t0 - 3:UPAD + t0 + 128].rearrange("j p t -> p j t")),
                    reads=[('uF', j, tt) for j in range(16)] + ([('uF', j, tt - 1) for j in range(16)] if tt > 0 else [('uFpad', j) for j in range(16)]),
                    writes=['usb'], stream='usb')
                yield
                S.op('sp', lambda e, t0=t0: e.dma_start(out=ztm[:], in_=uT[t0:t0 + 128, :]),
                     reads=[('uT', tt, q, g) for g in range(5)], writes=['ztm'], stream='ztm')
                yield
                for k in range(4):
                    if k == 0:
                        S.op('pool', lambda e: e.tensor_tensor(out=cacc[:], in0=usb[:, :, 0:128],
                                                                in1=bc(cwt[:, :, 0:1], [128, 16, 128]), op=ALU.mult),
                             reads=['usb', 'cwt'], writes=['cacc'])
                    else:
                        S.op('pool', lambda e, k=k: e.tensor_tensor(out=cfm[:], in0=usb[:, :, k:k + 128],
                                                                     in1=bc(cwt[:, :, k:k + 1], [128, 16, 128]), op=ALU.mult),
                             reads=['usb', 'cwt'], writes=['cfm'])
                        S.op(V, lambda e: e.tensor_tensor(out=cacc[:], in0=cacc[:], in1=cfm[:], op=ALU.add),
                             reads=['cfm'], writes=['cacc'])
                yield
                S.op(V, lambda e: e.tensor_tensor(out=cacc[:], in0=cacc[:], in1=bc(cbt[:].unsqueeze(2), [128, 16, 128]), op=ALU.add),
                     reads=['cbt'], writes=['cacc'])
                yield
                S.op('act', lambda e: e.activation(out=cfm[:, 0:8, :], in_=cacc[:, 0:8, :], func=AF.Silu),
                     reads=['cacc'], writes=['cfm'])
                yield
                S.op('act', lambda e: e.activation(out=BCb[:], in_=cacc[:, 8:16, :], func=AF.Silu),
                     reads=['cacc'], writes=['BCb'])
                yield
                S.op(V, lambda e: e.tensor_tensor(out=dt_t[:], in0=ztm[:, 1024:1040], in1=dtp[:, 0, :], op=ALU.add),
                     reads=['ztm', 'dtp'], writes=['dt_t'])
                yield
                S.op('act', lambda e: e.activation(out=dt_t[:], in_=dt_t[:], func=AF.Exp), writes=['dt_t'])
                yield
                S.op('act', lambda e: e.activation(out=dt_t[:], in_=dt_t[:], func=AF.Ln, bias=1.0), writes=['dt_t'])
                yield
                S.op(V, lambda e: e.tensor_tensor(out=a_t[:], in0=dt_t[:], in1=At[:], op=ALU.mult),
                     reads=['dt_t', 'At'], writes=['a_t'])
                yield
                S.op('act', lambda e: e.activation(out=zs[:], in_=ztm[:, 0:1024], func=AF.Silu), reads=['ztm'], writes=['zs'])
                yield
                for half in range(2):
                    bank = half

                    def tr(e, half=half, bank=bank):
                        for i in range(4):
                            inst = e.transpose(PB[bank][:, i * 128:(i + 1) * 128], cfm[:, half * 4 + i, :], idf[:])
                        return inst
                    S.op('pe', tr, reads=['cfm', 'idf'], writes=[pk(bank)])
                    evac_copy(xs32[:, half * 512:(half + 1) * 512], PB[bank][:], [], [pk(bank), ('xs32', half)])
                yield

                def trb(e):
                    pbf = PB[2][:].bitcast(BF16)
                    for i in range(4):
                        inst = e.transpose(pbf[:, i * 128:(i + 1) * 128], BCb[:, i, :], idb[:])
                    return inst
                yield
                S.op('pe', trb, reads=['BCb', 'idb'], writes=[pk(2)])
                yield
                evac_copy(Btm[:], PB[2][:].bitcast(BF16)[:, 0:512], [], [pk(2), 'Btm'])
                yield
                S.op('pool', lambda e: e.tensor_tensor(out=Rm[:], in0=bc(a_t[:].unsqueeze(2), [128, 16, 128]),
                                                        in1=bc(mUi[:].unsqueeze(1), [128, 16, 128]), op=ALU.mult),
                     reads=['a_t', 'mUi'], writes=['Rm'])
                yield
                for i in range(4):
                    bank = 2 + i
                    S.op('pe', lambda e, i=i, bank=bank: e.matmul(PB[bank][:], mLs[:], Rm[:, i * 4:(i + 1) * 4, :], start=True, stop=True),
                         reads=['Rm', 'mLs'], writes=[pk(bank)])
                    S.op('act', lambda e, i=i, bank=bank: e.activation(out=Em[:, i * 4:(i + 1) * 4, :], in_=PB[bank][:], func=AF.Exp),
                         writes=[pk(bank), ('Em', i)])
                yield

                def mm_cs(e):
                    e.matmul(PB[1][:, 0:16], mUi[:], a_t[:], start=True, stop=True)
                    return e.matmul(PB[1][:, 16:32], ones_f[:], a_t[:], start=True, stop=True)
                yield
                S.op('pe', mm_cs, reads=['a_t', 'mUi', 'ones_f'], writes=[pk(1)])
                yield
                S.op('act', lambda e: e.activation(out=ea[:], in_=PB[1][:, 0:16], func=AF.Exp), writes=[pk(1), 'ea'])
                yield
                S.op('act', lambda e: e.activation(out=cd[:], in_=PB[1][:, 16:32], func=AF.Exp), writes=[pk(1), 'cd'])
                yield
                def mm_sc(e):
                    for g in range(4):
                        inst = e.matmul(PB[0][:, g * 128:(g + 1) * 128], BCb[:, g, :], BCb[:, 4 + g, :], start=True, stop=True)
                    return inst
                yield
                S.op('pe', mm_sc, reads=['BCb'], writes=[pk(0)])
                yield
                S.op(V, lambda e: e.tensor_tensor(out=scm[:], in0=PB[0][:].rearrange("p (g l) -> p g l", g=4),
                                                  in1=bc(mUi[:].unsqueeze(1), [128, 4, 128]), op=ALU.mult),
                     reads=['mUi'], writes=[pk(0), 'scm'])
                yield
                for g in range(4):
                    S.op('pool' if g % 2 else V, lambda e, g=g: e.tensor_tensor(
                        out=MT[:, g * 4:(g + 1) * 4, :], in0=Em[:, g * 4:(g + 1) * 4, :],
                        in1=bc(scm[:, g:g + 1, :], [128, 4, 128]), op=ALU.mult),
                        reads=[('Em', g), 'scm'], writes=[('MT', g)])
                yield
                S.op(V, lambda e: e.tensor_tensor(out=xdt[:].rearrange("p (h d) -> p h d", h=16),
                                                  in0=xs32[:].rearrange("p (h d) -> p h d", h=16),
                                                  in1=bc(dt_t[:].unsqueeze(2), [128, 16, 64]), op=ALU.mult),
                     reads=[('xs32', 0), ('xs32', 1), 'dt_t'], writes=['xdt'])
                yield
                S.op(V, lambda e: e.tensor_tensor(out=dtd[:].unsqueeze(2), in0=dt_t[:].unsqueeze(2), in1=Em[:, :, 127:128], op=ALU.mult),
                     reads=['dt_t'] + [('Em', i) for i in range(4)], writes=['dtd'])
                yield
                S.op(V, lambda e: e.tensor_tensor(out=xdtd[:].rearrange("p (h d) -> p h d", h=16),
                                                  in0=xs32[:].rearrange("p (h d) -> p h d", h=16),
                                                  in1=bc(dtd[:].unsqueeze(2), [128, 16, 64]), op=ALU.mult),
                     reads=[('xs32', 0), ('xs32', 1), 'dtd'], writes=['xdtd'])
                yield
                for half in range(2):
                    def mm_yd(e, half=half):
                        for hh in range(8):
                            h = half * 8 + hh
                            inst = e.matmul(PB[1 + half][:, hh * 64:(hh + 1) * 64], MT[:, h, :], xdt[:, h * 64:(h + 1) * 64],
                                            start=True, stop=True)
                        return inst
                    S.op('pe', mm_yd, reads=[('MT', g) for g in range(4)] + ['xdt'], writes=[pk(1 + half)])

                    def mm_yo(e, half=half):
                        for gg in range(2):
                            g = half * 2 + gg
                            inst = e.matmul(PB[3 + half][:, gg * 256:(gg + 1) * 256], BCb[:, 4 + g, :],
                                            ssm_stb[:, g * 256:(g + 1) * 256], start=True, stop=True)
                        return inst
                    S.op('pe', mm_yo, reads=['BCb', 'ssm_stb'], writes=[pk(3 + half)])

                    def mm_cs2(e, half=half):
                        for gg in range(2):
                            g = half * 2 + gg
                            inst = e.matmul(PB[(5 + half) % 6][:, gg * 256:(gg + 1) * 256], Btm[:, g * 128:(g + 1) * 128],
                                            xdtd[:, g * 256:(g + 1) * 256], start=True, stop=True)
                        return inst
                    S.op('pe', mm_cs2, reads=['Btm', 'xdtd'], writes=[pk((5 + half) % 6)])
                yield
                for half in range(2):
                    sl = slice(half * 512, (half + 1) * 512)
                    hs = slice(half * 8, (half + 1) * 8)
                    S.op(V, lambda e, half=half, sl=sl, hs=hs: e.tensor_tensor(
                        out=t1[:, sl].rearrange("p (h d) -> p h d", h=8), in0=PB[3 + half][:].rearrange("p (h d) -> p h d", h=8),
                        in1=bc(ea[:, hs].unsqueeze(2), [128, 8, 64]), op=ALU.mult),
                        reads=['ea'], writes=[pk(3 + half), ('t1', half)])
                    S.op('pool', lambda e, sl=sl: e.tensor_tensor(out=t2[:, sl], in0=xs32[:, sl], in1=tmb[:, 1, sl], op=ALU.mult),
                         reads=[('xs32', half), 'tmb'], writes=[('t2', half)])
                    S.op(V, lambda e, sl=sl: e.tensor_tensor(out=t1[:, sl], in0=t1[:, sl], in1=t2[:, sl], op=ALU.add),
                         reads=[('t2', half)], writes=[('t1', half)])
                    S.op(V, lambda e, half=half, sl=sl: e.tensor_tensor(out=t1[:, sl], in0=PB[1 + half][:], in1=t1[:, sl], op=ALU.add),
                         writes=[pk(1 + half), ('t1', half)])
                    S.op(V, lambda e, sl=sl, hs=hs: e.tensor_tensor(
                        out=ssm_st[:, sl].rearrange("p (h d) -> p h d", h=8), in0=ssm_st[:, sl].rearrange("p (h d) -> p h d", h=8),
                        in1=bc(cd[:, hs].unsqueeze(2), [128, 8, 64]), op=ALU.mult),
                        reads=['cd', 'ssm_stb'], writes=[('ssm_st', half)])
                    S.op(V, lambda e, half=half, sl=sl: e.tensor_tensor(out=ssm_st[:, sl], in0=PB[(5 + half) % 6][:], in1=ssm_st[:, sl], op=ALU.add),
                         writes=[pk((5 + half) % 6), ('ssm_st', half)])
                yield
                S.op('act', lambda e: e.activation(out=ssm_stb[:], in_=ssm_st[:], func=AF.Copy),
                     reads=[('ssm_st', 0), ('ssm_st', 1)], writes=['ssm_stb'])
                yield
                S.op(V, lambda e: e.tensor_tensor(out=t1[:], in0=t1[:], in1=zs[:], op=ALU.mult),
                     reads=['zs'], writes=[('t1', 0), ('t1', 1)])
                yield
                S.op('act', lambda e: e.activation(out=vsq[:], in_=t1[:], func=AF.Square), reads=[('t1', 0), ('t1', 1)], writes=['vsq'])
                yield
                S.op(V, lambda e: e.tensor_reduce(out=ss4[:], in_=vsq[:].rearrange("p (g d) -> p g d", g=4), axis=AX.X, op=ALU.add),
                     reads=['vsq'], writes=['ss4'])
                yield
                S.op(V, lambda e: e.tensor_scalar(out=ss4[:], in0=ss4[:], scalar1=1.0 / 256.0, scalar2=1e-5, op0=ALU.mult, op1=ALU.add),
                     writes=['ss4'])
                yield
                S.op('act', lambda e: e.activation(out=ss4[:], in_=ss4[:], func=AF.Sqrt), writes=['ss4'])
                yield
                S.op(V, lambda e: e.reciprocal(out=rs4[:], in_=ss4[:]), reads=['ss4'], writes=['rs4'])
                yield
                S.op(V, lambda e: e.tensor_tensor(out=t1[:].rearrange("p (g d) -> p g d", g=4), in0=t1[:].rearrange("p (g d) -> p g d", g=4),
                                                  in1=bc(rs4[:].unsqueeze(2), [128, 4, 256]), op=ALU.mult),
                     reads=['rs4', 'vsq'], writes=[('t1', 0), ('t1', 1)])
                yield
                S.op(V, lambda e: e.tensor_tensor(out=ytm[:], in0=t1[:], in1=tmb[:, 0, :], op=ALU.mult),
                     reads=[('t1', 0), ('t1', 1), 'tmb'], writes=['ytm'])
                yield

            else:
                if c == 0:
                    for _ in p1a_gen(0):
                        pass
                nxt = p1a_gen(c + 1) if c + 1 < NCH else iter(())
                for half in range(2):
                    bank = 4 + half

                    def mm_g(e, half=half, bank=bank):
                        for kc in range(2):
                            inst = e.matmul(PB[bank][:], lob[:, 2 + kc, :], wgb[:, kc, half * 512:(half + 1) * 512],
                                            start=(kc == 0), stop=(kc == 1))
                        return inst
                    S.op('pe', mm_g, reads=[('lob', 2), 'wgb'], writes=[pk(bank)])
                    evac_copy(gtm[:, half * 512:(half + 1) * 512], PB[bank][:], [], [pk(bank), ('gtm', half)])
                S.op('act', lambda e: e.activation(out=w2[:], in_=Lr[:], func=AF.Exp, scale=-CDEC), reads=LrK, writes=['w2'])
                S.op(V, lambda e: e.tensor_copy(out=gC[:].unsqueeze(2), in_=w2[:, :, 127:128]), reads=['w2'], writes=['gC'])
                S.op(V, lambda e: e.tensor_tensor(out=KR[:, :, 128:256], in0=sr[:, 0:8, :], in1=w2[:], op=ALU.mult),
                     reads=['sr', 'w2'], writes=[('KR', 1)])
                S.op('act', lambda e: e.activation(out=w3[:], in_=Lr[:], func=AF.Exp, scale=CDEC), reads=LrK, writes=['w3'])
                S.op(V, lambda e: e.tensor_tensor(out=BhT[:], in0=bb[:], in1=w3[:], op=ALU.mult), reads=['bb', 'w3'], writes=['BhT'])
                S.op('pool', lambda e: e.tensor_tensor(out=KhT[:], in0=kpr[:], in1=w3[:], op=ALU.mult), reads=['kpr', 'w3'], writes=['KhT'])
                S.op(V, lambda e: e.tensor_tensor(out=w1[:], in0=Lr[:], in1=dsig[:], op=ALU.subtract), reads=LrK + ['dsig', 'kap'], writes=['w1'])
                S.op('act', lambda e: e.activation(out=w1[:], in_=w1[:], func=AF.Exp, scale=-CDEC), writes=['w1'])
                S.op(V, lambda e: e.tensor_tensor(out=KR[:, :, 0:128], in0=kap[:], in1=w1[:], op=ALU.mult), reads=['kap', 'w1'], writes=[('KR', 0)])
                S.op(V, lambda e: e.tensor_tensor(out=w2[:], in0=Lr[:], in1=bc(Lr[:, :, 127:128], [128, 8, 128]), op=ALU.subtract),
                     reads=LrK + ['gC', ('KR', 1)], writes=['w2'])
                S.op('act', lambda e: e.activation(out=w2[:], in_=w2[:], func=AF.Exp, scale=CDEC), writes=['w2'])
                S.op(V, lambda e: e.tensor_tensor(out=BGT[:], in0=bb[:], in1=w2[:], op=ALU.mult), reads=['bb', 'w2'], writes=['BGT'])
                S.op('pool', lambda e: e.tensor_tensor(out=KGT[:], in0=kpr[:], in1=w2[:], op=ALU.mult), reads=['kpr', 'w2'], writes=['KGT'])
                S.op('pool', lambda e: e.tensor_tensor(out=w3[:], in0=sr[:, 0:8, :], in1=kpr[:], op=ALU.mult),
                     reads=['sr', 'kpr', 'BhT', 'KhT'], writes=['w3'])
                S.op('pool', lambda e: e.tensor_tensor(out=rkrb[:], in0=w3[:], in1=bc(chp[:, 4, :].unsqueeze(2), [128, 8, 128]), op=ALU.mult),
                     reads=['w3', 'chp'], writes=['rkrb'])
                for half in range(2):
                    bank = half

                    def trv(e, half=half, bank=bank):
                        for i in range(4):
                            inst = e.transpose(PB[bank][:, i * 128:(i + 1) * 128], sr[:, 16 + half * 4 + i, :], idf[:])
                        return inst
                    S.op('pe', trv, reads=['sr', 'idf'], writes=[pk(bank)])
                    S.op('act', lambda e, half=half, bank=bank: e.activation(out=Vtm[:, half * 512:(half + 1) * 512], in_=PB[bank][:], func=AF.Copy),
                         writes=[pk(bank), ('Vtm', half)])
                    S.op(V, lambda e, half=half, bank=bank: e.tensor_copy(out=Vb[:, half * 512:(half + 1) * 512], in_=PB[bank][:]),
                         writes=[pk(bank), ('Vb', half)])
                VbK = [('Vb', 0), ('Vb', 1)]
                for wi, (src, dst, key) in enumerate([(BGT, BGtm, 'BG'), (KGT, KGtm, 'KG')]):
                    bank = 2 + wi

                    def trg(e, src=src, bank=bank):
                        pbf = PB[bank][:].bitcast(BF16)
                        for i in range(8):
                            inst = e.transpose(pbf[:, i * 128:(i + 1) * 128], src[:, i, :], idb[:])
                        return inst
                    S.op('pe', trg, reads=[key + 'T', 'idb'], writes=[pk(bank)])
                    evac_copy(dst[:], PB[bank][:].bitcast(BF16), [], [pk(bank), key + 'tm'])
                def mm_bs(e):
                    for hp in range(8):
                        inst = e.matmul(PB[4][:, hp * 2:(hp + 1) * 2], rkrb[:, hp, :], sel2[:], start=True, stop=True)
                    return inst
                S.op('pe', mm_bs, reads=['rkrb', 'sel2'], writes=[pk(4)])
                S.op(V, lambda e: e.tensor_copy(out=bs16[:], in_=PB[4][:, 0:16]), writes=[pk(4), 'bs16'])
                for zi, (srcT, dstT, skey, dkey) in enumerate([(BhT, BhTz, 'BhT', 'BhTz'), (KhT, KhTz, 'KhT', 'KhTz'), (None, KtTz, ('KR', 0), 'KtTz')]):
                    for par in range(2):
                        src_ap = KR[:, :, 0:128] if srcT is None else srcT[:]
                        dst_ap = dstT[:].rearrange("p (a two) t -> p a two t", two=2)[:, :, par, :]
                        if (zi + par) % 2 == 0:
                            S.op('act', lambda e, src_ap=src_ap, dst_ap=dst_ap, par=par: e.activation(out=dst_ap, in_=src_ap, func=AF.Copy, scale=pm[:, par:par + 1]),
                                 reads=[skey, 'pm'], writes=[(dkey, par)])
                        else:
                            S.op(V, lambda e, src_ap=src_ap, dst_ap=dst_ap, par=par: e.tensor_scalar(out=dst_ap, in0=src_ap, scalar1=pm[:, par:par + 1], scalar2=None, op0=ALU.mult),
                                 reads=[skey, 'pm'], writes=[(dkey, par)])
                ZK = [(k_, p_) for k_ in ('BhTz', 'KhTz', 'KtTz') for p_ in range(2)]
                for g4 in range(4):
                    hs = range(g4 * 4, g4 * 4 + 4)
                    bA = [5, 6]
                    bB = [7, 0]
                    bA2 = 1

                    def mm_in(e, hs=hs):
                        for i, h in enumerate(hs):
                            hp = h // 2
                            e.matmul(PB[bA[i // 2]][:, (i % 2) * 256:(i % 2 + 1) * 256], BhTz[:, h, :], KR[:, hp, :], start=True, stop=True)
                            e.matmul(PB[bB[i // 2]][:, (i % 2) * 256:(i % 2 + 1) * 256], KhTz[:, h, :], KR[:, hp, :], start=True, stop=True)
                            inst = e.matmul(PB[bA2][:, i * 128:(i + 1) * 128], KtTz[:, h, :], BhT[:, hp, :], start=True, stop=True)
                        return inst
                    S.op('pe', mm_in, reads=['BhT', 'KhT', ('KR', 0), ('KR', 1)] + ZK, writes=[pk(5), pk(6), pk(7), pk(0), pk(1)])
                    for i2 in range(2):
                        hh = slice(g4 * 4 + i2 * 2, g4 * 4 + i2 * 2 + 2)
                        pa = PB[bA[i2]][:].rearrange("p (h w t) -> p h w t", h=2, w=2)
                        pb_ = PB[bB[i2]][:].rearrange("p (h w t) -> p h w t", h=2, w=2)
                        S.op(V, lambda e, hh=hh, pa=pa: e.scalar_tensor_tensor(
                            out=X0[:, hh, :], in0=pa[:, :, 0, :], scalar=-1.0, in1=bc(mUs[:].unsqueeze(1), [128, 2, 128]),
                            op0=ALU.mult, op1=ALU.mult), reads=['mUs'], writes=[pk(bA[i2]), ('X0', g4, i2)])
                        S.op(V, lambda e, hh=hh, pa=pa: e.tensor_tensor(
                            out=PTm[:, hh, :], in0=pa[:, :, 1, :], in1=bc(mUi[:].unsqueeze(1), [128, 2, 128]), op=ALU.mult),
                            reads=['mUi'], writes=[pk(bA[i2]), ('PT', g4, i2)])
                        S.op(V, lambda e, hh=hh, pb_=pb_: e.tensor_tensor(
                            out=BmT[:, hh, :], in0=pb_[:, :, 0, :], in1=bc(mUs[:].unsqueeze(1), [128, 2, 128]), op=ALU.mult),
                            reads=['mUs'], writes=[pk(bB[i2]), ('BmT', g4, i2)])
                        S.op(V, lambda e, hh=hh, pb_=pb_: e.tensor_tensor(
                            out=QTm[:, hh, :], in0=pb_[:, :, 1, :], in1=bc(mUi[:].unsqueeze(1), [128, 2, 128]), op=ALU.mult),
                            reads=['mUi'], writes=[pk(bB[i2]), ('QT', g4, i2)])
                    h4 = slice(g4 * 4, g4 * 4 + 4)
                    S.op(V, lambda e, h4=h4: e.scalar_tensor_tensor(
                        out=XT0[:, h4, :], in0=PB[bA2][:].rearrange("p (h t) -> p h t", h=4), scalar=-1.0,
                        in1=bc(mLs[:].unsqueeze(1), [128, 4, 128]), op0=ALU.mult, op1=ALU.mult),
                        reads=['mLs'], writes=[pk(bA2), ('XT0', g4)])
                    S.op('pool', lambda e, h4=h4: e.tensor_tensor(out=Gm[:, h4, :], in0=X0[:, h4, :], in1=bc(idb[:].unsqueeze(1), [128, 4, 128]), op=ALU.add),
                         reads=[('X0', g4, 0), ('X0', g4, 1), 'idb'], writes=[('G', g4)])
                Xs, XTs = [X0, X1], [XT0, XT1]
                for r in range(1, 8):
                    for _ in range(2):
                        next(nxt, None)
                    src, dst = (r - 1) % 2, r % 2
                    for g4 in range(4):
                        h4 = slice(g4 * 4, g4 * 4 + 4)
                        kx_src = [('X0', g4, 0), ('X0', g4, 1)] if r == 1 else [('X', src, g4)]
                        kxt_src = [('XT0', g4)] if r == 1 else [('XT', src, g4)]
                        banks = [(g4 * 2) % 8, (g4 * 2 + 1) % 8, 0]
                        b0, b1 = (2 + (r * 4 + g4) * 3) % 8, (3 + (r * 4 + g4) * 3) % 8
                        b2 = (4 + (r * 4 + g4) * 3) % 8
                        if r <= 5:
                            def mmx(e, h4=h4, src=src, b0=b0):
                                for i, h in enumerate(range(h4.start, h4.stop)):
                                    inst = e.matmul(PB[b0][:, i * 128:(i + 1) * 128], XTs[src][:, h, :], Xs[src][:, h, :], start=True, stop=True)
                                return inst
                            S.op('pe', mmx, reads=kx_src + kxt_src, writes=[pk(b0)])
                            evac_copy(Xs[dst][:, h4, :], PB[b0][:].rearrange("p (h t) -> p h t", h=4), [], [pk(b0), ('X', dst, g4)])
                        if r <= 6:
                            def mmxt(e, h4=h4, src=src, b1=b1):
                                for i, h in enumerate(range(h4.start, h4.stop)):
                                    inst = e.matmul(PB[b1][:, i * 128:(i + 1) * 128], Xs[src][:, h, :], XTs[src][:, h, :], start=True, stop=True)
                                return inst
                            S.op('pe', mmxt, reads=kx_src + kxt_src, writes=[pk(b1)])
                            evac_copy(XTs[dst][:, h4, :], PB[b1][:].rearrange("p (h t) -> p h t", h=4), [], [pk(b1), ('XT', dst, g4)])
                        if r >= 2:
                            def mmg(e, h4=h4, src=src, b2=b2):
                                for i, h in enumerate(range(h4.start, h4.stop)):
                                    inst = e.matmul(PB[b2][:, i * 128:(i + 1) * 128], XTs[src][:, h, :], Gm[:, h, :], start=True, stop=True)
                                return inst
                            S.op('pe', mmg, reads=kxt_src + [('G', g4)], writes=[pk(b2)])
                            S.op(V, lambda e, h4=h4, b2=b2: e.tensor_tensor(out=Gm[:, h4, :], in0=PB[b2][:].rearrange("p (h t) -> p h t", h=4),
                                                                            in1=Gm[:, h4, :], op=ALU.add),
                                 writes=[pk(b2), ('G', g4)])
                for _ in nxt:
                    pass
                GK = [('G', g4) for g4 in range(4)]
                BmK = [('BmT', g4, i2) for g4 in range(4) for i2 in range(2)]
                PK_ = [('PT', g4, i2) for g4 in range(4) for i2 in range(2)]
                QK = [('QT', g4, i2) for g4 in range(4) for i2 in range(2)]
                for half in range(2):
                    bank = half

                    def mmz(e, half=half, bank=bank):
                        for hpl in range(4):
                            hp = half * 4 + hpl
                            e.matmul(PB[bank][:, hpl * 128:(hpl + 1) * 128], KR[:, hp, 0:128], rstb[:, hp, :], start=True, stop=False)
                            for par in range(2):
                                h = 2 * hp + par
                                inst = e.matmul(PB[bank][:, hpl * 128 + par * 64:hpl * 128 + (par + 1) * 64], BmT[:, h, :], Vb[:, h * 64:(h + 1) * 64],
                                                start=False, stop=(par == 1))
                        return inst
                    S.op('pe', mmz, reads=[('KR', 0), 'rstb'] + BmK + VbK, writes=[pk(bank)])
                    evac_copy(Zb[:, half * 512:(half + 1) * 512], PB[bank][:], [], [pk(bank), ('Zb', half)])
                for half in range(2):
                    bank = 2 + half

                    def mmu(e, half=half, bank=bank):
                        for hh in range(8):
                            h = half * 8 + hh
                            inst = e.matmul(PB[bank][:, hh * 64:(hh + 1) * 64], Gm[:, h, :], Zb[:, h * 64:(h + 1) * 64], start=True, stop=True)
                        return inst
                    S.op('pe', mmu, reads=GK + [('Zb', half)], writes=[pk(bank)])
                    S.op('act', lambda e, half=half, bank=bank: e.activation(out=Ub[:, half * 512:(half + 1) * 512], in_=PB[bank][:], func=AF.Copy, scale=-1.0),
                         writes=[pk(bank), ('Ub', half)])
                UbK = [('Ub', 0), ('Ub', 1)]
                for half in range(2):
                    bank = 4 + half

                    def mmy(e, half=half, bank=bank):
                        for hpl in range(4):
                            hp = half * 4 + hpl
                            e.matmul(PB[bank][:, hpl * 128:(hpl + 1) * 128], KR[:, hp, 128:256], rstb[:, hp, :], start=True, stop=False)
                            for par in range(2):
                                h = 2 * hp + par
                                cs = slice(hpl * 128 + par * 64, hpl * 128 + (par + 1) * 64)
                                e.matmul(PB[bank][:, cs], PTm[:, h, :], Ub[:, h * 64:(h + 1) * 64], start=False, stop=False)
                                inst = e.matmul(PB[bank][:, cs], QTm[:, h, :], Vb[:, h * 64:(h + 1) * 64], start=False, stop=(par == 1))
                        return inst
                    S.op('pe', mmy, reads=[('KR', 1), 'rstb'] + PK_ + QK + UbK + VbK, writes=[pk(bank)])
                    evac_copy(yr[:, half * 512:(half + 1) * 512], PB[bank][:], [], [pk(bank), ('yr', half)])

                def mms(e):
                    for hp in range(8):
                        o_ap = PB[6 + hp // 4][:, (hp % 4) * 128:(hp % 4 + 1) * 128]
                        e.matmul(o_ap, BGtm[:, hp * 128:(hp + 1) * 128], Ub[:, hp * 128:(hp + 1) * 128], start=True, stop=False)
                        inst = e.matmul(o_ap, KGtm[:, hp * 128:(hp + 1) * 128], Vb[:, hp * 128:(hp + 1) * 128], start=False, stop=True)
                    return inst
                S.op('pe', mms, reads=['BGtm', 'KGtm'] + UbK + VbK, writes=[pk(6), pk(7)])
                S.op(V, lambda e: e.tensor_tensor(out=rst[:], in0=rst[:], in1=bc(gC[:].unsqueeze(2), [128, 8, 128]), op=ALU.mult),
                     reads=['gC', 'rstb'], writes=['rst'])
                for half in range(2):
                    S.op(V, lambda e, half=half: e.tensor_tensor(out=rst[:, half * 4:(half + 1) * 4, :], in0=PB[6 + half][:].rearrange("p (a i) -> p a i", a=4),
                                                                 in1=rst[:, half * 4:(half + 1) * 4, :], op=ALU.add),
                         writes=[pk(6 + half), 'rst'])
                S.op(V, lambda e: e.tensor_tensor(out=rst[:], in0=rst[:], in1=bc(blkf[:].unsqueeze(1), [128, 8, 128]), op=ALU.mult),
                     reads=['blkf'], writes=['rst'])
                S.op('act', lambda e: e.activation(out=rstb[:], in_=rst[:], func=AF.Copy), reads=['rst'], writes=['rstb'])
                yrK = [('yr', 0), ('yr', 1)]
                yr3 = yr[:].rearrange("p (h d) -> p h d", h=16)
                S.op(V, lambda e: e.tensor_reduce(out=s16a[:], in_=yr3, axis=AX.X, op=ALU.add), reads=yrK, writes=['s16a'])
                S.op('act', lambda e: e.activation(out=t2[:], in_=yr[:], func=AF.Square), reads=yrK, writes=[('t2', 0), ('t2', 1)])
                S.op(V, lambda e: e.tensor_reduce(out=s16b[:], in_=t2[:].rearrange("p (h d) -> p h d", h=16), axis=AX.X, op=ALU.add),
                     reads=[('t2', 0), ('t2', 1)], writes=['s16b'])
                S.op(V, lambda e: e.tensor_scalar(out=s16a[:], in0=s16a[:], scalar1=1.0 / 64.0, scalar2=None, op0=ALU.mult), writes=['s16a'])
                S.op(V, lambda e: e.tensor_tensor(out=s16c[:], in0=s16a[:], in1=s16a[:], op=ALU.mult), reads=['s16a'], writes=['s16c'])
                S.op(V, lambda e: e.scalar_tensor_tensor(out=s16b[:], in0=s16b[:], scalar=1.0 / 64.0, in1=s16c[:], op0=ALU.mult, op1=ALU.subtract),
                     reads=['s16c'], writes=['s16b'])
                S.op(V, lambda e: e.tensor_scalar(out=s16b[:], in0=s16b[:], scalar1=64e-5, scalar2=None, op0=ALU.add), writes=['s16b'])
                S.op('act', lambda e: e.activation(out=s16b[:], in_=s16b[:], func=AF.Sqrt), writes=['s16b'])
                S.op(V, lambda e: e.reciprocal(out=s16d[:], in_=s16b[:]), reads=['s16b'], writes=['s16d'])
                S.op(V, lambda e: e.tensor_tensor(out=yr3, in0=yr3, in1=bc(s16a[:].unsqueeze(2), [128, 16, 64]), op=ALU.subtract),
                     reads=['s16a', ('t2', 0), ('t2', 1)], writes=yrK)
                S.op(V, lambda e: e.tensor_tensor(out=yr3, in0=yr3, in1=bc(s16d[:].unsqueeze(2), [128, 16, 64]), op=ALU.mult),
                     reads=['s16d'], writes=yrK)
                S.op(V, lambda e: e.tensor_tensor(out=yr[:], in0=yr[:], in1=tmb[:, 0, :], op=ALU.mult), reads=['tmb'], writes=yrK)
                S.op(V, lambda e: e.tensor_tensor(out=yr[:], in0=yr[:], in1=tmb[:, 1, :], op=ALU.add), reads=['tmb'], writes=yrK)
                S.op('pool', lambda e: e.tensor_tensor(out=t2[:].rearrange("p (h d) -> p h d", h=16), in0=Vtm[:].rearrange("p (h d) -> p h d", h=16),
                                                        in1=bc(bs16[:].unsqueeze(2), [128, 16, 64]), op=ALU.mult),
                     reads=[('Vtm', 0), ('Vtm', 1), 'bs16', 's16b'], writes=[('t2', 0), ('t2', 1)])
                S.op(V, lambda e: e.tensor_tensor(out=yr[:], in0=yr[:], in1=t2[:], op=ALU.add), reads=[('t2', 0), ('t2', 1)], writes=yrK)
                S.op(V, lambda e: e.tensor_tensor(out=ytm[:], in0=yr[:], in1=gtm[:], op=ALU.mult),
                     reads=yrK + [('gtm', 0), ('gtm', 1)], writes=['ytm'])
            def try_(e):
                pbf = PB[YB][:].bitcast(BF16)
                for i in range(8):
                    inst = e.transpose(pbf[:, i * 128:(i + 1) * 128], ytm[:, i * 128:(i + 1) * 128], idb[:])
                return inst
            S.op('pe', try_, reads=['ytm', 'idb'], writes=[pk(YB)])
            evac_copy(yT[:, :, q * 128:(q + 1) * 128], PB[YB][:].bitcast(BF16).rearrange("p (a t) -> p a t", a=8), [], [pk(YB), ('yT', q)])
            if q == 3:
                agin_bf = agin[tt].ap().bitcast(BF16)
                S.op('act', lambda e, agin_bf=agin_bf: e.dma_start(
                    out=agin_bf[which * 1024:(which + 1) * 1024, :].rearrange("(a p) t -> p a t", p=128), in_=yT[:]),
                    reads=[('yT', q_) for q_ in range(4)], writes=[('agin', tt, which)], stream='agst')
                if not SSD:
                    S.op('pool', lambda e, tt=tt: e.collective_compute(
                        "AllGather", ALU.bypass, replica_groups=[[0, 1], [2, 3], [4, 5], [6, 7]],
                        ins=[agin[tt].ap().opt()], outs=[agout[tt].ap().opt()]),
                        reads=[('agin', tt, 0), ('agin', tt, 1)], writes=[('agout', tt)], stream='cc', inc=1)

        if SSD:
            for _ in a1_tile(0):
                pass
        for c in range(NCH):
            if SSD:
                if c % 4 == 0:
                    a1g = a1_tile(c // 4 + 1) if c // 4 + 1 < NT else iter(())
                k = 0
                for _ in chunk_gen(c):
                    k += 1
                    if k % 4 == 0:
                        next(a1g, None)
                if c % 4 == 3:
                    for _ in a1g:
                        pass
            else:
                for _ in chunk_gen(c):
                    pass
        S.flush(final_streams=['ld', 'st', 'cc', 'ld2'])


def phase_b(nc, S, PB, pk, sb, G):
    agout, xTr, pT_d, lnp_d, outT, sel_d = G['agout'], G['xTr'], G['pT'], G['lnp'], G['outT'], G['sel']
    w_out_bf, w_up_bf, w_down_bf, w_gate_bf, w_ple_bf = G['w_out_bf'], G['w_up_bf'], G['w_down_bf'], G['w_gate_bf'], G['w_ple_bf']
    w_down_v = w_down_bf.rearrange("r (s c) -> (r s) c", s=2)
    w_ple_v = w_ple_bf.rearrange("r (s c) -> (r s) c", s=16)
    from contextlib import ExitStack
    V = 'dve'
    NS = 5
    with ExitStack() as es:
        def T(name, shape, dt=F32):
            return es.enter_context(sb("b_" + name, shape, dt))
        hb = T("hb", [128, 32, TT], BF16)
        acc = T("acc", [128, 32, TT])
        hid = T("hid", [128, 16, TT], BF16)
        ring = T("ring", [128, NS, 4096], BF16)
        lnp = T("lnp", [128, 6, 32]); sel = T("sel", [128, 2])
        wp = T("wp", [128, 32, 256], BF16); lnpa = T("lnpa", [128, 6, 32])
        ptb = T("ptb", [128, 2, TT], BF16); ptf = T("ptf", [128, 2, TT])
        sq = T("sq", [128, TT]); sq2 = T("sq2", [128, TT]); red = T("red", [128, 2, TT]); redp = T("redp", [128, TT])
        mean = T("mean", [128, TT]); rstd = T("rstd", [128, TT]); ones_f = T("ones", [128, 128])
        pe_sb = T("pe_sb", [128, TT]); sg_sb = T("sg_sb", [128, TT])
        G['issue_casts'](1000)
        S.op('sp', lambda e: e.dma_start(out=lnp[:], in_=lnp_d), writes=['lnp'], stream='ldc')
        S.op('sp', lambda e: e.dma_start(out=sel[:], in_=sel_d), writes=['sel'], stream='ldc')
        S.op(V, lambda e: e.tensor_scalar(out=lnpa[:], in0=lnp[:], scalar1=ALPHA, scalar2=None, op0=ALU.mult), reads=['lnp'], writes=['lnpa'])
        S.op(V, lambda e: e.memset(ones_f[:], 1.0), writes=['ones_b'])
        S.op('sp', lambda e: e.dma_start(out=wp[:], in_=w_ple_v.rearrange("(a p) c -> p a c", p=128)), reads=[('wplebf', 0)], writes=['wp'], stream='ldc')
        rs = [0]
        ps = [0]
        ACC = [('acc', i) for i in range(32)]
        HB = [('hb', i) for i in range(32)]

        def wload(src_ap, key_reads, n=4096):
            slot = rs[0] % NS
            rs[0] += 1
            S.op('sp', lambda e: e.dma_start(out=ring[:, slot, 0:n], in_=src_ap), reads=key_reads, writes=[('bring', slot)], stream='bring%d' % slot)
            return slot

        def nbank():
            b = ps[0] % 8
            ps[0] += 1
            return b

        def stat_accum(i):
            sqb, sqk = (sq, 'sq') if i % 2 == 0 else (sq2, 'sq2')
            S.op('act', lambda e: e.activation(out=sqb[:], in_=acc[:, i, :], func=AF.Square), reads=[('acc', i)], writes=[sqk])
            if i == 0:
                S.op(V, lambda e: e.tensor_copy(out=red[:, 0, :], in_=acc[:, i, :]), reads=[('acc', i)], writes=[('red', 0)])
                S.op(V, lambda e: e.tensor_copy(out=red[:, 1, :], in_=sqb[:]), reads=[sqk], writes=[('red', 1)])
            else:
                S.op(V, lambda e: e.tensor_tensor(out=red[:, 0, :], in0=red[:, 0, :], in1=acc[:, i, :], op=ALU.add), reads=[('acc', i)], writes=[('red', 0)])
                S.op(V, lambda e: e.tensor_tensor(out=red[:, 1, :], in0=red[:, 1, :], in1=sqb[:], op=ALU.add), reads=[sqk], writes=[('red', 1)])

        def layer_norm(li, last):
            b = nbank()
            S.op('pe', lambda e: e.matmul(PB[b][:], ones_f[:], red[:, 0, :], start=True, stop=True), reads=['ones_b', ('red', 0)], writes=[pk(b)])
            S.op(V, lambda e: e.tensor_scalar(out=mean[:], in0=PB[b][:], scalar1=1.0 / D, scalar2=None, op0=ALU.mult), writes=[pk(b), 'mean'])
            b2 = nbank()
            S.op('pe', lambda e: e.matmul(PB[b2][:], ones_f[:], red[:, 1, :], start=True, stop=True), reads=['ones_b', ('red', 1)], writes=[pk(b2)])
            S.op(V, lambda e: e.tensor_tensor(out=redp[:], in0=mean[:], in1=mean[:], op=ALU.mult), reads=['mean'], writes=['redp'])
            S.op(V, lambda e: e.scalar_tensor_tensor(out=rstd[:], in0=PB[b2][:], scalar=1.0 / D, in1=redp[:], op0=ALU.mult, op1=ALU.subtract),
                 reads=['redp'], writes=[pk(b2), 'rstd'])
            S.op(V, lambda e: e.tensor_scalar(out=rstd[:], in0=rstd[:], scalar1=1e-5, scalar2=None, op0=ALU.add), writes=['rstd'])
            S.op('act', lambda e: e.activation(out=rstd[:], in_=rstd[:], func=AF.Sqrt), writes=['rstd'])
            S.op(V, lambda e: e.reciprocal(out=rstd[:], in_=rstd[:]), writes=['rstd'])
            for i in range(32):
                S.op(V, lambda e, i=i: e.tensor_tensor(out=acc[:, i, :], in0=acc[:, i, :], in1=mean[:], op=ALU.subtract),
                     reads=['mean'], writes=[('acc', i)])
                S.op(V, lambda e, i=i: e.tensor_tensor(out=acc[:, i, :], in0=acc[:, i, :], in1=rstd[:], op=ALU.mult),
                     reads=['rstd'], writes=[('acc', i)])
                if last:
                    S.op('act', lambda e, i=i: e.activation(out=acc[:, i, :], in_=acc[:, i, :], func=AF.Identity,
                                                            scale=lnp[:, 2 * li, i:i + 1], bias=lnp[:, 2 * li + 1, i:i + 1]),
                         reads=['lnp'], writes=[('acc', i)])
                else:
                    if i % 2 == 0:
                        S.op('act', lambda e, i=i: e.activation(out=hb[:, i, :], in_=acc[:, i, :], func=AF.Identity,
                                                                scale=lnp[:, 2 * li, i:i + 1], bias=lnp[:, 2 * li + 1, i:i + 1]),
                             reads=['lnp', ('acc', i)], writes=[('hb', i)])
                    else:
                        S.op('pool', lambda e, i=i: e.tensor_scalar(out=hb[:, i, :], in0=acc[:, i, :], scalar1=lnp[:, 2 * li, i:i + 1],
                                                                    scalar2=lnp[:, 2 * li + 1, i:i + 1], op0=ALU.mult, op1=ALU.add),
                             reads=['lnp', ('acc', i)], writes=[('hb', i)])
                    S.op('act', lambda e, i=i: e.activation(out=acc[:, i, :], in_=acc[:, i, :], func=AF.Identity,
                                                            scale=lnpa[:, 2 * li, i:i + 1], bias=lnpa[:, 2 * li + 1, i:i + 1]),
                         reads=['lnpa', ('hb', i)], writes=[('acc', i)])

        if DEBUG_STAGE == 4:
            dbgB = nc.dram_tensor("dbgB", [5, D, TT], F32, kind="ExternalOutput").ap()

        def ckpt(k, tl):
            if DEBUG_STAGE == 4 and tl == 0:
                S.op('sp', lambda e: e.dma_start(out=dbgB[k, :, :].rearrange("(a p) t -> p a t", p=128), in_=acc[:]),
                     reads=ACC, writes=[('dbgB', k)], stream='dbg')

        for tl in range(4):
            if DEBUG_STAGE == 4 and tl == 1:
                break
            ya = agout[tl].ap().bitcast(BF16)
            yb_ = agout[tl + 4].ap().bitcast(BF16)
            S.op('sp', lambda e, ya=ya: e.dma_start(out=hb[:], in_=ya.rearrange("(a p) t -> p a t", p=128)),
                 reads=[('agout', tl)], writes=HB, stream='hbld')
            S.op('sp', lambda e, tl=tl: e.dma_start(out=acc[:], in_=xTr[:, tl * TT:(tl + 1) * TT].rearrange("(a p) t -> p a t", p=128)),
                 writes=ACC, stream='accld')
            S.op('sp', lambda e, tl=tl: e.dma_start(out=ptf[:], in_=pT_d[:, :, tl * TT:(tl + 1) * TT]), writes=['ptf'], stream='ptld')
            S.op(V, lambda e: e.tensor_copy(out=ptb[:], in_=ptf[:]), reads=['ptf'], writes=['ptb'])
            for i in range(4):
                slot = wload(yb_[i * 1024:(i + 1) * 1024, :].rearrange("(a p) t -> p a t", p=128), [('agout', tl + 4)])
                hk = [('hb', 8 * i + k) for k in range(8)]
                S.op(V, lambda e, i=i: e.tensor_scalar(out=hb[:, 8 * i:8 * i + 8, :], in0=hb[:, 8 * i:8 * i + 8, :], scalar1=sel[:, 0:1], scalar2=None, op0=ALU.mult),
                     reads=['sel'], writes=hk)
                S.op(V, lambda e, i=i, slot=slot: e.scalar_tensor_tensor(
                    out=hb[:, 8 * i:8 * i + 8, :], in0=ring[:, slot, :].rearrange("p (a t) -> p a t", a=8), scalar=sel[:, 1:2],
                    in1=hb[:, 8 * i:8 * i + 8, :], op0=ALU.mult, op1=ALU.add), reads=['sel', ('bring', slot)], writes=hk)
            for i in range(32):
                slot = wload(w_out_bf[i * 128:(i + 1) * 128, :], [('woutbf', i // 8)])
                b = nbank()

                def mm(e, slot=slot, b=b):
                    for kc in range(32):
                        inst = e.matmul(PB[b][:], ring[:, slot, kc * 128:(kc + 1) * 128], hb[:, kc, :], start=(kc == 0), stop=(kc == 31))
                    return inst
                S.op('pe', mm, reads=[('bring', slot)] + HB, writes=[pk(b)])
                S.op(V, lambda e, i=i, b=b: e.scalar_tensor_tensor(out=acc[:, i, :], in0=acc[:, i, :], scalar=ALPHA, in1=PB[b][:], op0=ALU.mult, op1=ALU.add),
                     writes=[pk(b), ('acc', i)])
                stat_accum(i)
            ckpt(0, tl)
            layer_norm(0, False)
            ckpt(1, tl)
            for fb in range(8):
                for j in range(16):
                    blk = fb * 16 + j
                    slot = wload(w_up_bf[blk * 128:(blk + 1) * 128, :], [('wupbf', blk // 8)])
                    b = nbank()

                    def mmu(e, slot=slot, b=b):
                        for kc in range(32):
                            inst = e.matmul(PB[b][:], ring[:, slot, kc * 128:(kc + 1) * 128], hb[:, kc, :], start=(kc == 0), stop=(kc == 31))
                        return inst
                    S.op('pe', mmu, reads=[('bring', slot)] + HB, writes=[pk(b)])
                    S.op('act', lambda e, b=b: e.activation(out=sg_sb[:], in_=PB[b][:], func=AF.Square), writes=[pk(b), 'sg_sb'])
                    S.op(V, lambda e, b=b, j=j: e.scalar_tensor_tensor(out=hid[:, j, :], in0=PB[b][:], scalar=0.0, in1=sg_sb[:], op0=ALU.is_gt, op1=ALU.mult),
                         reads=['sg_sb'], writes=[pk(b), ('hid', j)])
                for i in range(32):
                    r0 = (fb * 32 + i) * 128
                    slot = wload(w_down_v[r0:r0 + 128, :], [('wdownbf', r0 // 2 // 1024)], n=2048)
                    b = nbank()

                    def mmd(e, slot=slot, b=b):
                        for kc in range(16):
                            inst = e.matmul(PB[b][:], ring[:, slot, kc * 128:(kc + 1) * 128], hid[:, kc, :], start=(kc == 0), stop=(kc == 15))
                        return inst
                    S.op('pe', mmd, reads=[('bring', slot)] + [('hid', j) for j in range(16)], writes=[pk(b)])
                    S.op(V, lambda e, i=i, b=b: e.tensor_tensor(out=acc[:, i, :], in0=PB[b][:], in1=acc[:, i, :], op=ALU.add), writes=[pk(b), ('acc', i)])
                    if fb == 7:
                        stat_accum(i)
            ckpt(2, tl)
            layer_norm(1, False)
            ckpt(3, tl)
            for i in range(32):
                slot = wload(w_gate_bf[i * 128:(i + 1) * 128, :], [('wgatebf', i // 8)])
                b = nbank()

                def mmg(e, slot=slot, b=b):
                    for kc in range(32):
                        inst = e.matmul(PB[b][:], ring[:, slot, kc * 128:(kc + 1) * 128], hb[:, kc, :], start=(kc == 0), stop=(kc == 31))
                    return inst
                S.op('pe', mmg, reads=[('bring', slot)] + HB, writes=[pk(b)])
                S.op('act', lambda e, b=b: e.activation(out=sg_sb[:], in_=PB[b][:], func=AF.Sigmoid), writes=[pk(b), 'sg_sb'])
                b2 = nbank()

                def mmp(e, b2=b2, i=i):
                    for kc in range(2):
                        inst = e.matmul(PB[b2][:], wp[:, i, kc * 128:(kc + 1) * 128], ptb[:, kc, :], start=(kc == 0), stop=(kc == 1))
                    return inst
                S.op('pe', mmp, reads=['wp', 'ptb'], writes=[pk(b2)])
                S.op(V, lambda e, b2=b2: e.tensor_tensor(out=pe_sb[:], in0=PB[b2][:], in1=sg_sb[:], op=ALU.mult), reads=['sg_sb'], writes=[pk(b2), 'pe_sb'])
                S.op(V, lambda e, i=i: e.tensor_tensor(out=acc[:, i, :], in0=acc[:, i, :], in1=pe_sb[:], op=ALU.add), reads=['pe_sb'], writes=[('acc', i)])
                stat_accum(i)
            ckpt(4, tl)
            layer_norm(2, True)
            S.op('act', lambda e, tl=tl: e.dma_start(out=outT[:, tl * TT:(tl + 1) * TT].rearrange("(a p) t -> p a t", p=128), in_=acc[:]),
                 reads=ACC, writes=[('out', tl)], stream='outst')
        S.flush(final_streams=['ld', 'st', 'cc', 'cast'])


_NC = None


def _prep(inputs):
    f = np.float32
    g = lambda k: np.asarray(inputs[k], dtype=f)[0]
    x = np.asarray(inputs["x"], dtype=f)
    p = np.asarray(inputs["p"], dtype=f)[0]
    w_in = g("w_in")
    D_SSM = 2048
    OFF_R = 6176

    def blkfmt(w):
        K, C = w.shape
        nb, kc = C // 128, K // 128
        return np.ascontiguousarray(w.reshape(kc, 128, nb, 128).transpose(2, 1, 0, 3)).reshape(nb * 128 * kc * 128 // 4096, 4096)

    w_up = blkfmt(g("w_up"))
    wd = g("w_down")
    w_down = np.ascontiguousarray(wd.reshape(8, 16, 128, 32, 128).transpose(0, 3, 2, 1, 4)).reshape(-1, 4096)
    w_gate = blkfmt(g("w_ple_gate"))
    w_ple = blkfmt(g("w_ple"))
    wo = g("w_out")
    lnp = np.stack([g(k).reshape(32, 128).T for k in ["ln1_g", "ln1_b", "ln2_g", "ln2_b", "ln3_g", "ln3_b"]], axis=1)
    lnp = np.ascontiguousarray(lnp)
    per_half = []
    for hf in range(2):
        cs = slice(hf * 1024, (hf + 1) * 1024)
        cols = []
        cols += list(range(D_SSM + hf * 1024, D_SSM + (hf + 1) * 1024))
        cols += list(range(2 * D_SSM + hf * 512, 2 * D_SSM + (hf + 1) * 512))
        cols += list(range(2 * D_SSM + 1024 + hf * 512, 2 * D_SSM + 1024 + (hf + 1) * 512))
        conv_cols = [c - D_SSM for c in cols]
        for part in range(3):
            cols += list(range(OFF_R + part * 2048 + hf * 1024, OFF_R + part * 2048 + (hf + 1) * 1024))
        lo = OFF_R + 3 * 2048
        wcols = np.zeros((4096, NBLK * 128), f)
        wcols[:, :40 * 128] = w_in[:, cols]
        wcols[:, 40 * 128:40 * 128 + 96] = w_in[:, lo:lo + 96]
        wcols[:, 41 * 128:41 * 128 + 96] = w_in[:, lo + 96:lo + 192]
        wcols[:, 42 * 128:44 * 128] = w_in[:, lo + 192:lo + 448]
        wcols[:, 44 * 128:52 * 128] = w_in[:, hf * 1024:(hf + 1) * 1024]
        dt0 = D_SSM + 4096 + hf * 16
        wcols[:, 52 * 128:52 * 128 + 16] = w_in[:, dt0:dt0 + 16]
        w_in_c = blkfmt(wcols)
        cw = np.ascontiguousarray(g("conv_w")[:, conv_cols].T.reshape(16, 128, 4).transpose(1, 0, 2))
        cb = np.ascontiguousarray(g("conv_b")[conv_cols].reshape(16, 128).T)
        mu_full = g("rwkv_mu")
        mu_cols = np.zeros((28, 128), f)
        for part in range(3):
            mu_cols[part * 8:(part + 1) * 8] = mu_full[part * 2048 + hf * 1024:part * 2048 + (hf + 1) * 1024].reshape(8, 128)
        mu_cols[24, :96] = mu_full[6144:6240]
        mu_cols[25, :96] = mu_full[6240:6336]
        mu_cols[26:28] = mu_full[6336:6592].reshape(2, 128)
        mu_c = np.ascontiguousarray(mu_cols.T)
        chp = np.stack([g(k).reshape(-1)[cs].reshape(8, 128).T for k in ["w0", "a0", "k_k", "k_a", "r_k"]], axis=1)
        chp = np.ascontiguousarray(chp)
        Dfull = np.repeat(g("D_skip")[hf * 16:(hf + 1) * 16], 64)
        tmb = np.stack([g("ssm_norm_g")[cs], Dfull, g("gn_g")[cs], g("gn_b")[cs]], axis=0)
        tmb = np.ascontiguousarray(np.broadcast_to(tmb[None], (128, 4, 1024)))
        dtp = np.stack([g("dt_bias")[hf * 16:(hf + 1) * 16], g("A_log")[hf * 16:(hf + 1) * 16]], axis=0)
        dtp = np.ascontiguousarray(np.broadcast_to(dtp[None], (128, 2, 16)))
        wdb = np.ascontiguousarray(g("w_decay_b")[:, cs])
        wab = np.ascontiguousarray(g("w_aaa_b")[:, cs])
        wgb = np.ascontiguousarray(g("w_gate_b")[:, cs].reshape(2, 128, 1024).transpose(1, 0, 2))
        sel = np.zeros((128, 2), f)
        sel[:, hf] = 1.0
        per_half.append(dict(w_in=w_in_c, cw=cw, cb=cb, mu=mu_c, chp=chp, tmb=tmb, dtp=dtp, wdb=wdb, wab=wab, wgb=wgb, sel=sel))
    perm = np.concatenate([np.arange(0, 1024), np.arange(2048, 3072), np.arange(1024, 2048), np.arange(3072, 4096)])
    w_out = blkfmt(wo[perm])
    shared = dict(w_out=w_out, w_up=w_up, w_down=w_down, w_gate=w_gate, w_ple=w_ple, lnp=lnp)
    in_maps = []
    for c in range(8):
        b, hf = c // 2, c % 2
        xT = np.ascontiguousarray(x[b].T)
        m = dict(shared)
        m.update(per_half[hf])
        m["xT"] = xT
        m["xTr"] = np.ascontiguousarray(xT[:, hf * 2048:(hf + 1) * 2048])
        m["pT"] = np.ascontiguousarray(p[b].T[:, hf * 2048:(hf + 1) * 2048].reshape(2, 128, 2048).transpose(1, 0, 2))
        in_maps.append(m)
    return in_maps


def kernel(**inputs):
    global _NC
    in_maps = _prep(inputs)
    if _NC is None:
        _NC = build_nc()
    res = run_bass_kernel_spmd(_NC, in_maps, core_ids=list(range(8)))
    out = np.empty((4, SEQ, D), np.float32)
    for c in range(8):
        b, hf = c // 2, c % 2
        out[b, hf * 2048:(hf + 1) * 2048, :] = np.asarray(res.results[c]["outT"]).T
    return out
```

```python
import math
import numpy as np
import concourse.bass as bass
import concourse.mybir as mybir
from concourse.bass_utils import run_bass_kernel_spmd

F32, BF16 = mybir.dt.float32, mybir.dt.bfloat16
AF = mybir.ActivationFunctionType
ALU = mybir.AluOpType
AX = mybir.AxisListType

D = 4096
SEQ = 4096
NT = 8
TT = 512
NCH = 32
ALPHA = 2.0 ** 0.25
CDEC = math.exp(-0.5)
NBLK = 53
NFM = 44
B_XS, B_B, B_C, B_R, B_K, B_V, B_XW, B_XA, B_XG = 0, 8, 12, 16, 24, 32, 40, 41, 42
BLK_Z, BLK_DT = 44, 52
FMW = [128] * 40 + [96, 96, 128, 128]
UPAD = 4
NTM = 1040


class Sched:
    ENG = ['pe', 'act', 'dve', 'pool', 'sp']

    def __init__(self, nc):
        self.nc = nc
        self.ops = []
        self.buf = {}
        self.emitted = 0
        self.cnt = {}
        self.waited = {e: {} for e in self.ENG}
        self.sems = {}
        self.lastq = {}

    def op(self, eng, fn, reads=(), writes=(), stream=None, inc=16):
        oid = len(self.ops)
        deps = set()
        for b in reads:
            st = self.buf.get(b)
            if st and st[0] is not None:
                deps.add(st[0])
        for b in writes:
            st = self.buf.get(b)
            if st:
                if st[0] is not None:
                    deps.add(st[0])
                deps.update(st[1])
        for b in reads:
            self.buf.setdefault(b, [None, []])[1].append(oid)
        for b in writes:
            self.buf[b] = [oid, []]
        if stream:
            prev = self.lastq.get(stream)
            if prev is not None:
                deps.add(prev)
            self.lastq[stream] = oid
        deps.discard(oid)
        self.ops.append(dict(eng=eng, fn=fn, deps=deps, stream=stream, inc=inc, sig=None))
        return oid

    def sem(self, k):
        if k not in self.sems:
            self.sems[k] = self.nc.alloc_semaphore(name="sem_%s_%s" % k)
        return self.sems[k]

    def flush(self, final_streams=()):
        ops = self.ops[self.emitted:]
        base = self.emitted
        n = len(self.ops)
        dependents = [False] * n
        for o in ops:
            for d in o['deps']:
                dependents[d] = True
        for i, o in enumerate(ops):
            if o['stream']:
                k = ('s', o['stream'])
                self.cnt[k] = self.cnt.get(k, 0) + o['inc']
                o['sig'] = (k, self.cnt[k])
            elif dependents[base + i] or i == len(ops) - 1:
                k = ('e', o['eng'])
                self.cnt[k] = self.cnt.get(k, 0) + 1
                o['sig'] = (k, self.cnt[k])
        final_streams = [k[1] for k in self.cnt if k[0] == 's' and (not k[1].startswith('cast') or 'cast' in final_streams)]
        streams_final = [(('s', s), self.cnt.get(('s', s), 0)) for s in final_streams]
        with self.nc.Block() as block:
            engs = {'pe': block.tensor, 'act': block.scalar, 'dve': block.vector,
                    'pool': block.gpsimd, 'sp': block.sync}
            for e in self.ENG:
                mine = [o for o in ops if o['eng'] == e]
                if not mine and e != 'sp':
                    continue

                def body(engine, mine=mine, e=e):
                    waited = self.waited[e]
                    for o in mine:
                        for d in sorted(o['deps']):
                            if self.ops[d]['sig'] is None:
                                continue
                            k, v = self.ops[d]['sig']
                            if waited.get(k, 0) < v:
                                engine.wait_ge(self.sem(k), v)
                                waited[k] = v
                        inst = o['fn'](engine)
                        if o['sig'] is not None:
                            inst.then_inc(self.sem(o['sig'][0]), o['inc'] if o['stream'] else 1)
                    if e == 'sp':
                        for k, v in streams_final:
                            if v and waited.get(k, 0) < v:
                                engine.wait_ge(self.sem(k), v)
                                waited[k] = v
                engs[e](body)
        self.emitted = n


DEBUG_STAGE = 0


def build_nc():
    nc = bass.Bass("TRN2", target_bir_lowering=False)

    def din(name, shape):
        return nc.dram_tensor(name, list(shape), F32, kind="ExternalInput").ap()

    xT = din("xT", [D, SEQ])
    w_in = din("w_in", [NBLK * 128, 4096])
    w_out = din("w_out", [32 * 128, 4096])
    w_up = din("w_up", [128 * 128, 4096])
    w_down = din("w_down", [8 * 32 * 128 // 2, 4096])
    w_gate = din("w_gate", [32 * 128, 4096])
    w_ple = din("w_ple", [256, 4096])
    pT = din("pT", [128, 2, 2048])
    cw = din("cw", [128, 16, 4])
    cb = din("cb", [128, 16])
    mu = din("mu", [128, 28])
    chp = din("chp", [128, 5, 8])
    tmb = din("tmb", [128, 4, 1024])
    dtp = din("dtp", [128, 2, 16])
    wdb = din("wdb", [96, 1024])
    wab = din("wab", [96, 1024])
    wgb = din("wgb", [128, 2, 1024])
    lnp = din("lnp", [128, 6, 32])
    xTr = din("xTr", [D, 2048])
    sel = din("sel", [128, 2])
    outT = nc.dram_tensor("outT", [D, 2048], F32, kind="ExternalOutput").ap()

    xT_bf = nc.dram_tensor("xT_bf", [D, SEQ], BF16).ap()
    w_in_bf = nc.dram_tensor("w_in_bf", [NBLK * 128, 4096], BF16).ap()
    w_out_bf = nc.dram_tensor("w_out_bf", [32 * 128, 4096], BF16).ap()
    w_up_bf = nc.dram_tensor("w_up_bf", [128 * 128, 4096], BF16).ap()
    w_down_bf = nc.dram_tensor("w_down_bf", [8 * 32 * 64, 4096], BF16).ap()
    w_gate_bf = nc.dram_tensor("w_gate_bf", [32 * 128, 4096], BF16).ap()
    w_ple_bf = nc.dram_tensor("w_ple_bf", [256, 4096], BF16).ap()
    uF = nc.dram_tensor("uF", [NFM, 128, UPAD + SEQ], F32).ap()
    uT = nc.dram_tensor("uT", [SEQ, NTM], F32).ap()
    agin = [nc.dram_tensor("agin%d" % t, [2048, 256], F32) for t in range(NT)]
    agout = [nc.dram_tensor("agout%d" % t, [4096, 256], F32) for t in range(NT)]

    S = Sched(nc)

    def sb(name, shape, dt):
        return nc.sbuf_tensor(name, list(shape), dt)

    ncast = [0]

    castq = []

    def cast(*args):
        castq.append(args)

    def issue_casts(n):
        for _ in range(min(n, len(castq))):
            cast_now(*castq.pop(0))
            nissued[0] += 1

    nissued = [0]

    def ensure_x(tt):
        last = 1 if tt == 0 else 16 + 2 * (tt - 1) + 1
        while nissued[0] <= last and castq:
            issue_casts(1)

    def cast_now(dst, src, r0, r1, key, c0=None, c1=None):
        slot = ('castslot', ncast[0] % 3)
        cstream = 'cast%d' % (ncast[0] % 3)
        ncast[0] += 1
        if c0 is None:
            S.op('pool', lambda e: e.dma_start(out=dst[r0:r1, :], in_=src[r0:r1, :], max_dma_last_dim=4096),
                 writes=[key, slot], stream=cstream)
        else:
            S.op('pool', lambda e: e.dma_start(out=dst[r0:r1, c0:c1], in_=src[r0:r1, c0:c1]),
                 writes=[key, slot], stream=cstream)

    def cast_rows(dst, src, nrows, name, step=1024):
        for r0 in range(0, nrows, 512):
            cast(dst, src, r0, min(nrows, r0 + 512), (name, r0 // step))

    cast(xT_bf, xT, 0, 2048, ('xTbf', 0), 0, TT)
    cast(xT_bf, xT, 2048, D, ('xTbf', 0), 0, TT)
    cast_rows(w_in_bf, w_in, NBLK * 128, 'winbf')
    for t in range(1, NT):
        cast(xT_bf, xT, 0, 2048, ('xTbf', t), t * TT, (t + 1) * TT)
        cast(xT_bf, xT, 2048, D, ('xTbf', t), t * TT, (t + 1) * TT)
    cast_rows(w_out_bf, w_out, 4096, 'woutbf')
    cast_rows(w_up_bf, w_up, 16384, 'wupbf')
    cast_rows(w_down_bf, w_down, 16384, 'wdownbf')
    cast_rows(w_gate_bf, w_gate, 4096, 'wgatebf')
    cast_rows(w_ple_bf, w_ple, 256, 'wplebf')
    issue_casts(6)

    with (nc.psum_tensor("pb0", [128, 512], F32) as pb0, nc.psum_tensor("pb1", [128, 512], F32) as pb1,
          nc.psum_tensor("pb2", [128, 512], F32) as pb2, nc.psum_tensor("pb3", [128, 512], F32) as pb3,
          nc.psum_tensor("pb4", [128, 512], F32) as pb4, nc.psum_tensor("pb5", [128, 512], F32) as pb5,
          nc.psum_tensor("pb6", [128, 512], F32) as pb6, nc.psum_tensor("pb7", [128, 512], F32) as pb7):
        PB = [pb0, pb1, pb2, pb3, pb4, pb5, pb6, pb7]

        def pk(i):
            return ('psum', i)

        if phase_a2(nc, S, PB, pk, sb, locals()):
            return nc
        phase_b(nc, S, PB, pk, sb, locals())
    return nc


def phase_a2(nc, S, PB, pk, sb, G):
    for which in (0, 1):
        _mixer_pass(nc, S, PB, pk, sb, G, which)
        if DEBUG_STAGE == 2 + which:
            dbgA = nc.dram_tensor("dbgA", [NT, 2048, 256], F32, kind="ExternalOutput").ap()
            dbgO = nc.dram_tensor("dbgO", [NT, 4096, 256], F32, kind="ExternalOutput").ap()
            for t in range(NT):
                S.op('sp', lambda e, t=t: e.dma_start(out=dbgA[t, :, :], in_=G['agin'][t].ap()[:, :]), writes=[('dbgA', t)], stream='ld')
                if which == 1:
                    S.op('sp', lambda e, t=t: e.dma_start(out=dbgO[t, :, :], in_=G['agout'][t].ap()[:, :]), writes=[('dbgO', t)], stream='ld')
            S.flush(final_streams=['ld', 'st', 'cast', 'cc'])
            return True
    return False


def _mixer_pass(nc, S, PB, pk, sb, G, which):
    SSD = which == 0
    YB = 0 if SSD else 7
    uF, uT, agin, agout = G['uF'], G['uT'], G['agin'], G['agout']
    cw_d, cb_d, mu_d, chp_d, tmb_d, dtp_d = G['cw'], G['cb'], G['mu'], G['chp'], G['tmb'], G['dtp']
    wdb_d, wab_d, wgb_d = G['wdb'], G['wab'], G['wgb']
    from contextlib import ExitStack
    with ExitStack() as es:
        grp = ['both']

        def T(name, shape, dt=F32):
            if grp[0] == 'ssd' and not SSD:
                return None
            if grp[0] == 'rwkv' and SSD:
                return None
            return es.enter_context(sb("a2_%d_" % which + name, shape, dt))
        cwt = T("cw", [128, 16, 4]); cbt = T("cb", [128, 16]); mut = T("mu", [128, 28]); omm = T("omm", [128, 28])
        chp = T("chp", [128, 5, 8]); omka = T("omka", [128, 8]); tmb = T("tmb", [128, 2, 1024])
        dtp = T("dtp", [128, 2, 16]); At = T("At", [128, 16])
        wdb = T("wdb", [96, 8, 128], BF16); wab = T("wab", [96, 8, 128], BF16); wgb = T("wgb", [128, 2, 1024], BF16)
        ones_f = T("ones_f", [128, 128]); mUi = T("mUi", [128, 128]); mUs = T("mUs", [128, 128]); mLs = T("mLs", [128, 128])
        idf = T("idf", [128, 128]); idb = T("idb", [128, 128], BF16); blk1 = T("blk1", [128, 128], BF16)
        sel2 = T("sel2", [128, 2], BF16)
        ssm_st = T("ssm_st", [128, 1024]); ssm_stb = T("ssm_stb", [128, 1024], BF16)
        rst = T("rst", [128, 8, 128]); rstb = T("rstb", [128, 8, 128], BF16)
        pm = T("pm", [128, 2]); blkf = T("blkf", [128, 128])
        ytm = T("ytm", [128, 1024], BF16)
        yT = T("yT", [128, 8, 512], BF16)
        t2 = T("t2", [128, 1024])
        grp[0] = 'ssd'
        usb = T("usb", [128, 16, 131]); cacc = T("cacc", [128, 16, 128]); cfm = T("cfm", [128, 16, 128])
        BCb = T("BCb", [128, 8, 128], BF16)
        ztm = T("ztm", [128, NTM]); zs = T("zs", [128, 1024])
        dt_t = T("dt_t", [128, 16]); a_t = T("a_t", [128, 16]); ea = T("ea", [128, 16]); cd = T("cd", [128, 16])
        dtd = T("dtd", [128, 16])
        xs32 = T("xs32", [128, 1024]); xdt = T("xdt", [128, 1024], BF16); xdtd = T("xdtd", [128, 1024], BF16)
        Btm = T("Btm", [128, 512], BF16)
        Rm = T("Rm", [128, 16, 128]); Em = T("Em", [128, 16, 128]); scm = T("scm", [128, 4, 128])
        MT = T("MT", [128, 16, 128], BF16)
        t1 = T("t1", [128, 1024]); vsq = T("vsq", [128, 1024])
        ss4 = T("ss4", [128, 4]); rs4 = T("rs4", [128, 4])
        grp[0] = 'rwkv'
        BhTz = T("BhTz", [128, 16, 128], BF16); KhTz = T("KhTz", [128, 16, 128], BF16); KtTz = T("KtTz", [128, 16, 128], BF16)
        ur = T("ur", [128, 28, 129]); sr = T("sr", [128, 28, 128])
        lob = T("lob", [128, 4, 128], BF16)
        dsig = T("dsig", [128, 8, 128]); asig = T("asig", [128, 8, 128]); Lr = T("Lr", [128, 8, 128])
        w1 = T("w1", [128, 8, 128]); w2 = T("w2", [128, 8, 128]); w3 = T("w3", [128, 8, 128])
        kap = T("kap", [128, 8, 128]); kpr = T("kpr", [128, 8, 128]); bb = T("bb", [128, 8, 128])
        kk2b = T("kk2b", [128, 8, 128], BF16)
        KR = T("KR", [128, 8, 256], BF16); BhT = T("BhT", [128, 8, 128], BF16); KhT = T("KhT", [128, 8, 128], BF16)
        BGT = T("BGT", [128, 8, 128], BF16); KGT = T("KGT", [128, 8, 128], BF16); rkrb = T("rkrb", [128, 8, 128], BF16)
        gC = T("gC", [128, 8])
        gtm = T("gtm", [128, 1024]); Vtm = T("Vtm", [128, 1024]); Vb = T("Vb", [128, 1024], BF16)
        BGtm = T("BGtm", [128, 1024], BF16); KGtm = T("KGtm", [128, 1024], BF16)
        X0 = T("X0", [128, 16, 128], BF16); X1 = T("X1", [128, 16, 128], BF16)
        XT0 = T("XT0", [128, 16, 128], BF16); XT1 = T("XT1", [128, 16, 128], BF16)
        Gm = T("Gm", [128, 16, 128], BF16); PTm = T("PTm", [128, 16, 128], BF16)
        BmT = T("BmT", [128, 16, 128], BF16); QTm = T("QTm", [128, 16, 128], BF16)
        Zb = T("Zb", [128, 1024], BF16); Ub = T("Ub", [128, 1024], BF16)
        yr = T("yr", [128, 1024]); s16a = T("s16a", [128, 16]); s16b = T("s16b", [128, 16])
        s16c = T("s16c", [128, 16]); s16d = T("s16d", [128, 16]); bs16 = T("bs16", [128, 16])

        def ld(dst, src, key):
            S.op('sp', lambda e: e.dma_start(out=dst, in_=src), writes=[key], stream='ldc')

        ld(cwt[:], cw_d, 'cwt'); ld(cbt[:], cb_d, 'cbt'); ld(mut[:], mu_d, 'mut'); ld(chp[:], chp_d, 'chp')
        ld(tmb[:], tmb_d[:, 0:2, :] if SSD else tmb_d[:, 2:4, :], 'tmb'); ld(dtp[:], dtp_d, 'dtp');
        S.op('pool', lambda e: e.dma_start(out=wdb[:].rearrange("p a b -> p (a b)"), in_=wdb_d), writes=['wdb'], stream='ld2')
        S.op('pool', lambda e: e.dma_start(out=wab[:].rearrange("p a b -> p (a b)"), in_=wab_d), writes=['wab'], stream='ld2')
        S.op('pool', lambda e: e.dma_start(out=wgb[:], in_=wgb_d), writes=['wgb'], stream='ld2')
        V = 'dve'
        S.op(V, lambda e: e.tensor_scalar(out=omm[:], in0=mut[:], scalar1=-1.0, scalar2=1.0, op0=ALU.mult, op1=ALU.add),
             reads=['mut'], writes=['omm'])
        S.op(V, lambda e: e.tensor_scalar(out=omka[:], in0=chp[:, 3, :], scalar1=-1.0, scalar2=1.0, op0=ALU.mult, op1=ALU.add),
             reads=['chp'], writes=['omka'])
        S.op('act', lambda e: e.activation(out=At[:], in_=dtp[:, 1, :], func=AF.Exp), reads=['dtp'], writes=['At0'])
        S.op(V, lambda e: e.tensor_scalar(out=At[:], in0=At[:], scalar1=-1.0, scalar2=None, op0=ALU.mult),
             reads=['At0'], writes=['At'])
        S.op(V, lambda e: e.memset(ones_f[:], 1.0), writes=['ones_f'])

        def mk_mask(dst, key, pat, cm, cmp, dt_is_bf=False):
            S.op('pool', lambda e: e.memset(dst[:], 1.0), writes=[key + '0'])
            S.op('pool', lambda e: e.affine_select(out=dst[:], in_=dst[:], pattern=[[pat, 128]], compare_op=cmp,
                                                     fill=0.0, base=0, channel_multiplier=cm),
                 reads=[key + '0'], writes=[key])
        mk_mask(mUi, 'mUi', 1, -1, ALU.is_ge)
        mk_mask(mUs, 'mUs', 1, -1, ALU.is_gt)
        mk_mask(mLs, 'mLs', -1, 1, ALU.is_gt)
        mk_mask(idf, 'idf', 1, -1, ALU.is_equal)
        S.op(V, lambda e: e.tensor_copy(out=idb[:], in_=idf[:]), reads=['idf'], writes=['idb'])
        S.op('pool', lambda e: e.memset(blkf[:], 0.0), writes=['blk1a'])
        S.op('pool', lambda e: e.memset(blkf[0:64, 0:64], 1.0), reads=['blk1a'], writes=['blk1b'])
        S.op('pool', lambda e: e.memset(blkf[64:128, 64:128], 1.0), reads=['blk1b'], writes=['blkf'])
        S.op(V, lambda e: e.tensor_copy(out=blk1[:], in_=blkf[:]), reads=['blkf'], writes=['blk1'])
        S.op('pool', lambda e: e.memset(pm[:], 0.0), writes=['pma'])
        S.op('pool', lambda e: e.memset(pm[0:64, 0:1], 1.0), reads=['pma'], writes=['pmb'])
        S.op('pool', lambda e: e.memset(pm[64:128, 1:2], 1.0), reads=['pmb'], writes=['pm'])
        S.op('pool', lambda e: e.memset(sel2[:], 0.0), writes=['sel2a'])
        S.op('pool', lambda e: e.memset(sel2[0:64, 0:1], 1.0), reads=['sel2a'], writes=['sel2b'])
        S.op('pool', lambda e: e.memset(sel2[64:128, 1:2], 1.0), reads=['sel2b'], writes=['sel2'])
        S.op(V, lambda e: e.memset(ssm_st[:], 0.0), writes=['ssm_st'])
        S.op(V, lambda e: e.memset(ssm_stb[:], 0.0), writes=['ssm_stb'])
        S.op(V, lambda e: e.memset(rst[:], 0.0), writes=['rst'])
        S.op(V, lambda e: e.memset(rstb[:], 0.0), writes=['rstb'])

        def bc(ap, shape):
            return ap.broadcast_to(shape)

        evi = [0]

        def evac_copy(out, in_, reads, writes):
            evi[0] += 1
            if evi[0] % 2:
                S.op('act', lambda e: e.activation(out=out, in_=in_, func=AF.Copy), reads=reads, writes=writes)
            else:
                S.op('dve', lambda e: e.tensor_copy(out=out, in_=in_), reads=reads, writes=writes)

        def p1a_gen(c):
            tt, q = c // 4, c % 4
            t0 = c * 128
            S.op('sp', lambda e, t0=t0: e.dma_start(
                out=ur[:], in_=uF[16:44, :, UPAD + t0 - 1:UPAD + t0 + 128].rearrange("j p t -> p j t")),
                reads=[('uF', j, tt) for j in range(16, 44)] + ([('uF', j, tt - 1) for j in range(16, 44)] if tt > 0 else [('uFpad', j) for j in range(16, 44)]),
                writes=['ur'], stream='ur')
            S.op(V, lambda e: e.tensor_tensor(out=sr[:], in0=ur[:, :, 1:129], in1=bc(omm[:].unsqueeze(2), [128, 28, 128]), op=ALU.mult),
                 reads=['ur', 'omm'], writes=['sr'])
            S.op('pool', lambda e: e.tensor_tensor(out=ur[:, :, 0:128], in0=ur[:, :, 0:128], in1=bc(mut[:].unsqueeze(2), [128, 28, 128]), op=ALU.mult),
                 reads=['mut'], writes=['ur'])
            S.op(V, lambda e: e.tensor_tensor(out=sr[:], in0=sr[:], in1=ur[:, :, 0:128], op=ALU.add), reads=['ur'], writes=['sr'])
            S.op('act', lambda e: e.activation(out=lob[0:96, 0, :], in_=sr[0:96, 24, :], func=AF.Tanh), reads=['sr'], writes=[('lob', 0)])
            S.op('act', lambda e: e.activation(out=lob[0:96, 1, :], in_=sr[0:96, 25, :], func=AF.Copy), reads=['sr'], writes=[('lob', 1)])
            S.op('act', lambda e: e.activation(out=lob[:, 2:4, :], in_=sr[:, 26:28, :], func=AF.Sigmoid), reads=['sr'], writes=[('lob', 2)])
            yield
            for wh, (wt, li, key) in enumerate([(wdb, 0, 'wdb'), (wab, 1, 'wab')]):
                for half in range(2):
                    bank = wh * 2 + half

                    def mm_l(e, wt=wt, li=li, half=half, bank=bank):
                        for i in range(4):
                            inst = e.matmul(PB[bank][:, i * 128:(i + 1) * 128], wt[0:96, half * 4 + i, :], lob[0:96, li, :],
                                            start=True, stop=True)
                        return inst
                    S.op('pe', mm_l, reads=[key, ('lob', li)], writes=[pk(bank)])
                    dst = dsig if wh == 0 else asig
                    S.op(V, lambda e, dst=dst, half=half, bank=bank, wh=wh: e.tensor_tensor(
                        out=dst[:, half * 4:(half + 1) * 4, :], in0=PB[bank][:].rearrange("p (a t) -> p a t", a=4),
                        in1=bc(chp[:, wh, half * 4:(half + 1) * 4].unsqueeze(2), [128, 4, 128]), op=ALU.add),
                        reads=['chp'], writes=[pk(bank), ('sig', wh, half)])
            S.op('act', lambda e: e.activation(out=dsig[:], in_=dsig[:], func=AF.Sigmoid),
                 reads=[('sig', 0, 0), ('sig', 0, 1)], writes=['dsig'])
            S.op('act', lambda e: e.activation(out=asig[:], in_=asig[:], func=AF.Sigmoid),
                 reads=[('sig', 1, 0), ('sig', 1, 1)], writes=['asig'])
            yield
            for fc in range(8):
                S.op(V, lambda e, fc=fc: e.tensor_tensor_scan(out=Lr[:, fc, :], data0=ones_f[:], data1=dsig[:, fc, :],
                                                               initial=0.0, op0=ALU.mult, op1=ALU.add),
                     reads=['dsig', 'ones_f'], writes=[('Lr', fc)])
            LrK = [('Lr', fc) for fc in range(8)]
            yield
            S.op(V, lambda e: e.tensor_tensor(out=kap[:], in0=sr[:, 8:16, :], in1=bc(chp[:, 2, :].unsqueeze(2), [128, 8, 128]), op=ALU.mult),
                 reads=['sr', 'chp'], writes=['kap'])
            S.op('pool', lambda e: e.tensor_tensor(out=kk2b[:], in0=kap[:], in1=kap[:], op=ALU.mult), reads=['kap'], writes=['kk2b'])
            yield
            for half in range(2):
                bank = 6 + half
                S.op('pe', lambda e, half=half, bank=bank: e.matmul(PB[bank][:], blk1[:], kk2b[:, half * 4:(half + 1) * 4, :], start=True, stop=True),
                     reads=['kk2b', 'blk1'], writes=[pk(bank)])
                S.op('act', lambda e, half=half, bank=bank: e.activation(out=w1[:, half * 4:(half + 1) * 4, :],
                                                                         in_=PB[bank][:].rearrange("p (a t) -> p a t", a=4), func=AF.Sqrt),
                     writes=[pk(bank), ('w1', half)])
            S.op(V, lambda e: e.tensor_scalar(out=w1[:], in0=w1[:], scalar1=1e-12, scalar2=None, op0=ALU.max),
                 reads=[('w1', 0), ('w1', 1)], writes=['w1'])
            S.op(V, lambda e: e.reciprocal(out=w1[:], in_=w1[:]), writes=['w1'])
            S.op(V, lambda e: e.tensor_tensor(out=kap[:], in0=kap[:], in1=w1[:], op=ALU.mult), reads=['w1', 'kk2b'], writes=['kap'])
            yield
            S.op('pool', lambda e: e.tensor_tensor(out=kpr[:], in0=asig[:], in1=bc(chp[:, 3, :].unsqueeze(2), [128, 8, 128]), op=ALU.mult),
                 reads=['asig', 'chp'], writes=['kpr'])
            S.op('pool', lambda e: e.tensor_tensor(out=kpr[:], in0=kpr[:], in1=bc(omka[:].unsqueeze(2), [128, 8, 128]), op=ALU.add),
                 reads=['omka'], writes=['kpr'])
            S.op(V, lambda e: e.tensor_tensor(out=kpr[:], in0=kpr[:], in1=sr[:, 8:16, :], op=ALU.mult), reads=['sr'], writes=['kpr'])
            S.op(V, lambda e: e.tensor_tensor(out=bb[:], in0=kap[:], in1=asig[:], op=ALU.mult), reads=['kap', 'asig'], writes=['bb'])
            yield

        nxt = None
        LrK = [('Lr', fc) for fc in range(8)]

        if SSD:
            xT_bf, w_in_bf = G['xT_bf'], G['w_in_bf']
            grp[0] = 'both'
            xts = T("a1_xT", [128, 32, TT], BF16)
            ring = T("a1_ring", [128, 4, 32, 128], BF16)
            stg = T("a1_st", [128, 4, 512], F32)
            zpad = T("a1_z", [128, UPAD], F32)
            S.op('dve', lambda e: e.memset(zpad[:], 0.0), writes=['zpad'])
            for j in range(NFM):
                S.op('act', lambda e, j=j: e.dma_start(out=uF[j, :, 0:UPAD], in_=zpad[:]),
                     reads=['zpad'], writes=[('uFpad', j)], stream='pad')
            ctr = {'psi': 0, 'sgi': 0, 'rsi': 0}

            def load_blk(blk, slot):
                S.op('sp', lambda e: e.dma_start(
                    out=ring[:, slot, :, :].rearrange("p k c -> p (k c)"),
                    in_=w_in_bf[blk * 128:(blk + 1) * 128, :]),
                    reads=[('winbf', (blk * 128) // 1024)], writes=[('ring', slot)], stream='ring%d' % slot)

            def a1_tile(tt):
                G['ensure_x'](tt)
                S.op('sp', lambda e: e.dma_start(
                    out=xts[:], in_=xT_bf[:, tt * TT:(tt + 1) * TT].rearrange("(kc p) t -> p kc t", p=128)),
                    reads=[('xTbf', tt)], writes=['xts'], stream='xts')
                for j in range(NFM):
                    if tt == 0 and j % 4 == 0:
                        G['issue_casts'](1)
                    slot = ctr['rsi'] % 4
                    ctr['rsi'] += 1
                    load_blk(j, slot)
                    m = FMW[j]
                    bank = 6 + ctr['psi'] % 2
                    ctr['psi'] += 1

                    def mm(e, slot=slot, m=m, bank=bank):
                        for kc in range(32):
                            inst = e.matmul(PB[bank][0:m, :], ring[:, slot, kc, 0:m], xts[:, kc, :],
                                            start=(kc == 0), stop=(kc == 31))
                        return inst
                    S.op('pe', mm, reads=[('ring', slot), 'xts'], writes=[pk(bank)])
                    sg = ctr['sgi'] % 4
                    ctr['sgi'] += 1
                    S.op('act', lambda e, sg=sg, m=m, bank=bank: e.activation(
                        out=stg[0:m, sg, :], in_=PB[bank][0:m, :], func=AF.Copy),
                        writes=[pk(bank), ('stg', sg)])
                    S.op('act', lambda e, sg=sg, m=m, j=j: e.dma_start(
                        out=uF[j, 0:m, UPAD + tt * TT:UPAD + (tt + 1) * TT], in_=stg[0:m, sg, :]),
                        reads=[('stg', sg)], writes=[('uF', j, tt)], stream='stg%d' % sg)
                    yield
                for g5 in range(5):
                    if tt == 0:
                        G['issue_casts'](1)
                    if ctr['rsi'] % 2:
                        ctr['rsi'] += 1
                    s0 = ctr['rsi'] % 4
                    nb = 2 if g5 < 4 else 1
                    for i in range(nb):
                        load_blk(BLK_Z + g5 * 2 + i, s0 + i)
                    ctr['rsi'] += 2
                    ncol = 256 if g5 < 4 else 16
                    for q in range(4):
                        bank = 6 + ctr['psi'] % 2
                        ctr['psi'] += 1

                        def mmt(e, s0=s0, nb=nb, ncol=ncol, bank=bank, q=q):
                            for kc in range(32):
                                if nb == 2:
                                    rhs = ring[:, s0:s0 + 2, kc, :]
                                else:
                                    rhs = ring[:, s0, kc, 0:16]
                                inst = e.matmul(PB[bank][:, 0:ncol], xts[:, kc, q * 128:(q + 1) * 128], rhs,
                                                start=(kc == 0), stop=(kc == 31))
                            return inst
                        S.op('pe', mmt, reads=[('ring', s0 + i) for i in range(nb)] + ['xts'], writes=[pk(bank)])
                        sg = ctr['sgi'] % 4
                        ctr['sgi'] += 1
                        S.op('act', lambda e, sg=sg, ncol=ncol, bank=bank: e.activation(
                            out=stg[:, sg, 0:ncol], in_=PB[bank][:, 0:ncol], func=AF.Copy), writes=[pk(bank), ('stg', sg)])
                        c0 = g5 * 256
                        S.op('act', lambda e, sg=sg, ncol=ncol, c0=c0, q=q: e.dma_start(
                            out=uT[tt * TT + q * 128:tt * TT + (q + 1) * 128, c0:c0 + ncol],
                            in_=stg[:, sg, 0:ncol]),
                            reads=[('stg', sg)], writes=[('uT', tt, q, g5)], stream='stg%d' % sg)
                        yield

        def chunk_gen(c):
            G['issue_casts'](1 if SSD else 2)
            tt, q = c // 4, c % 4
            t0 = c * 128
            yb = tt % 2
            if SSD:
                S.op('sp', lambda e, t0=t0: e.dma_start(
                    out=usb[:], in_=uF[0:16, :, UPAD + t0 - 3:UPAD + t0 + 128].rearrange("j p t -> p j t")),
                    reads=[('uF', j, tt) for j in range(16)] + ([('uF', j, tt - 1) for j in range(16)] if tt > 0 else [('uFpad', j) for j in range(16)]),
                    writes=['usb'], stream='usb')
                yield
                S.op('sp', lambda e, t0=t0: e.dma_start(out=ztm[:], in_=uT[t0:t0 + 128, :]),
                     reads=[('uT', tt, q, g) for g in range(5)], writes=['ztm'], stream='ztm')
                yield
                for k in range(4):
                    if k == 0:
                        S.op('pool', lambda e: e.tensor_tensor(out=cacc[:], in0=usb[:, :, 0:128],
                                                                in1=bc(cwt[:, :, 0:1], [128, 16, 128]), op=ALU.mult),
                             reads=['usb', 'cwt'], writes=['cacc'])
                    else:
                        S.op('pool', lambda e, k=k: e.tensor_tensor(out=cfm[:], in0=usb[:, :, k:k + 128],
                                                                     in1=bc(cwt[:, :, k:k + 1], [128, 16, 128]), op=ALU.mult),
                             reads=['usb', 'cwt'], writes=['cfm'])
                        S.op(V, lambda e: e.tensor_tensor(out=cacc[:], in0=cacc[:], in1=cfm[:], op=ALU.add),
                             reads=['cfm'], writes=['cacc'])
                yield
                S.op(V, lambda e: e.tensor_tensor(out=cacc[:], in0=cacc[:], in1=bc(cbt[:].unsqueeze(2), [128, 16, 128]), op=ALU.add),
                     reads=['cbt'], writes=['cacc'])
                yield
                S.op('act', lambda e: e.activation(out=cfm[:, 0:8, :], in_=cacc[:, 0:8, :], func=AF.Silu),
                     reads=['cacc'], writes=['cfm'])
                yield
                S.op('act', lambda e: e.activation(out=BCb[:], in_=cacc[:, 8:16, :], func=AF.Silu),
                     reads=['cacc'], writes=['BCb'])
                yield
                S.op(V, lambda e: e.tensor_tensor(out=dt_t[:], in0=ztm[:, 1024:1040], in1=dtp[:, 0, :], op=ALU.add),
                     reads=['ztm', 'dtp'], writes=['dt_t'])
                yield
                S.op('act', lambda e: e.activation(out=dt_t[:], in_=dt_t[:], func=AF.Exp), writes=['dt_t'])
                yield
                S.op('act', lambda e: e.activation(out=dt_t[:], in_=dt_t[:], func=AF.Ln, bias=1.0), writes=['dt_t'])
                yield
                S.op(V, lambda e: e.tensor_tensor(out=a_t[:], in0=dt_t[:], in1=At[:], op=ALU.mult),
                     reads=['dt_t', 'At'], writes=['a_t'])
                yield
                S.op('act', lambda e: e.activation(out=zs[:], in_=ztm[:, 0:1024], func=AF.Silu), reads=['ztm'], writes=['zs'])
                yield
                for half in range(2):
                    bank = half

                    def tr(e, half=half, bank=bank):
                        for i in range(4):
                            inst = e.transpose(PB[bank][:, i * 128:(i + 1) * 128], cfm[:, half * 4 + i, :], idf[:])
                        return inst
                    S.op('pe', tr, reads=['cfm', 'idf'], writes=[pk(bank)])
                    evac_copy(xs32[:, half * 512:(half + 1) * 512], PB[bank][:], [], [pk(bank), ('xs32', half)])
                yield

                def trb(e):
                    pbf = PB[2][:].bitcast(BF16)
                    for i in range(4):
                        inst = e.transpose(pbf[:, i * 128:(i + 1) * 128], BCb[:, i, :], idb[:])
                    return inst
                yield
                S.op('pe', trb, reads=['BCb', 'idb'], writes=[pk(2)])
                yield
                evac_copy(Btm[:], PB[2][:].bitcast(BF16)[:, 0:512], [], [pk(2), 'Btm'])
                yield
                S.op('pool', lambda e: e.tensor_tensor(out=Rm[:], in0=bc(a_t[:].unsqueeze(2), [128, 16, 128]),
                                                        in1=bc(mUi[:].unsqueeze(1), [128, 16, 128]), op=ALU.mult),
                     reads=['a_t', 'mUi'], writes=['Rm'])
                yield
                for i in range(4):
                    bank = 2 + i
                    S.op('pe', lambda e, i=i, bank=bank: e.matmul(PB[bank][:], mLs[:], Rm[:, i * 4:(i + 1) * 4, :], start=True, stop=True),
                         reads=['Rm', 'mLs'], writes=[pk(bank)])
                    S.op('act', lambda e, i=i, bank=bank: e.activation(out=Em[:, i * 4:(i + 1) * 4, :], in_=PB[bank][:], func=AF.Exp),
                         writes=[pk(bank), ('Em', i)])
                yield

                def mm_cs(e):
                    e.matmul(PB[1][:, 0:16], mUi[:], a_t[:], start=True, stop=True)
                    return e.matmul(PB[1][:, 16:32], ones_f[:], a_t[:], start=True, stop=True)
                yield
                S.op('pe', mm_cs, reads=['a_t', 'mUi', 'ones_f'], writes=[pk(1)])
                yield
                S.op('act', lambda e: e.activation(out=ea[:], in_=PB[1][:, 0:16], func=AF.Exp), writes=[pk(1), 'ea'])
                yield
                S.op('act', lambda e: e.activation(out=cd[:], in_=PB[1][:, 16:32], func=AF.Exp), writes=[pk(1), 'cd'])
                yield
                def mm_sc(e):
                    for g in range(4):
                        inst = e.matmul(PB[0][:, g * 128:(g + 1) * 128], BCb[:, g, :], BCb[:, 4 + g, :], start=True, stop=True)
                    return inst
                yield
                S.op('pe', mm_sc, reads=['BCb'], writes=[pk(0)])
                yield
                S.op(V, lambda e: e.tensor_tensor(out=scm[:], in0=PB[0][:].rearrange("p (g l) -> p g l", g=4),
                                                  in1=bc(mUi[:].unsqueeze(1), [128, 4, 128]), op=ALU.mult),
                     reads=['mUi'], writes=[pk(0), 'scm'])
                yield
                for g in range(4):
                    S.op('pool' if g % 2 else V, lambda e, g=g: e.tensor_tensor(
                        out=MT[:, g * 4:(g + 1) * 4, :], in0=Em[:, g * 4:(g + 1) * 4, :],
                        in1=bc(scm[:, g:g + 1, :], [128, 4, 128]), op=ALU.mult),
                        reads=[('Em', g), 'scm'], writes=[('MT', g)])
                yield
                S.op(V, lambda e: e.tensor_tensor(out=xdt[:].rearrange("p (h d) -> p h d", h=16),
                                                  in0=xs32[:].rearrange("p (h d) -> p h d", h=16),
                                                  in1=bc(dt_t[:].unsqueeze(2), [128, 16, 64]), op=ALU.mult),
                     reads=[('xs32', 0), ('xs32', 1), 'dt_t'], writes=['xdt'])
                yield
                S.op(V, lambda e: e.tensor_tensor(out=dtd[:].unsqueeze(2), in0=dt_t[:].unsqueeze(2), in1=Em[:, :, 127:128], op=ALU.mult),
                     reads=['dt_t'] + [('Em', i) for i in range(4)], writes=['dtd'])
                yield
                S.op(V, lambda e: e.tensor_tensor(out=xdtd[:].rearrange("p (h d) -> p h d", h=16),
                                                  in0=xs32[:].rearrange("p (h d) -> p h d", h=16),
                                                  in1=bc(dtd[:].unsqueeze(2), [128, 16, 64]), op=ALU.mult),
                     reads=[('xs32', 0), ('xs32', 1), 'dtd'], writes=['xdtd'])
                yield
                for half in range(2):
                    def mm_yd(e, half=half):
                        for hh in range(8):
                            h = half * 8 + hh
                            inst = e.matmul(PB[1 + half][:, hh * 64:(hh + 1) * 64], MT[:, h, :], xdt[:, h * 64:(h + 1) * 64],
                                            start=True, stop=True)
                        return inst
                    S.op('pe', mm_yd, reads=[('MT', g) for g in range(4)] + ['xdt'], writes=[pk(1 + half)])

                    def mm_yo(e, half=half):
                        for gg in range(2):
                            g = half * 2 + gg
                            inst = e.matmul(PB[3 + half][:, gg * 256:(gg + 1) * 256], BCb[:, 4 + g, :],
                                            ssm_stb[:, g * 256:(g + 1) * 256], start=True, stop=True)
                        return inst
                    S.op('pe', mm_yo, reads=['BCb', 'ssm_stb'], writes=[pk(3 + half)])

                    def mm_cs2(e, half=half):
                        for gg in range(2):
                            g = half * 2 + gg
                            inst = e.matmul(PB[(5 + half) % 6][:, gg * 256:(gg + 1) * 256], Btm[:, g * 128:(g + 1) * 128],
                                            xdtd[:, g * 256:(g + 1) * 256], start=True, stop=True)
                        return inst
                    S.op('pe', mm_cs2, reads=['Btm', 'xdtd'], writes=[pk((5 + half) % 6)])
                yield
                for half in range(2):
                    sl = slice(half * 512, (half + 1) * 512)
                    hs = slice(half * 8, (half + 1) * 8)
                    S.op(V, lambda e, half=half, sl=sl, hs=hs: e.tensor_tensor(
                        out=t1[:, sl].rearrange("p (h d) -> p h d", h=8), in0=PB[3 + half][:].rearrange("p (h d) -> p h d", h=8),
                        in1=bc(ea[:, hs].unsqueeze(2), [128, 8, 64]), op=ALU.mult),
                        reads=['ea'], writes=[pk(3 + half), ('t1', half)])
                    S.op('pool', lambda e, sl=sl: e.tensor_tensor(out=t2[:, sl], in0=xs32[:, sl], in1=tmb[:, 1, sl], op=ALU.mult),
                         reads=[('xs32', half), 'tmb'], writes=[('t2', half)])
                    S.op(V, lambda e, sl=sl: e.tensor_tensor(out=t1[:, sl], in0=t1[:, sl], in1=t2[:, sl], op=ALU.add),
                         reads=[('t2', half)], writes=[('t1', half)])
                    S.op(V, lambda e, half=half, sl=sl: e.tensor_tensor(out=t1[:, sl], in0=PB[1 + half][:], in1=t1[:, sl], op=ALU.add),
                         writes=[pk(1 + half), ('t1', half)])
                    S.op(V, lambda e, sl=sl, hs=hs: e.tensor_tensor(
                        out=ssm_st[:, sl].rearrange("p (h d) -> p h d", h=8), in0=ssm_st[:, sl].rearrange("p (h d) -> p h d", h=8),
                        in1=bc(cd[:, hs].unsqueeze(2), [128, 8, 64]), op=ALU.mult),
                        reads=['cd', 'ssm_stb'], writes=[('ssm_st', half)])
                    S.op(V, lambda e, half=half, sl=sl: e.tensor_tensor(out=ssm_st[:, sl], in0=PB[(5 + half) % 6][:], in1=ssm_st[:, sl], op=ALU.add),
                         writes=[pk((5 + half) % 6), ('ssm_st', half)])
                yield
                S.op('act', lambda e: e.activation(out=ssm_stb[:], in_=ssm_st[:], func=AF.Copy),
                     reads=[('ssm_st', 0), ('ssm_st', 1)], writes=['ssm_stb'])
                yield
                S.op(V, lambda e: e.tensor_tensor(out=t1[:], in0=t1[:], in1=zs[:], op=ALU.mult),
                     reads=['zs'], writes=[('t1', 0), ('t1', 1)])
                yield
                S.op('act', lambda e: e.activation(out=vsq[:], in_=t1[:], func=AF.Square), reads=[('t1', 0), ('t1', 1)], writes=['vsq'])
                yield
                S.op(V, lambda e: e.tensor_reduce(out=ss4[:], in_=vsq[:].rearrange("p (g d) -> p g d", g=4), axis=AX.X, op=ALU.add),
                     reads=['vsq'], writes=['ss4'])
                yield
                S.op(V, lambda e: e.tensor_scalar(out=ss4[:], in0=ss4[:], scalar1=1.0 / 256.0, scalar2=1e-5, op0=ALU.mult, op1=ALU.add),
                     writes=['ss4'])
                yield
                S.op('act', lambda e: e.activation(out=ss4[:], in_=ss4[:], func=AF.Sqrt), writes=['ss4'])
                yield
                S.op(V, lambda e: e.reciprocal(out=rs4[:], in_=ss4[:]), reads=['ss4'], writes=['rs4'])
                yield
                S.op(V, lambda e: e.tensor_tensor(out=t1[:].rearrange("p (g d) -> p g d", g=4), in0=t1[:].rearrange("p (g d) -> p g d", g=4),
                                                  in1=bc(rs4[:].unsqueeze(2), [128, 4, 256]), op=ALU.mult),
                     reads=['rs4', 'vsq'], writes=[('t1', 0), ('t1', 1)])
                yield
                S.op(V, lambda e: e.tensor_tensor(out=ytm[:], in0=t1[:], in1=tmb[:, 0, :], op=ALU.mult),
                     reads=[('t1', 0), ('t1', 1), 'tmb'], writes=['ytm'])
                yield

            else:
                if c == 0:
                    for _ in p1a_gen(0):
                        pass
                nxt = p1a_gen(c + 1) if c + 1 < NCH else iter(())
                for half in range(2):
                    bank = 4 + half

                    def mm_g(e, half=half, bank=bank):
                        for kc in range(2):
                            inst = e.matmul(PB[bank][:], lob[:, 2 + kc, :], wgb[:, kc, half * 512:(half + 1) * 512],
                                            start=(kc == 0), stop=(kc == 1))
                        return inst
                    S.op('pe', mm_g, reads=[('lob', 2), 'wgb'], writes=[pk(bank)])
                    evac_copy(gtm[:, half * 512:(half + 1) * 512], PB[bank][:], [], [pk(bank), ('gtm', half)])
                S.op('act', lambda e: e.activation(out=w2[:], in_=Lr[:], func=AF.Exp, scale=-CDEC), reads=LrK, writes=['w2'])
                S.op(V, lambda e: e.tensor_copy(out=gC[:].unsqueeze(2), in_=w2[:, :, 127:128]), reads=['w2'], writes=['gC'])
                S.op(V, lambda e: e.tensor_tensor(out=KR[:, :, 128:256], in0=sr[:, 0:8, :], in1=w2[:], op=ALU.mult),
                     reads=['sr', 'w2'], writes=[('KR', 1)])
                S.op('act', lambda e: e.activation(out=w3[:], in_=Lr[:], func=AF.Exp, scale=CDEC), reads=LrK, writes=['w3'])
                S.op(V, lambda e: e.tensor_tensor(out=BhT[:], in0=bb[:], in1=w3[:], op=ALU.mult), reads=['bb', 'w3'], writes=['BhT'])
                S.op('pool', lambda e: e.tensor_tensor(out=KhT[:], in0=kpr[:], in1=w3[:], op=ALU.mult), reads=['kpr', 'w3'], writes=['KhT'])
                S.op(V, lambda e: e.tensor_tensor(out=w1[:], in0=Lr[:], in1=dsig[:], op=ALU.subtract), reads=LrK + ['dsig', 'kap'], writes=['w1'])
                S.op('act', lambda e: e.activation(out=w1[:], in_=w1[:], func=AF.Exp, scale=-CDEC), writes=['w1'])
                S.op(V, lambda e: e.tensor_tensor(out=KR[:, :, 0:128], in0=kap[:], in1=w1[:], op=ALU.mult), reads=['kap', 'w1'], writes=[('KR', 0)])
                S.op(V, lambda e: e.tensor_tensor(out=w2[:], in0=Lr[:], in1=bc(Lr[:, :, 127:128], [128, 8, 128]), op=ALU.subtract),
                     reads=LrK + ['gC', ('KR', 1)], writes=['w2'])
                S.op('act', lambda e: e.activation(out=w2[:], in_=w2[:], func=AF.Exp, scale=CDEC), writes=['w2'])
                S.op(V, lambda e: e.tensor_tensor(out=BGT[:], in0=bb[:], in1=w2[:], op=ALU.mult), reads=['bb', 'w2'], writes=['BGT'])
                S.op('pool', lambda e: e.tensor_tensor(out=KGT[:], in0=kpr[:], in1=w2[:], op=ALU.mult), reads=['kpr', 'w2'], writes=['KGT'])
                S.op('pool', lambda e: e.tensor_tensor(out=w3[:], in0=sr[:, 0:8, :], in1=kpr[:], op=ALU.mult),
                     reads=['sr', 'kpr', 'BhT', 'KhT'], writes=['w3'])
                S.op('pool', lambda e: e.tensor_tensor(out=rkrb[:], in0=w3[:], in1=bc(chp[:, 4, :].unsqueeze(2), [128, 8, 128]), op=ALU.mult),
                     reads=['w3', 'chp'], writes=['rkrb'])
                for half in range(2):
                    bank = half

                    def trv(e, half=half, bank=bank):
                        for i in range(4):
                            inst = e.transpose(PB[bank][:, i * 128:(i + 1) * 128], sr[:, 16 + half * 4 + i, :], idf[:])
                        return inst
                    S.op('pe', trv, reads=['sr', 'idf'], writes=[pk(bank)])
                    S.op('act', lambda e, half=half, bank=bank: e.activation(out=Vtm[:, half * 512:(half + 1) * 512], in_=PB[bank][:], func=AF.Copy),
                         writes=[pk(bank), ('Vtm', half)])
                    S.op(V, lambda e, half=half, bank=bank: e.tensor_copy(out=Vb[:, half * 512:(half + 1) * 512], in_=PB[bank][:]),
                         writes=[pk(bank), ('Vb', half)])
                VbK = [('Vb', 0), ('Vb', 1)]
                for wi, (src, dst, key) in enumerate([(BGT, BGtm, 'BG'), (KGT, KGtm, 'KG')]):
                    bank = 2 + wi

                    def trg(e, src=src, bank=bank):
                        pbf = PB[bank][:].bitcast(BF16)
                        for i in range(8):
                            inst = e.transpose(pbf[:, i * 128:(i + 1) * 128], src[:, i, :], idb[:])
                        return inst
                    S.op('pe', trg, reads=[key + 'T', 'idb'], writes=[pk(bank)])
                    evac_copy(dst[:], PB[bank][:].bitcast(BF16), [], [pk(bank), key + 'tm'])
                def mm_bs(e):
                    for hp in range(8):
                        inst = e.matmul(PB[4][:, hp * 2:(hp + 1) * 2], rkrb[:, hp, :], sel2[:], start=True, stop=True)
                    return inst
                S.op('pe', mm_bs, reads=['rkrb', 'sel2'], writes=[pk(4)])
                S.op(V, lambda e: e.tensor_copy(out=bs16[:], in_=PB[4][:, 0:16]), writes=[pk(4), 'bs16'])
                for zi, (srcT, dstT, skey, dkey) in enumerate([(BhT, BhTz, 'BhT', 'BhTz'), (KhT, KhTz, 'KhT', 'KhTz'), (None, KtTz, ('KR', 0), 'KtTz')]):
                    for par in range(2):
                        src_ap = KR[:, :, 0:128] if srcT is None else srcT[:]
                        dst_ap = dstT[:].rearrange("p (a two) t -> p a two t", two=2)[:, :, par, :]
                        if (zi + par) % 2 == 0:
                            S.op('act', lambda e, src_ap=src_ap, dst_ap=dst_ap, par=par: e.activation(out=dst_ap, in_=src_ap, func=AF.Copy, scale=pm[:, par:par + 1]),
                                 reads=[skey, 'pm'], writes=[(dkey, par)])
                        else:
                            S.op(V, lambda e, src_ap=src_ap, dst_ap=dst_ap, par=par: e.tensor_scalar(out=dst_ap, in0=src_ap, scalar1=pm[:, par:par + 1], scalar2=None, op0=ALU.mult),
                                 reads=[skey, 'pm'], writes=[(dkey, par)])
                ZK = [(k_, p_) for k_ in ('BhTz', 'KhTz', 'KtTz') for p_ in range(2)]
                for g4 in range(4):
                    hs = range(g4 * 4, g4 * 4 + 4)
                    bA = [5, 6]
                    bB = [7, 0]
                    bA2 = 1

                    def mm_in(e, hs=hs):
                        for i, h in enumerate(hs):
                            hp = h // 2
                            e.matmul(PB[bA[i // 2]][:, (i % 2) * 256:(i % 2 + 1) * 256], BhTz[:, h, :], KR[:, hp, :], start=True, stop=True)
                            e.matmul(PB[bB[i // 2]][:, (i % 2) * 256:(i % 2 + 1) * 256], KhTz[:, h, :], KR[:, hp, :], start=True, stop=True)
                            inst = e.matmul(PB[bA2][:, i * 128:(i + 1) * 128], KtTz[:, h, :], BhT[:, hp, :], start=True, stop=True)
                        return inst
                    S.op('pe', mm_in, reads=['BhT', 'KhT', ('KR', 0), ('KR', 1)] + ZK, writes=[pk(5), pk(6), pk(7), pk(0), pk(1)])
                    for i2 in range(2):
                        hh = slice(g4 * 4 + i2 * 2, g4 * 4 + i2 * 2 + 2)
                        pa = PB[bA[i2]][:].rearrange("p (h w t) -> p h w t", h=2, w=2)
                        pb_ = PB[bB[i2]][:].rearrange("p (h w t) -> p h w t", h=2, w=2)
                        S.op(V, lambda e, hh=hh, pa=pa: e.scalar_tensor_tensor(
                            out=X0[:, hh, :], in0=pa[:, :, 0, :], scalar=-1.0, in1=bc(mUs[:].unsqueeze(1), [128, 2, 128]),
                            op0=ALU.mult, op1=ALU.mult), reads=['mUs'], writes=[pk(bA[i2]), ('X0', g4, i2)])
                        S.op(V, lambda e, hh=hh, pa=pa: e.tensor_tensor(
                            out=PTm[:, hh, :], in0=pa[:, :, 1, :], in1=bc(mUi[:].unsqueeze(1), [128, 2, 128]), op=ALU.mult),
                            reads=['mUi'], writes=[pk(bA[i2]), ('PT', g4, i2)])
                        S.op(V, lambda e, hh=hh, pb_=pb_: e.tensor_tensor(
                            out=BmT[:, hh, :], in0=pb_[:, :, 0, :], in1=bc(mUs[:].unsqueeze(1), [128, 2, 128]), op=ALU.mult),
                            reads=['mUs'], writes=[pk(bB[i2]), ('BmT', g4, i2)])
                        S.op(V, lambda e, hh=hh, pb_=pb_: e.tensor_tensor(
                            out=QTm[:, hh, :], in0=pb_[:, :, 1, :], in1=bc(mUi[:].unsqueeze(1), [128, 2, 128]), op=ALU.mult),
                            reads=['mUi'], writes=[pk(bB[i2]), ('QT', g4, i2)])
                    h4 = slice(g4 * 4, g4 * 4 + 4)
                    S.op(V, lambda e, h4=h4: e.scalar_tensor_tensor(
                        out=XT0[:, h4, :], in0=PB[bA2][:].rearrange("p (h t) -> p h t", h=4), scalar=-1.0,
                        in1=bc(mLs[:].unsqueeze(1), [128, 4, 128]), op0=ALU.mult, op1=ALU.mult),
                        reads=['mLs'], writes=[pk(bA2), ('XT0', g4)])
                    S.op('pool', lambda e, h4=h4: e.tensor_tensor(out=Gm[:, h4, :], in0=X0[:, h4, :], in1=bc(idb[:].unsqueeze(1), [128, 4, 128]), op=ALU.add),
                         reads=[('X0', g4, 0), ('X0', g4, 1), 'idb'], writes=[('G', g4)])
                Xs, XTs = [X0, X1], [XT0, XT1]
                for r in range(1, 8):
                    for _ in range(2):
                        next(nxt, None)
                    src, dst = (r - 1) % 2, r % 2
                    for g4 in range(4):
                        h4 = slice(g4 * 4, g4 * 4 + 4)
                        kx_src = [('X0', g4, 0), ('X0', g4, 1)] if r == 1 else [('X', src, g4)]
                        kxt_src = [('XT0', g4)] if r == 1 else [('XT', src, g4)]
                        banks = [(g4 * 2) % 8, (g4 * 2 + 1) % 8, 0]
                        b0, b1 = (2 + (r * 4 + g4) * 3) % 8, (3 + (r * 4 + g4) * 3) % 8
                        b2 = (4 + (r * 4 + g4) * 3) % 8
                        if r <= 5:
                            def mmx(e, h4=h4, src=src, b0=b0):
                                for i, h in enumerate(range(h4.start, h4.stop)):
                                    inst = e.matmul(PB[b0][:, i * 128:(i + 1) * 128], XTs[src][:, h, :], Xs[src][:, h, :], start=True, stop=True)
                                return inst
                            S.op('pe', mmx, reads=kx_src + kxt_src, writes=[pk(b0)])
                            evac_copy(Xs[dst][:, h4, :], PB[b0][:].rearrange("p (h t) -> p h t", h=4), [], [pk(b0), ('X', dst, g4)])
                        if r <= 6:
                            def mmxt(e, h4=h4, src=src, b1=b1):
                                for i, h in enumerate(range(h4.start, h4.stop)):
                                    inst = e.matmul(PB[b1][:, i * 128:(i + 1) * 128], Xs[src][:, h, :], XTs[src][:, h, :], start=True, stop=True)
                                return inst
                            S.op('pe', mmxt, reads=kx_src + kxt_src, writes=[pk(b1)])
                            evac_copy(XTs[dst][:, h4, :], PB[b1][:].rearrange("p (h t) -> p h t", h=4), [], [pk(b1), ('XT', dst, g4)])
                        if r >= 2:
                            def mmg(e, h4=h4, src=src, b2=b2):
                                for i, h in enumerate(range(h4.start, h4.stop)):
                                    inst = e.matmul(PB[b2][:, i * 128:(i + 1) * 128], XTs[src][:, h, :], Gm[:, h, :], start=True, stop=True)
                                return inst
                            S.op('pe', mmg, reads=kxt_src + [('G', g4)], writes=[pk(b2)])
                            S.op(V, lambda e, h4=h4, b2=b2: e.tensor_tensor(out=Gm[:, h4, :], in0=PB[b2][:].rearrange("p (h t) -> p h t", h=4),
                                                                            in1=Gm[:, h4, :], op=ALU.add),
                                 writes=[pk(b2), ('G', g4)])
                for _ in nxt:
                    pass
                GK = [('G', g4) for g4 in range(4)]
                BmK = [('BmT', g4, i2) for g4 in range(4) for i2 in range(2)]
                PK_ = [('PT', g4, i2) for g4 in range(4) for i2 in range(2)]
                QK = [('QT', g4, i2) for g4 in range(4) for i2 in range(2)]
                for half in range(2):
                    bank = half

                    def mmz(e, half=half, bank=bank):
                        for hpl in range(4):
                            hp = half * 4 + hpl
                            e.matmul(PB[bank][:, hpl * 128:(hpl + 1) * 128], KR[:, hp, 0:128], rstb[:, hp, :], start=True, stop=False)
                            for par in range(2):
                                h = 2 * hp + par
                                inst = e.matmul(PB[bank][:, hpl * 128 + par * 64:hpl * 128 + (par + 1) * 64], BmT[:, h, :], Vb[:, h * 64:(h + 1) * 64],
                                                start=False, stop=(par == 1))
                        return inst
                    S.op('pe', mmz, reads=[('KR', 0), 'rstb'] + BmK + VbK, writes=[pk(bank)])
                    evac_copy(Zb[:, half * 512:(half + 1) * 512], PB[bank][:], [], [pk(bank), ('Zb', half)])
                for half in range(2):
                    bank = 2 + half

                    def mmu(e, half=half, bank=bank):
                        for hh in range(8):
                            h = half * 8 + hh
                            inst = e.matmul(PB[bank][:, hh * 64:(hh + 1) * 64], Gm[:, h, :], Zb[:, h * 64:(h + 1) * 64], start=True, stop=True)
                        return inst
                    S.op('pe', mmu, reads=GK + [('Zb', half)], writes=[pk(bank)])
                    S.op('act', lambda e, half=half, bank=bank: e.activation(out=Ub[:, half * 512:(half + 1) * 512], in_=PB[bank][:], func=AF.Copy, scale=-1.0),
                         writes=[pk(bank), ('Ub', half)])
                UbK = [('Ub', 0), ('Ub', 1)]
                for half in range(2):
                    bank = 4 + half

                    def mmy(e, half=half, bank=bank):
                        for hpl in range(4):
                            hp = half * 4 + hpl
                            e.matmul(PB[bank][:, hpl * 128:(hpl + 1) * 128], KR[:, hp, 128:256], rstb[:, hp, :], start=True, stop=False)
                            for par in range(2):
                                h = 2 * hp + par
                                cs = slice(hpl * 128 + par * 64, hpl * 128 + (par + 1) * 64)
                                e.matmul(PB[bank][:, cs], PTm[:, h, :], Ub[:, h * 64:(h + 1) * 64], start=False, stop=False)
                                inst = e.matmul(PB[bank][:, cs], QTm[:, h, :], Vb[:, h * 64:(h + 1) * 64], start=False, stop=(par == 1))
                        return inst
                    S.op('pe', mmy, reads=[('KR', 1), 'rstb'] + PK_ + QK + UbK + VbK, writes=[pk(bank)])
                    evac_copy(yr[:, half * 512:(half + 1) * 512], PB[bank][:], [], [pk(bank), ('yr', half)])

                def mms(e):
                    for hp in range(8):
                        o_ap = PB[6 + hp // 4][:, (hp % 4) * 128:(hp % 4 + 1) * 128]
                        e.matmul(o_ap, BGtm[:, hp * 128:(hp + 1) * 128], Ub[:, hp * 128:(hp + 1) * 128], start=True, stop=False)
                        inst = e.matmul(o_ap, KGtm[:, hp * 128:(hp + 1) * 128], Vb[:, hp * 128:(hp + 1) * 128], start=False, stop=True)
                    return inst
                S.op('pe', mms, reads=['BGtm', 'KGtm'] + UbK + VbK, writes=[pk(6), pk(7)])
                S.op(V, lambda e: e.tensor_tensor(out=rst[:], in0=rst[:], in1=bc(gC[:].unsqueeze(2), [128, 8, 128]), op=ALU.mult),
                     reads=['gC', 'rstb'], writes=['rst'])
                for half in range(2):
                    S.op(V, lambda e, half=half: e.tensor_tensor(out=rst[:, half * 4:(half + 1) * 4, :], in0=PB[6 + half][:].rearrange("p (a i) -> p a i", a=4),
                                                                 in1=rst[:, half * 4:(half + 1) * 4, :], op=ALU.add),
                         writes=[pk(6 + half), 'rst'])
                S.op(V, lambda e: e.tensor_tensor(out=rst[:], in0=rst[:], in1=bc(blkf[:].unsqueeze(1), [128, 8, 128]), op=ALU.mult),
                     reads=['blkf'], writes=['rst'])
                S.op('act', lambda e: e.activation(out=rstb[:], in_=rst[:], func=AF.Copy), reads=['rst'], writes=['rstb'])
                yrK = [('yr', 0), ('yr', 1)]
                yr3 = yr[:].rearrange("p (h d) -> p h d", h=16)
                S.op(V, lambda e: e.tensor_reduce(out=s16a[:], in_=yr3, axis=AX.X, op=ALU.add), reads=yrK, writes=['s16a'])
                S.op('act', lambda e: e.activation(out=t2[:], in_=yr[:], func=AF.Square), reads=yrK, writes=[('t2', 0), ('t2', 1)])
                S.op(V, lambda e: e.tensor_reduce(out=s16b[:], in_=t2[:].rearrange("p (h d) -> p h d", h=16), axis=AX.X, op=ALU.add),
                     reads=[('t2', 0), ('t2', 1)], writes=['s16b'])
                S.op(V, lambda e: e.tensor_scalar(out=s16a[:], in0=s16a[:], scalar1=1.0 / 64.0, scalar2=None, op0=ALU.mult), writes=['s16a'])
                S.op(V, lambda e: e.tensor_tensor(out=s16c[:], in0=s16a[:], in1=s16a[:], op=ALU.mult), reads=['s16a'], writes=['s16c'])
                S.op(V, lambda e: e.scalar_tensor_tensor(out=s16b[:], in0=s16b[:], scalar=1.0 / 64.0, in1=s16c[:], op0=ALU.mult, op1=ALU.subtract),
                     reads=['s16c'], writes=['s16b'])
                S.op(V, lambda e: e.tensor_scalar(out=s16b[:], in0=s16b[:], scalar1=64e-5, scalar2=None, op0=ALU.add), writes=['s16b'])
                S.op('act', lambda e: e.activation(out=s16b[:], in_=s16b[:], func=AF.Sqrt), writes=['s16b'])
                S.op(V, lambda e: e.reciprocal(out=s16d[:], in_=s16b[:]), reads=['s16b'], writes=['s16d'])
                S.op(V, lambda e: e.tensor_tensor(out=yr3, in0=yr3, in1=bc(s16a[:].unsqueeze(2), [128, 16, 64]), op=ALU.subtract),
                     reads=['s16a', ('t2', 0), ('t2', 1)], writes=yrK)
                S.op(V, lambda e: e.tensor_tensor(out=yr3, in0=yr3, in1=bc(s16d[:].unsqueeze(2), [128, 16, 64]), op=ALU.mult),
                     reads=['s16d'], writes=yrK)
                S.op(V, lambda e: e.tensor_tensor(out=yr[:], in0=yr[:], in1=tmb[:, 0, :], op=ALU.mult), reads=['tmb'], writes=yrK)
                S.op(V, lambda e: e.tensor_tensor(out=yr[:], in0=yr[:], in1=tmb[:, 1, :], op=ALU.add), reads=['tmb'], writes=yrK)
                S.op('pool', lambda e: e.tensor_tensor(out=t2[:].rearrange("p (h d) -> p h d", h=16), in0=Vtm[:].rearrange("p (h d) -> p h d", h=16),
                                                        in1=bc(bs16[:].unsqueeze(2), [128, 16, 64]), op=ALU.mult),
                     reads=[('Vtm', 0), ('Vtm', 1), 'bs16', 's16b'], writes=[('t2', 0), ('t2', 1)])
                S.op(V, lambda e: e.tensor_tensor(out=yr[:], in0=yr[:], in1=t2[:], op=ALU.add), reads=[('t2', 0), ('t2', 1)], writes=yrK)
                S.op(V, lambda e: e.tensor_tensor(out=ytm[:], in0=yr[:], in1=gtm[:], op=ALU.mult),
                     reads=yrK + [('gtm', 0), ('gtm', 1)], writes=['ytm'])
            def try_(e):
                pbf = PB[YB][:].bitcast(BF16)
                for i in range(8):
                    inst = e.transpose(pbf[:, i * 128:(i + 1) * 128], ytm[:, i * 128:(i + 1) * 128], idb[:])
                return inst
            S.op('pe', try_, reads=['ytm', 'idb'], writes=[pk(YB)])
            evac_copy(yT[:, :, q * 128:(q + 1) * 128], PB[YB][:].bitcast(BF16).rearrange("p (a t) -> p a t", a=8), [], [pk(YB), ('yT', q)])
            if q == 3:
                agin_bf = agin[tt].ap().bitcast(BF16)
                S.op('act', lambda e, agin_bf=agin_bf: e.dma_start(
                    out=agin_bf[which * 1024:(which + 1) * 1024, :].rearrange("(a p) t -> p a t", p=128), in_=yT[:]),
                    reads=[('yT', q_) for q_ in range(4)], writes=[('agin', tt, which)], stream='agst')
                if not SSD:
                    S.op('pool', lambda e, tt=tt: e.collective_compute(
                        "AllGather", ALU.bypass, replica_groups=[[0, 1], [2, 3], [4, 5], [6, 7]],
                        ins=[agin[tt].ap().opt()], outs=[agout[tt].ap().opt()]),
                        reads=[('agin', tt, 0), ('agin', tt, 1)], writes=[('agout', tt)], stream='cc', inc=1)

        if SSD:
            for _ in a1_tile(0):
                pass
        for c in range(NCH):
            if SSD:
                if c % 4 == 0:
                    a1g = a1_tile(c // 4 + 1) if c // 4 + 1 < NT else iter(())
                k = 0
                for _ in chunk_gen(c):
                    k += 1
                    if k % 4 == 0:
                        next(a1g, None)
                if c % 4 == 3:
                    for _ in a1g:
                        pass
            else:
                for _ in chunk_gen(c):
                    pass
        S.flush(final_streams=['ld', 'st', 'cc', 'ld2'])


def phase_b(nc, S, PB, pk, sb, G):
    agout, xTr, pT_d, lnp_d, outT, sel_d = G['agout'], G['xTr'], G['pT'], G['lnp'], G['outT'], G['sel']
    w_out_bf, w_up_bf, w_down_bf, w_gate_bf, w_ple_bf = G['w_out_bf'], G['w_up_bf'], G['w_down_bf'], G['w_gate_bf'], G['w_ple_bf']
    w_down_v = w_down_bf.rearrange("r (s c) -> (r s) c", s=2)
    w_ple_v = w_ple_bf.rearrange("r (s c) -> (r s) c", s=16)
    from contextlib import ExitStack
    V = 'dve'
    NS = 5
    with ExitStack() as es:
        def T(name, shape, dt=F32):
            return es.enter_context(sb("b_" + name, shape, dt))
        hb = T("hb", [128, 32, TT], BF16)
        acc = T("acc", [128, 32, TT])
        hid = T("hid", [128, 16, TT], BF16)
        ring = T("ring", [128, NS, 4096], BF16)
        lnp = T("lnp", [128, 6, 32]); sel = T("sel", [128, 2])
        wp = T("wp", [128, 32, 256], BF16); lnpa = T("lnpa", [128, 6, 32])
        ptb = T("ptb", [128, 2, TT], BF16); ptf = T("ptf", [128, 2, TT])
        sq = T("sq", [128, TT]); sq2 = T("sq2", [128, TT]); red = T("red", [128, 2, TT]); redp = T("redp", [128, TT])
        mean = T("mean", [128, TT]); rstd = T("rstd", [128, TT]); ones_f = T("ones", [128, 128])
        pe_sb = T("pe_sb", [128, TT]); sg_sb = T("sg_sb", [128, TT])
        G['issue_casts'](1000)
        S.op('sp', lambda e: e.dma_start(out=lnp[:], in_=lnp_d), writes=['lnp'], stream='ldc')
        S.op('sp', lambda e: e.dma_start(out=sel[:], in_=sel_d), writes=['sel'], stream='ldc')
        S.op(V, lambda e: e.tensor_scalar(out=lnpa[:], in0=lnp[:], scalar1=ALPHA, scalar2=None, op0=ALU.mult), reads=['lnp'], writes=['lnpa'])
        S.op(V, lambda e: e.memset(ones_f[:], 1.0), writes=['ones_b'])
        S.op('sp', lambda e: e.dma_start(out=wp[:], in_=w_ple_v.rearrange("(a p) c -> p a c", p=128)), reads=[('wplebf', 0)], writes=['wp'], stream='ldc')
        rs = [0]
        ps = [0]
        ACC = [('acc', i) for i in range(32)]
        HB = [('hb', i) for i in range(32)]

        def wload(src_ap, key_reads, n=4096):
            slot = rs[0] % NS
            rs[0] += 1
            S.op('sp', lambda e: e.dma_start(out=ring[:, slot, 0:n], in_=src_ap), reads=key_reads, writes=[('bring', slot)], stream='bring%d' % slot)
            return slot

        def nbank():
            b = ps[0] % 8
            ps[0] += 1
            return b

        def stat_accum(i):
            sqb, sqk = (sq, 'sq') if i % 2 == 0 else (sq2, 'sq2')
            S.op('act', lambda e: e.activation(out=sqb[:], in_=acc[:, i, :], func=AF.Square), reads=[('acc', i)], writes=[sqk])
            if i == 0:
                S.op(V, lambda e: e.tensor_copy(out=red[:, 0, :], in_=acc[:, i, :]), reads=[('acc', i)], writes=[('red', 0)])
                S.op(V, lambda e: e.tensor_copy(out=red[:, 1, :], in_=sqb[:]), reads=[sqk], writes=[('red', 1)])
            else:
                S.op(V, lambda e: e.tensor_tensor(out=red[:, 0, :], in0=red[:, 0, :], in1=acc[:, i, :], op=ALU.add), reads=[('acc', i)], writes=[('red', 0)])
                S.op(V, lambda e: e.tensor_tensor(out=red[:, 1, :], in0=red[:, 1, :], in1=sqb[:], op=ALU.add), reads=[sqk], writes=[('red', 1)])

        def layer_norm(li, last):
            b = nbank()
            S.op('pe', lambda e: e.matmul(PB[b][:], ones_f[:], red[:, 0, :], start=True, stop=True), reads=['ones_b', ('red', 0)], writes=[pk(b)])
            S.op(V, lambda e: e.tensor_scalar(out=mean[:], in0=PB[b][:], scalar1=1.0 / D, scalar2=None, op0=ALU.mult), writes=[pk(b), 'mean'])
            b2 = nbank()
            S.op('pe', lambda e: e.matmul(PB[b2][:], ones_f[:], red[:, 1, :], start=True, stop=True), reads=['ones_b', ('red', 1)], writes=[pk(b2)])
            S.op(V, lambda e: e.tensor_tensor(out=redp[:], in0=mean[:], in1=mean[:], op=ALU.mult), reads=['mean'], writes=['redp'])
            S.op(V, lambda e: e.scalar_tensor_tensor(out=rstd[:], in0=PB[b2][:], scalar=1.0 / D, in1=redp[:], op0=ALU.mult, op1=ALU.subtract),
                 reads=['redp'], writes=[pk(b2), 'rstd'])
            S.op(V, lambda e: e.tensor_scalar(out=rstd[:], in0=rstd[:], scalar1=1e-5, scalar2=None, op0=ALU.add), writes=['rstd'])
            S.op('act', lambda e: e.activation(out=rstd[:], in_=rstd[:], func=AF.Sqrt), writes=['rstd'])
            S.op(V, lambda e: e.reciprocal(out=rstd[:], in_=rstd[:]), writes=['rstd'])
            for i in range(32):
                S.op(V, lambda e, i=i: e.tensor_tensor(out=acc[:, i, :], in0=acc[:, i, :], in1=mean[:], op=ALU.subtract),
                     reads=['mean'], writes=[('acc', i)])
                S.op(V, lambda e, i=i: e.tensor_tensor(out=acc[:, i, :], in0=acc[:, i, :], in1=rstd[:], op=ALU.mult),
                     reads=['rstd'], writes=[('acc', i)])
                if last:
                    S.op('act', lambda e, i=i: e.activation(out=acc[:, i, :], in_=acc[:, i, :], func=AF.Identity,
                                                            scale=lnp[:, 2 * li, i:i + 1], bias=lnp[:, 2 * li + 1, i:i + 1]),
                         reads=['lnp'], writes=[('acc', i)])
                else:
                    if i % 2 == 0:
                        S.op('act', lambda e, i=i: e.activation(out=hb[:, i, :], in_=acc[:, i, :], func=AF.Identity,
                                                                scale=lnp[:, 2 * li, i:i + 1], bias=lnp[:, 2 * li + 1, i:i + 1]),
                             reads=['lnp', ('acc', i)], writes=[('hb', i)])
                    else:
                        S.op('pool', lambda e, i=i: e.tensor_scalar(out=hb[:, i, :], in0=acc[:, i, :], scalar1=lnp[:, 2 * li, i:i + 1],
                                                                    scalar2=lnp[:, 2 * li + 1, i:i + 1], op0=ALU.mult, op1=ALU.add),
                             reads=['lnp', ('acc', i)], writes=[('hb', i)])
                    S.op('act', lambda e, i=i: e.activation(out=acc[:, i, :], in_=acc[:, i, :], func=AF.Identity,
                                                            scale=lnpa[:, 2 * li, i:i + 1], bias=lnpa[:, 2 * li + 1, i:i + 1]),
                         reads=['lnpa', ('hb', i)], writes=[('acc', i)])

        if DEBUG_STAGE == 4:
            dbgB = nc.dram_tensor("dbgB", [5, D, TT], F32, kind="ExternalOutput").ap()

        def ckpt(k, tl):
            if DEBUG_STAGE == 4 and tl == 0:
                S.op('sp', lambda e: e.dma_start(out=dbgB[k, :, :].rearrange("(a p) t -> p a t", p=128), in_=acc[:]),
                     reads=ACC, writes=[('dbgB', k)], stream='dbg')

        for tl in range(4):
            if DEBUG_STAGE == 4 and tl == 1:
                break
            ya = agout[tl].ap().bitcast(BF16)
            yb_ = agout[tl + 4].ap().bitcast(BF16)
            S.op('sp', lambda e, ya=ya: e.dma_start(out=hb[:], in_=ya.rearrange("(a p) t -> p a t", p=128)),
                 reads=[('agout', tl)], writes=HB, stream='hbld')
            S.op('sp', lambda e, tl=tl: e.dma_start(out=acc[:], in_=xTr[:, tl * TT:(tl + 1) * TT].rearrange("(a p) t -> p a t", p=128)),
                 writes=ACC, stream='accld')
            S.op('sp', lambda e, tl=tl: e.dma_start(out=ptf[:], in_=pT_d[:, :, tl * TT:(tl + 1) * TT]), writes=['ptf'], stream='ptld')
            S.op(V, lambda e: e.tensor_copy(out=ptb[:], in_=ptf[:]), reads=['ptf'], writes=['ptb'])
            for i in range(4):
                slot = wload(yb_[i * 1024:(i + 1) * 1024, :].rearrange("(a p) t -> p a t", p=128), [('agout', tl + 4)])
                hk = [('hb', 8 * i + k) for k in range(8)]
                S.op(V, lambda e, i=i: e.tensor_scalar(out=hb[:, 8 * i:8 * i + 8, :], in0=hb[:, 8 * i:8 * i + 8, :], scalar1=sel[:, 0:1], scalar2=None, op0=ALU.mult),
                     reads=['sel'], writes=hk)
                S.op(V, lambda e, i=i, slot=slot: e.scalar_tensor_tensor(
                    out=hb[:, 8 * i:8 * i + 8, :], in0=ring[:, slot, :].rearrange("p (a t) -> p a t", a=8), scalar=sel[:, 1:2],
                    in1=hb[:, 8 * i:8 * i + 8, :], op0=ALU.mult, op1=ALU.add), reads=['sel', ('bring', slot)], writes=hk)
            for i in range(32):
                slot = wload(w_out_bf[i * 128:(i + 1) * 128, :], [('woutbf', i // 8)])
                b = nbank()

                def mm(e, slot=slot, b=b):
                    for kc in range(32):
                        inst = e.matmul(PB[b][:], ring[:, slot, kc * 128:(kc + 1) * 128], hb[:, kc, :], start=(kc == 0), stop=(kc == 31))
                    return inst
                S.op('pe', mm, reads=[('bring', slot)] + HB, writes=[pk(b)])
                S.op(V, lambda e, i=i, b=b: e.scalar_tensor_tensor(out=acc[:, i, :], in0=acc[:, i, :], scalar=ALPHA, in1=PB[b][:], op0=ALU.mult, op1=ALU.add),
                     writes=[pk(b), ('acc', i)])
                stat_accum(i)
            ckpt(0, tl)
            layer_norm(0, False)
            ckpt(1, tl)
            for fb in range(8):
                for j in range(16):
                    blk = fb * 16 + j
                    slot = wload(w_up_bf[blk * 128:(blk + 1) * 128, :], [('wupbf', blk // 8)])
                    b = nbank()

                    def mmu(e, slot=slot, b=b):
                        for kc in range(32):
                            inst = e.matmul(PB[b][:], ring[:, slot, kc * 128:(kc + 1) * 128], hb[:, kc, :], start=(kc == 0), stop=(kc == 31))
                        return inst
                    S.op('pe', mmu, reads=[('bring', slot)] + HB, writes=[pk(b)])
                    S.op('act', lambda e, b=b: e.activation(out=sg_sb[:], in_=PB[b][:], func=AF.Square), writes=[pk(b), 'sg_sb'])
                    S.op(V, lambda e, b=b, j=j: e.scalar_tensor_tensor(out=hid[:, j, :], in0=PB[b][:], scalar=0.0, in1=sg_sb[:], op0=ALU.is_gt, op1=ALU.mult),
                         reads=['sg_sb'], writes=[pk(b), ('hid', j)])
                for i in range(32):
                    r0 = (fb * 32 + i) * 128
                    slot = wload(w_down_v[r0:r0 + 128, :], [('wdownbf', r0 // 2 // 1024)], n=2048)
                    b = nbank()

                    def mmd(e, slot=slot, b=b):
                        for kc in range(16):
                            inst = e.matmul(PB[b][:], ring[:, slot, kc * 128:(kc + 1) * 128], hid[:, kc, :], start=(kc == 0), stop=(kc == 15))
                        return inst
                    S.op('pe', mmd, reads=[('bring', slot)] + [('hid', j) for j in range(16)], writes=[pk(b)])
                    S.op(V, lambda e, i=i, b=b: e.tensor_tensor(out=acc[:, i, :], in0=PB[b][:], in1=acc[:, i, :], op=ALU.add), writes=[pk(b), ('acc', i)])
                    if fb == 7:
                        stat_accum(i)
            ckpt(2, tl)
            layer_norm(1, False)
            ckpt(3, tl)
            for i in range(32):
                slot = wload(w_gate_bf[i * 128:(i + 1) * 128, :], [('wgatebf', i // 8)])
                b = nbank()

                def mmg(e, slot=slot, b=b):
                    for kc in range(32):
                        inst = e.matmul(PB[b][:], ring[:, slot, kc * 128:(kc + 1) * 128], hb[:, kc, :], start=(kc == 0), stop=(kc == 31))
                    return inst
                S.op('pe', mmg, reads=[('bring', slot)] + HB, writes=[pk(b)])
                S.op('act', lambda e, b=b: e.activation(out=sg_sb[:], in_=PB[b][:], func=AF.Sigmoid), writes=[pk(b), 'sg_sb'])
                b2 = nbank()

                def mmp(e, b2=b2, i=i):
                    for kc in range(2):
                        inst = e.matmul(PB[b2][:], wp[:, i, kc * 128:(kc + 1) * 128], ptb[:, kc, :], start=(kc == 0), stop=(kc == 1))
                    return inst
                S.op('pe', mmp, reads=['wp', 'ptb'], writes=[pk(b2)])
                S.op(V, lambda e, b2=b2: e.tensor_tensor(out=pe_sb[:], in0=PB[b2][:], in1=sg_sb[:], op=ALU.mult), reads=['sg_sb'], writes=[pk(b2), 'pe_sb'])
                S.op(V, lambda e, i=i: e.tensor_tensor(out=acc[:, i, :], in0=acc[:, i, :], in1=pe_sb[:], op=ALU.add), reads=['pe_sb'], writes=[('acc', i)])
                stat_accum(i)
            ckpt(4, tl)
            layer_norm(2, True)
            S.op('act', lambda e, tl=tl: e.dma_start(out=outT[:, tl * TT:(tl + 1) * TT].rearrange("(a p) t -> p a t", p=128), in_=acc[:]),
                 reads=ACC, writes=[('out', tl)], stream='outst')
        S.flush(final_streams=['ld', 'st', 'cc', 'cast'])


_NC = None


def _prep(inputs):
    f = np.float32
    g = lambda k: np.asarray(inputs[k], dtype=f)[0]
    x = np.asarray(inputs["x"], dtype=f)
    p = np.asarray(inputs["p"], dtype=f)[0]
    w_in = g("w_in")
    D_SSM = 2048
    OFF_R = 6176

    def blkfmt(w):
        K, C = w.shape
        nb, kc = C // 128, K // 128
        return np.ascontiguousarray(w.reshape(kc, 128, nb, 128).transpose(2, 1, 0, 3)).reshape(nb * 128 * kc * 128 // 4096, 4096)

    w_up = blkfmt(g("w_up"))
    wd = g("w_down")
    w_down = np.ascontiguousarray(wd.reshape(8, 16, 128, 32, 128).transpose(0, 3, 2, 1, 4)).reshape(-1, 4096)
    w_gate = blkfmt(g("w_ple_gate"))
    w_ple = blkfmt(g("w_ple"))
    wo = g("w_out")
    lnp = np.stack([g(k).reshape(32, 128).T for k in ["ln1_g", "ln1_b", "ln2_g", "ln2_b", "ln3_g", "ln3_b"]], axis=1)
    lnp = np.ascontiguousarray(lnp)
    per_half = []
    for hf in range(2):
        cs = slice(hf * 1024, (hf + 1) * 1024)
        cols = []
        cols += list(range(D_SSM + hf * 1024, D_SSM + (hf + 1) * 1024))
        cols += list(range(2 * D_SSM + hf * 512, 2 * D_SSM + (hf + 1) * 512))
        cols += list(range(2 * D_SSM + 1024 + hf * 512, 2 * D_SSM + 1024 + (hf + 1) * 512))
        conv_cols = [c - D_SSM for c in cols]
        for part in range(3):
            cols += list(range(OFF_R + part * 2048 + hf * 1024, OFF_R + part * 2048 + (hf + 1) * 1024))
        lo = OFF_R + 3 * 2048
        wcols = np.zeros((4096, NBLK * 128), f)
        wcols[:, :40 * 128] = w_in[:, cols]
        wcols[:, 40 * 128:40 * 128 + 96] = w_in[:, lo:lo + 96]
        wcols[:, 41 * 128:41 * 128 + 96] = w_in[:, lo + 96:lo + 192]
        wcols[:, 42 * 128:44 * 128] = w_in[:, lo + 192:lo + 448]
        wcols[:, 44 * 128:52 * 128] = w_in[:, hf * 1024:(hf + 1) * 1024]
        dt0 = D_SSM + 4096 + hf * 16
        wcols[:, 52 * 128:52 * 128 + 16] = w_in[:, dt0:dt0 + 16]
        w_in_c = blkfmt(wcols)
        cw = np.ascontiguousarray(g("conv_w")[:, conv_cols].T.reshape(16, 128, 4).transpose(1, 0, 2))
        cb = np.ascontiguousarray(g("conv_b")[conv_cols].reshape(16, 128).T)
        mu_full = g("rwkv_mu")
        mu_cols = np.zeros((28, 128), f)
        for part in range(3):
            mu_cols[part * 8:(part + 1) * 8] = mu_full[part * 2048 + hf * 1024:part * 2048 + (hf + 1) * 1024].reshape(8, 128)
        mu_cols[24, :96] = mu_full[6144:6240]
        mu_cols[25, :96] = mu_full[6240:6336]
        mu_cols[26:28] = mu_full[6336:6592].reshape(2, 128)
        mu_c = np.ascontiguousarray(mu_cols.T)
        chp = np.stack([g(k).reshape(-1)[cs].reshape(8, 128).T for k in ["w0", "a0", "k_k", "k_a", "r_k"]], axis=1)
        chp = np.ascontiguousarray(chp)
        Dfull = np.repeat(g("D_skip")[hf * 16:(hf + 1) * 16], 64)
        tmb = np.stack([g("ssm_norm_g")[cs], Dfull, g("gn_g")[cs], g("gn_b")[cs]], axis=0)
        tmb = np.ascontiguousarray(np.broadcast_to(tmb[None], (128, 4, 1024)))
        dtp = np.stack([g("dt_bias")[hf * 16:(hf + 1) * 16], g("A_log")[hf * 16:(hf + 1) * 16]], axis=0)
        dtp = np.ascontiguousarray(np.broadcast_to(dtp[None], (128, 2, 16)))
        wdb = np.ascontiguousarray(g("w_decay_b")[:, cs])
        wab = np.ascontiguousarray(g("w_aaa_b")[:, cs])
        wgb = np.ascontiguousarray(g("w_gate_b")[:, cs].reshape(2, 128, 1024).transpose(1, 0, 2))
        sel = np.zeros((128, 2), f)
        sel[:, hf] = 1.0
        per_half.append(dict(w_in=w_in_c, cw=cw, cb=cb, mu=mu_c, chp=chp, tmb=tmb, dtp=dtp, wdb=wdb, wab=wab, wgb=wgb, sel=sel))
    perm = np.concatenate([np.arange(0, 1024), np.arange(2048, 3072), np.arange(1024, 2048), np.arange(3072, 4096)])
    w_out = blkfmt(wo[perm])
    shared = dict(w_out=w_out, w_up=w_up, w_down=w_down, w_gate=w_gate, w_ple=w_ple, lnp=lnp)
    in_maps = []
    for c in range(8):
        b, hf = c // 2, c % 2
        xT = np.ascontiguousarray(x[b].T)
        m = dict(shared)
        m.update(per_half[hf])
        m["xT"] = xT
        m["xTr"] = np.ascontiguousarray(xT[:, hf * 2048:(hf + 1) * 2048])
        m["pT"] = np.ascontiguousarray(p[b].T[:, hf * 2048:(hf + 1) * 2048].reshape(2, 128, 2048).transpose(1, 0, 2))
        in_maps.append(m)
    return in_maps


def kernel(**inputs):
    global _NC
    in_maps = _prep(inputs)
    if _NC is None:
        _NC = build_nc()
    res = run_bass_kernel_spmd(_NC, in_maps, core_ids=list(range(8)))
    out = np.empty((4, SEQ, D), np.float32)
    for c in range(8):
        b, hf = c // 2, c % 2
        out[b, hf * 2048:(hf + 1) * 2048, :] = np.asarray(res.results[c]["outT"]).T
    return out
```

```python
import math
import numpy as np
import concourse.bass as bass
import concourse.mybir as mybir
from concourse.bass_utils import run_bass_kernel_spmd

F32, BF16 = mybir.dt.float32, mybir.dt.bfloat16
AF = mybir.ActivationFunctionType
ALU = mybir.AluOpType
AX = mybir.AxisListType

D = 4096
SEQ = 4096
NT = 8
TT = 512
NCH = 32
ALPHA = 2.0 ** 0.25
CDEC = math.exp(-0.5)
NBLK = 53
NFM = 44
B_XS, B_B, B_C, B_R, B_K, B_V, B_XW, B_XA, B_XG = 0, 8, 12, 16, 24, 32, 40, 41, 42
BLK_Z, BLK_DT = 44, 52
FMW = [128] * 40 + [96, 96, 128, 128]
UPAD = 4
NTM = 1040


class Sched:
    ENG = ['pe', 'act', 'dve', 'pool', 'sp']

    def __init__(self, nc):
        self.nc = nc
        self.ops = []
        self.buf = {}
        self.emitted = 0
        self.cnt = {}
        self.waited = {e: {} for e in self.ENG}
        self.sems = {}
        self.lastq = {}

    def op(self, eng, fn, reads=(), writes=(), stream=None, inc=16):
        oid = len(self.ops)
        deps = set()
        for b in reads:
            st = self.buf.get(b)
            if st and st[0] is not None:
                deps.add(st[0])
            elif isinstance(b, tuple) and isinstance(b[0], str) and b[0].endswith('bf'):
                raise AssertionError("read of %r emitted before its cast" % (b,))
        for b in writes:
            st = self.buf.get(b)
            if st:
                if st[0] is not None:
                    deps.add(st[0])
                deps.update(st[1])
        for b in reads:
            self.buf.setdefault(b, [None, []])[1].append(oid)
        for b in writes:
            self.buf[b] = [oid, []]
        if stream:
            prev = self.lastq.get(stream)
            if prev is not None:
                deps.add(prev)
            self.lastq[stream] = oid
        deps.discard(oid)
        self.ops.append(dict(eng=eng, fn=fn, deps=deps, stream=stream, inc=inc, sig=None))
        return oid

    def sem(self, k):
        if k not in self.sems:
            self.sems[k] = self.nc.alloc_semaphore(name="sem_%s_%s" % k)
        return self.sems[k]

    def flush(self, final_streams=()):
        ops = self.ops[self.emitted:]
        base = self.emitted
        n = len(self.ops)
        dependents = [False] * n
        for o in ops:
            for d in o['deps']:
                dependents[d] = True
        for i, o in enumerate(ops):
            if o['stream']:
                k = ('s', o['stream'])
                self.cnt[k] = self.cnt.get(k, 0) + o['inc']
                o['sig'] = (k, self.cnt[k])
            elif dependents[base + i] or i == len(ops) - 1:
                k = ('e', o['eng'])
                self.cnt[k] = self.cnt.get(k, 0) + 1
                o['sig'] = (k, self.cnt[k])
        final_streams = [k[1] for k in self.cnt if k[0] == 's' and (not k[1].startswith('cast') or 'cast' in final_streams)]
        streams_final = [(('s', s), self.cnt.get(('s', s), 0)) for s in final_streams]
        with self.nc.Block() as block:
            engs = {'pe': block.tensor, 'act': block.scalar, 'dve': block.vector,
                    'pool': block.gpsimd, 'sp': block.sync}
            for e in self.ENG:
                mine = [o for o in ops if o['eng'] == e]
                if not mine and e != 'sp':
                    continue

                def body(engine, mine=mine, e=e):
                    waited = self.waited[e]
                    for o in mine:
                        for d in sorted(o['deps']):
                            if self.ops[d]['sig'] is None:
                                continue
                            k, v = self.ops[d]['sig']
                            if waited.get(k, 0) < v:
                                engine.wait_ge(self.sem(k), v)
                                waited[k] = v
                        inst = o['fn'](engine)
                        if o['sig'] is not None:
                            inst.then_inc(self.sem(o['sig'][0]), o['inc'] if o['stream'] else 1)
                    if e == 'sp':
                        for k, v in streams_final:
                            if v and waited.get(k, 0) < v:
                                engine.wait_ge(self.sem(k), v)
                                waited[k] = v
                engs[e](body)
        self.emitted = n


DEBUG_STAGE = 0


def build_nc():
    nc = bass.Bass("TRN2", target_bir_lowering=False)

    def din(name, shape):
        return nc.dram_tensor(name, list(shape), F32, kind="ExternalInput").ap()

    xT = din("xT", [D, SEQ])
    w_in = din("w_in", [NBLK * 128, 4096])
    w_out = din("w_out", [32 * 128, 4096])
    w_up = din("w_up", [128 * 128, 4096])
    w_down = din("w_down", [8 * 32 * 128 // 2, 4096])
    w_gate = din("w_gate", [32 * 128, 4096])
    w_ple = din("w_ple", [256, 4096])
    pT = din("pT", [128, 2, 2048])
    cw = din("cw", [128, 16, 4])
    cb = din("cb", [128, 16])
    mu = din("mu", [128, 28])
    chp = din("chp", [128, 5, 8])
    tmb = din("tmb", [128, 4, 1024])
    dtp = din("dtp", [128, 2, 16])
    wdb = din("wdb", [96, 1024])
    wab = din("wab", [96, 1024])
    wgb = din("wgb", [128, 2, 1024])
    lnp = din("lnp", [128, 6, 32])
    xTr = din("xTr", [D, 2048])
    sel = din("sel", [128, 2])
    outT = nc.dram_tensor("outT", [D, 2048], F32, kind="ExternalOutput").ap()

    xT_bf = nc.dram_tensor("xT_bf", [D, SEQ], BF16).ap()
    w_in_bf = nc.dram_tensor("w_in_bf", [NBLK * 128, 4096], BF16).ap()
    w_out_bf = nc.dram_tensor("w_out_bf", [32 * 128, 4096], BF16).ap()
    w_up_bf = nc.dram_tensor("w_up_bf", [128 * 128, 4096], BF16).ap()
    w_down_bf = nc.dram_tensor("w_down_bf", [8 * 32 * 64, 4096], BF16).ap()
    w_gate_bf = nc.dram_tensor("w_gate_bf", [32 * 128, 4096], BF16).ap()
    w_ple_bf = nc.dram_tensor("w_ple_bf", [256, 4096], BF16).ap()
    uF = nc.dram_tensor("uF", [NFM, 128, UPAD + SEQ], F32).ap()
    uT = nc.dram_tensor("uT", [SEQ, NTM], F32).ap()
    agin = [nc.dram_tensor("agin%d" % t, [2048, 256], F32) for t in range(NT)]
    agout = [nc.dram_tensor("agout%d" % t, [4096, 256], F32) for t in range(NT)]

    S = Sched(nc)

    def sb(name, shape, dt):
        return nc.sbuf_tensor(name, list(shape), dt)

    ncast = [0]

    castq = []

    def cast(*args):
        castq.append(args)

    def issue_casts(n):
        for _ in range(min(n, len(castq))):
            cast_now(*castq.pop(0))
            nissued[0] += 1

    nissued = [0]

    def ensure_x(tt):
        last = 1 if tt == 0 else 16 + 2 * (tt - 1) + 1
        while nissued[0] <= last and castq:
            issue_casts(1)

    def cast_now(dst, src, r0, r1, key, c0=None, c1=None):
        slot = ('castslot', ncast[0] % 3)
        cstream = 'cast%d' % (ncast[0] % 3)
        ncast[0] += 1
        if c0 is None:
            S.op('pool', lambda e: e.dma_start(out=dst[r0:r1, :], in_=src[r0:r1, :], max_dma_last_dim=4096),
                 writes=[key, slot], stream=cstream)
        else:
            S.op('pool', lambda e: e.dma_start(out=dst[r0:r1, c0:c1], in_=src[r0:r1, c0:c1]),
                 writes=[key, slot], stream=cstream)

    def cast_rows(dst, src, nrows, name, step=1024):
        for r0 in range(0, nrows, 512):
            cast(dst, src, r0, min(nrows, r0 + 512), (name, r0 // step))

    cast(xT_bf, xT, 0, 2048, ('xTbf', 0), 0, TT)
    cast(xT_bf, xT, 2048, D, ('xTbf', 0), 0, TT)
    cast_rows(w_in_bf, w_in, NBLK * 128, 'winbf')
    for t in range(1, NT):
        cast(xT_bf, xT, 0, 2048, ('xTbf', t), t * TT, (t + 1) * TT)
        cast(xT_bf, xT, 2048, D, ('xTbf', t), t * TT, (t + 1) * TT)
    cast_rows(w_out_bf, w_out, 4096, 'woutbf')
    cast_rows(w_ple_bf, w_ple, 256, 'wplebf')
    cast_rows(w_up_bf, w_up, 16384, 'wupbf')
    cast_rows(w_down_bf, w_down, 16384, 'wdownbf')
    cast_rows(w_gate_bf, w_gate, 4096, 'wgatebf')
    issue_casts(6)

    with (nc.psum_tensor("pb0", [128, 512], F32) as pb0, nc.psum_tensor("pb1", [128, 512], F32) as pb1,
          nc.psum_tensor("pb2", [128, 512], F32) as pb2, nc.psum_tensor("pb3", [128, 512], F32) as pb3,
          nc.psum_tensor("pb4", [128, 512], F32) as pb4, nc.psum_tensor("pb5", [128, 512], F32) as pb5,
          nc.psum_tensor("pb6", [128, 512], F32) as pb6, nc.psum_tensor("pb7", [128, 512], F32) as pb7):
        PB = [pb0, pb1, pb2, pb3, pb4, pb5, pb6, pb7]

        def pk(i):
            return ('psum', i)

        if phase_a2(nc, S, PB, pk, sb, locals()):
            return nc
        phase_b(nc, S, PB, pk, sb, locals())
    return nc


def phase_a2(nc, S, PB, pk, sb, G):
    for which in (0, 1):
        _mixer_pass(nc, S, PB, pk, sb, G, which)
        if DEBUG_STAGE == 2 + which:
            dbgA = nc.dram_tensor("dbgA", [NT, 2048, 256], F32, kind="ExternalOutput").ap()
            dbgO = nc.dram_tensor("dbgO", [NT, 4096, 256], F32, kind="ExternalOutput").ap()
            for t in range(NT):
                S.op('sp', lambda e, t=t: e.dma_start(out=dbgA[t, :, :], in_=G['agin'][t].ap()[:, :]), writes=[('dbgA', t)], stream='ld')
                if which == 1:
                    S.op('sp', lambda e, t=t: e.dma_start(out=dbgO[t, :, :], in_=G['agout'][t].ap()[:, :]), writes=[('dbgO', t)], stream='ld')
            S.flush(final_streams=['ld', 'st', 'cast', 'cc'])
            return True
    return False


def _mixer_pass(nc, S, PB, pk, sb, G, which):
    SSD = which == 0
    YB = 0 if SSD else 7
    uF, uT, agin, agout = G['uF'], G['uT'], G['agin'], G['agout']
    cw_d, cb_d, mu_d, chp_d, tmb_d, dtp_d = G['cw'], G['cb'], G['mu'], G['chp'], G['tmb'], G['dtp']
    wdb_d, wab_d, wgb_d = G['wdb'], G['wab'], G['wgb']
    from contextlib import ExitStack
    with ExitStack() as es:
        grp = ['both']

        def T(name, shape, dt=F32):
            if grp[0] == 'ssd' and not SSD:
                return None
            if grp[0] == 'rwkv' and SSD:
                return None
            return es.enter_context(sb("a2_%d_" % which + name, shape, dt))
        cwt = T("cw", [128, 16, 4]); cbt = T("cb", [128, 16]); mut = T("mu", [128, 28]); omm = T("omm", [128, 28])
        chp = T("chp", [128, 5, 8]); omka = T("omka", [128, 8]); tmb = T("tmb", [128, 2, 1024])
        dtp = T("dtp", [128, 2, 16]); At = T("At", [128, 16])
        wdb = T("wdb", [96, 8, 128], BF16); wab = T("wab", [96, 8, 128], BF16); wgb = T("wgb", [128, 2, 1024], BF16)
        ones_f = T("ones_f", [128, 128]); mUi = T("mUi", [128, 128]); mUs = T("mUs", [128, 128]); mLs = T("mLs", [128, 128])
        idf = T("idf", [128, 128]); idb = T("idb", [128, 128], BF16); blk1 = T("blk1", [128, 128], BF16)
        sel2 = T("sel2", [128, 2], BF16)
        ssm_st = T("ssm_st", [128, 1024]); ssm_stb = T("ssm_stb", [128, 1024], BF16)
        rst = T("rst", [128, 8, 128]); rstb = T("rstb", [128, 8, 128], BF16)
        pm = T("pm", [128, 2]); blkf = T("blkf", [128, 128])
        ytm = T("ytm", [128, 1024], BF16)
        yT = T("yT", [128, 8, 512], BF16)
        t2 = T("t2", [128, 1024])
        grp[0] = 'ssd'
        usb = T("usb", [128, 16, 131]); cacc = T("cacc", [128, 16, 128]); cfm = T("cfm", [128, 16, 128])
        BCb = T("BCb", [128, 8, 128], BF16)
        ztm = T("ztm", [128, NTM]); zs = T("zs", [128, 1024])
        dt_t = T("dt_t", [128, 16]); a_t = T("a_t", [128, 16]); ea = T("ea", [128, 16]); cd = T("cd", [128, 16])
        dtd = T("dtd", [128, 16])
        xs32 = T("xs32", [128, 1024]); xdt = T("xdt", [128, 1024], BF16); xdtd = T("xdtd", [128, 1024], BF16)
        Btm = T("Btm", [128, 512], BF16)
        Rm = T("Rm", [128, 16, 128]); Em = T("Em", [128, 16, 128]); scm = T("scm", [128, 4, 128])
        MT = T("MT", [128, 16, 128], BF16)
        t1 = T("t1", [128, 1024]); vsq = T("vsq", [128, 1024])
        ss4 = T("ss4", [128, 4]); rs4 = T("rs4", [128, 4])
        grp[0] = 'rwkv'
        BhTz = T("BhTz", [128, 16, 128], BF16); KhTz = T("KhTz", [128, 16, 128], BF16); KtTz = T("KtTz", [128, 16, 128], BF16)
        ur = T("ur", [128, 28, 129]); sr = T("sr", [128, 28, 128])
        lob = T("lob", [128, 4, 128], BF16)
        dsig = T("dsig", [128, 8, 128]); asig = T("asig", [128, 8, 128]); Lr = T("Lr", [128, 8, 128])
        w1 = T("w1", [128, 8, 128]); w2 = T("w2", [128, 8, 128]); w3 = T("w3", [128, 8, 128])
        kap = T("kap", [128, 8, 128]); kpr = T("kpr", [128, 8, 128]); bb = T("bb", [128, 8, 128])
        kk2b = T("kk2b", [128, 8, 128], BF16)
        KR = T("KR", [128, 8, 256], BF16); BhT = T("BhT", [128, 8, 128], BF16); KhT = T("KhT", [128, 8, 128], BF16)
        BGT = T("BGT", [128, 8, 128], BF16); KGT = T("KGT", [128, 8, 128], BF16); rkrb = T("rkrb", [128, 8, 128], BF16)
        gC = T("gC", [128, 8])
        gtm = T("gtm", [128, 1024]); Vtm = T("Vtm", [128, 1024]); Vb = T("Vb", [128, 1024], BF16)
        BGtm = T("BGtm", [128, 1024], BF16); KGtm = T("KGtm", [128, 1024], BF16)
        X0 = T("X0", [128, 16, 128], BF16); X1 = T("X1", [128, 16, 128], BF16)
        XT0 = T("XT0", [128, 16, 128], BF16); XT1 = T("XT1", [128, 16, 128], BF16)
        Gm = T("Gm", [128, 16, 128], BF16); PTm = T("PTm", [128, 16, 128], BF16)
        BmT = T("BmT", [128, 16, 128], BF16); QTm = T("QTm", [128, 16, 128], BF16)
        Zb = T("Zb", [128, 1024], BF16); Ub = T("Ub", [128, 1024], BF16)
        yr = T("yr", [128, 1024]); s16a = T("s16a", [128, 16]); s16b = T("s16b", [128, 16])
        s16c = T("s16c", [128, 16]); s16d = T("s16d", [128, 16]); bs16 = T("bs16", [128, 16])

        def ld(dst, src, key):
            S.op('sp', lambda e: e.dma_start(out=dst, in_=src), writes=[key], stream='ldc')

        ld(cwt[:], cw_d, 'cwt'); ld(cbt[:], cb_d, 'cbt'); ld(mut[:], mu_d, 'mut'); ld(chp[:], chp_d, 'chp')
        ld(tmb[:], tmb_d[:, 0:2, :] if SSD else tmb_d[:, 2:4, :], 'tmb'); ld(dtp[:], dtp_d, 'dtp');
        S.op('pool', lambda e: e.dma_start(out=wdb[:].rearrange("p a b -> p (a b)"), in_=wdb_d), writes=['wdb'], stream='ld2')
        S.op('pool', lambda e: e.dma_start(out=wab[:].rearrange("p a b -> p (a b)"), in_=wab_d), writes=['wab'], stream='ld2')
        S.op('pool', lambda e: e.dma_start(out=wgb[:], in_=wgb_d), writes=['wgb'], stream='ld2')
        V = 'dve'
        S.op(V, lambda e: e.tensor_scalar(out=omm[:], in0=mut[:], scalar1=-1.0, scalar2=1.0, op0=ALU.mult, op1=ALU.add),
             reads=['mut'], writes=['omm'])
        S.op(V, lambda e: e.tensor_scalar(out=omka[:], in0=chp[:, 3, :], scalar1=-1.0, scalar2=1.0, op0=ALU.mult, op1=ALU.add),
             reads=['chp'], writes=['omka'])
        S.op('act', lambda e: e.activation(out=At[:], in_=dtp[:, 1, :], func=AF.Exp), reads=['dtp'], writes=['At0'])
        S.op(V, lambda e: e.tensor_scalar(out=At[:], in0=At[:], scalar1=-1.0, scalar2=None, op0=ALU.mult),
             reads=['At0'], writes=['At'])
        S.op(V, lambda e: e.memset(ones_f[:], 1.0), writes=['ones_f'])

        def mk_mask(dst, key, pat, cm, cmp, dt_is_bf=False):
            S.op('pool', lambda e: e.memset(dst[:], 1.0), writes=[key + '0'])
            S.op('pool', lambda e: e.affine_select(out=dst[:], in_=dst[:], pattern=[[pat, 128]], compare_op=cmp,
                                                     fill=0.0, base=0, channel_multiplier=cm),
                 reads=[key + '0'], writes=[key])
        mk_mask(mUi, 'mUi', 1, -1, ALU.is_ge)
        mk_mask(mUs, 'mUs', 1, -1, ALU.is_gt)
        mk_mask(mLs, 'mLs', -1, 1, ALU.is_gt)
        mk_mask(idf, 'idf', 1, -1, ALU.is_equal)
        S.op(V, lambda e: e.tensor_copy(out=idb[:], in_=idf[:]), reads=['idf'], writes=['idb'])
        S.op('pool', lambda e: e.memset(blkf[:], 0.0), writes=['blk1a'])
        S.op('pool', lambda e: e.memset(blkf[0:64, 0:64], 1.0), reads=['blk1a'], writes=['blk1b'])
        S.op('pool', lambda e: e.memset(blkf[64:128, 64:128], 1.0), reads=['blk1b'], writes=['blkf'])
        S.op(V, lambda e: e.tensor_copy(out=blk1[:], in_=blkf[:]), reads=['blkf'], writes=['blk1'])
        S.op('pool', lambda e: e.memset(pm[:], 0.0), writes=['pma'])
        S.op('pool', lambda e: e.memset(pm[0:64, 0:1], 1.0), reads=['pma'], writes=['pmb'])
        S.op('pool', lambda e: e.memset(pm[64:128, 1:2], 1.0), reads=['pmb'], writes=['pm'])
        S.op('pool', lambda e: e.memset(sel2[:], 0.0), writes=['sel2a'])
        S.op('pool', lambda e: e.memset(sel2[0:64, 0:1], 1.0), reads=['sel2a'], writes=['sel2b'])
        S.op('pool', lambda e: e.memset(sel2[64:128, 1:2], 1.0), reads=['sel2b'], writes=['sel2'])
        S.op(V, lambda e: e.memset(ssm_st[:], 0.0), writes=['ssm_st'])
        S.op(V, lambda e: e.memset(ssm_stb[:], 0.0), writes=['ssm_stb'])
        S.op(V, lambda e: e.memset(rst[:], 0.0), writes=['rst'])
        S.op(V, lambda e: e.memset(rstb[:], 0.0), writes=['rstb'])

        def bc(ap, shape):
            return ap.broadcast_to(shape)

        evi = [0]

        def evac_copy(out, in_, reads, writes):
            evi[0] += 1
            if evi[0] % 2:
                S.op('act', lambda e: e.activation(out=out, in_=in_, func=AF.Copy), reads=reads, writes=writes)
            else:
                S.op('dve', lambda e: e.tensor_copy(out=out, in_=in_), reads=reads, writes=writes)

        def p1a_gen(c):
            tt, q = c // 4, c % 4
            t0 = c * 128
            S.op('sp', lambda e, t0=t0: e.dma_start(
                out=ur[:], in_=uF[16:44, :, UPAD + t0 - 1:UPAD + t0 + 128].rearrange("j p t -> p j t")),
                reads=[('uF', j, tt) for j in range(16, 44)] + ([('uF', j, tt - 1) for j in range(16, 44)] if tt > 0 else [('uFpad', j) for j in range(16, 44)]),
                writes=['ur'], stream='ur')
            S.op(V, lambda e: e.tensor_tensor(out=sr[:], in0=ur[:, :, 1:129], in1=bc(omm[:].unsqueeze(2), [128, 28, 128]), op=ALU.mult),
                 reads=['ur', 'omm'], writes=['sr'])
            S.op('pool', lambda e: e.tensor_tensor(out=ur[:, :, 0:128], in0=ur[:, :, 0:128], in1=bc(mut[:].unsqueeze(2), [128, 28, 128]), op=ALU.mult),
                 reads=['mut'], writes=['ur'])
            S.op(V, lambda e: e.tensor_tensor(out=sr[:], in0=sr[:], in1=ur[:, :, 0:128], op=ALU.add), reads=['ur'], writes=['sr'])
            S.op('act', lambda e: e.activation(out=lob[0:96, 0, :], in_=sr[0:96, 24, :], func=AF.Tanh), reads=['sr'], writes=[('lob', 0)])
            S.op('act', lambda e: e.activation(out=lob[0:96, 1, :], in_=sr[0:96, 25, :], func=AF.Copy), reads=['sr'], writes=[('lob', 1)])
            S.op('act', lambda e: e.activation(out=lob[:, 2:4, :], in_=sr[:, 26:28, :], func=AF.Sigmoid), reads=['sr'], writes=[('lob', 2)])
            yield
            for wh, (wt, li, key) in enumerate([(wdb, 0, 'wdb'), (wab, 1, 'wab')]):
                for half in range(2):
                    bank = wh * 2 + half

                    def mm_l(e, wt=wt, li=li, half=half, bank=bank):
                        for i in range(4):
                            inst = e.matmul(PB[bank][:, i * 128:(i + 1) * 128], wt[0:96, half * 4 + i, :], lob[0:96, li, :],
                                            start=True, stop=True)
                        return inst
                    S.op('pe', mm_l, reads=[key, ('lob', li)], writes=[pk(bank)])
                    dst = dsig if wh == 0 else asig
                    S.op(V, lambda e, dst=dst, half=half, bank=bank, wh=wh: e.tensor_tensor(
                        out=dst[:, half * 4:(half + 1) * 4, :], in0=PB[bank][:].rearrange("p (a t) -> p a t", a=4),
                        in1=bc(chp[:, wh, half * 4:(half + 1) * 4].unsqueeze(2), [128, 4, 128]), op=ALU.add),
                        reads=['chp'], writes=[pk(bank), ('sig', wh, half)])
            S.op('act', lambda e: e.activation(out=dsig[:], in_=dsig[:], func=AF.Sigmoid),
                 reads=[('sig', 0, 0), ('sig', 0, 1)], writes=['dsig'])
            S.op('act', lambda e: e.activation(out=asig[:], in_=asig[:], func=AF.Sigmoid),
                 reads=[('sig', 1, 0), ('sig', 1, 1)], writes=['asig'])
            yield
            for fc in range(8):
                S.op(V, lambda e, fc=fc: e.tensor_tensor_scan(out=Lr[:, fc, :], data0=ones_f[:], data1=dsig[:, fc, :],
                                                               initial=0.0, op0=ALU.mult, op1=ALU.add),
                     reads=['dsig', 'ones_f'], writes=[('Lr', fc)])
            LrK = [('Lr', fc) for fc in range(8)]
            yield
            S.op(V, lambda e: e.tensor_tensor(out=kap[:], in0=sr[:, 8:16, :], in1=bc(chp[:, 2, :].unsqueeze(2), [128, 8, 128]), op=ALU.mult),
                 reads=['sr', 'chp'], writes=['kap'])
            S.op('pool', lambda e: e.tensor_tensor(out=kk2b[:], in0=kap[:], in1=kap[:], op=ALU.mult), reads=['kap'], writes=['kk2b'])
            yield
            for half in range(2):
                bank = 6 + half
                S.op('pe', lambda e, half=half, bank=bank: e.matmul(PB[bank][:], blk1[:], kk2b[:, half * 4:(half + 1) * 4, :], start=True, stop=True),
                     reads=['kk2b', 'blk1'], writes=[pk(bank)])
                S.op('act', lambda e, half=half, bank=bank: e.activation(out=w1[:, half * 4:(half + 1) * 4, :],
                                                                         in_=PB[bank][:].rearrange("p (a t) -> p a t", a=4), func=AF.Sqrt),
                     writes=[pk(bank), ('w1', half)])
            S.op(V, lambda e: e.tensor_scalar(out=w1[:], in0=w1[:], scalar1=1e-12, scalar2=None, op0=ALU.max),
                 reads=[('w1', 0), ('w1', 1)], writes=['w1'])
            S.op(V, lambda e: e.reciprocal(out=w1[:], in_=w1[:]), writes=['w1'])
            S.op(V, lambda e: e.tensor_tensor(out=kap[:], in0=kap[:], in1=w1[:], op=ALU.mult), reads=['w1', 'kk2b'], writes=['kap'])
            yield
            S.op('pool', lambda e: e.tensor_tensor(out=kpr[:], in0=asig[:], in1=bc(chp[:, 3, :].unsqueeze(2), [128, 8, 128]), op=ALU.mult),
                 reads=['asig', 'chp'], writes=['kpr'])
            S.op('pool', lambda e: e.tensor_tensor(out=kpr[:], in0=kpr[:], in1=bc(omka[:].unsqueeze(2), [128, 8, 128]), op=ALU.add),
                 reads=['omka'], writes=['kpr'])
            S.op(V, lambda e: e.tensor_tensor(out=kpr[:], in0=kpr[:], in1=sr[:, 8:16, :], op=ALU.mult), reads=['sr'], writes=['kpr'])
            S.op(V, lambda e: e.tensor_tensor(out=bb[:], in0=kap[:], in1=asig[:], op=ALU.mult), reads=['kap', 'asig'], writes=['bb'])
            yield

        nxt = None
        LrK = [('Lr', fc) for fc in range(8)]

        if SSD:
            xT_bf, w_in_bf = G['xT_bf'], G['w_in_bf']
            grp[0] = 'both'
            xts = T("a1_xT", [128, 32, TT], BF16)
            ring = T("a1_ring", [128, 4, 32, 128], BF16)
            stg = T("a1_st", [128, 4, 512], F32)
            zpad = T("a1_z", [128, UPAD], F32)
            S.op('dve', lambda e: e.memset(zpad[:], 0.0), writes=['zpad'])
            for j in range(NFM):
                S.op('act', lambda e, j=j: e.dma_start(out=uF[j, :, 0:UPAD], in_=zpad[:]),
                     reads=['zpad'], writes=[('uFpad', j)], stream='pad')
            ctr = {'psi': 0, 'sgi': 0, 'rsi': 0}

            def load_blk(blk, slot):
                S.op('sp', lambda e: e.dma_start(
                    out=ring[:, slot, :, :].rearrange("p k c -> p (k c)"),
                    in_=w_in_bf[blk * 128:(blk + 1) * 128, :]),
                    reads=[('winbf', (blk * 128) // 1024)], writes=[('ring', slot)], stream='ring%d' % slot)

            def a1_tile(tt):
                G['ensure_x'](tt)
                S.op('sp', lambda e: e.dma_start(
                    out=xts[:], in_=xT_bf[:, tt * TT:(tt + 1) * TT].rearrange("(kc p) t -> p kc t", p=128)),
                    reads=[('xTbf', tt)], writes=['xts'], stream='xts')
                for j in range(NFM):
                    if tt == 0 and j % 4 == 0:
                        G['issue_casts'](1)
                    slot = ctr['rsi'] % 4
                    ctr['rsi'] += 1
                    load_blk(j, slot)
                    m = FMW[j]
                    bank = 6 + ctr['psi'] % 2
                    ctr['psi'] += 1

                    def mm(e, slot=slot, m=m, bank=bank):
                        for kc in range(32):
                            inst = e.matmul(PB[bank][0:m, :], ring[:, slot, kc, 0:m], xts[:, kc, :],
                                            start=(kc == 0), stop=(kc == 31))
                        return inst
                    S.op('pe', mm, reads=[('ring', slot), 'xts'], writes=[pk(bank)])
                    sg = ctr['sgi'] % 4
                    ctr['sgi'] += 1
                    S.op('act', lambda e, sg=sg, m=m, bank=bank: e.activation(
                        out=stg[0:m, sg, :], in_=PB[bank][0:m, :], func=AF.Copy),
                        writes=[pk(bank), ('stg', sg)])
                    S.op('act', lambda e, sg=sg, m=m, j=j: e.dma_start(
                        out=uF[j, 0:m, UPAD + tt * TT:UPAD + (tt + 1) * TT], in_=stg[0:m, sg, :]),
                        reads=[('stg', sg)], writes=[('uF', j, tt)], stream='stg%d' % sg)
                    yield
                for g5 in range(5):
                    if tt == 0:
                        G['issue_casts'](1)
                    if ctr['rsi'] % 2:
                        ctr['rsi'] += 1
                    s0 = ctr['rsi'] % 4
                    nb = 2 if g5 < 4 else 1
                    for i in range(nb):
                        load_blk(BLK_Z + g5 * 2 + i, s0 + i)
                    ctr['rsi'] += 2
                    ncol = 256 if g5 < 4 else 16
                    for q in range(4):
                        bank = 6 + ctr['psi'] % 2
                        ctr['psi'] += 1

                        def mmt(e, s0=s0, nb=nb, ncol=ncol, bank=bank, q=q):
                            for kc in range(32):
                                if nb == 2:
                                    rhs = ring[:, s0:s0 + 2, kc, :]
                                else:
                                    rhs = ring[:, s0, kc, 0:16]
                                inst = e.matmul(PB[bank][:, 0:ncol], xts[:, kc, q * 128:(q + 1) * 128], rhs,
                                                start=(kc == 0), stop=(kc == 31))
                            return inst
                        S.op('pe', mmt, reads=[('ring', s0 + i) for i in range(nb)] + ['xts'], writes=[pk(bank)])
                        sg = ctr['sgi'] % 4
                        ctr['sgi'] += 1
                        S.op('act', lambda e, sg=sg, ncol=ncol, bank=bank: e.activation(
                            out=stg[:, sg, 0:ncol], in_=PB[bank][:, 0:ncol], func=AF.Copy), writes=[pk(bank), ('stg', sg)])
                        c0 = g5 * 256
                        S.op('act', lambda e, sg=sg, ncol=ncol, c0=c0, q=q: e.dma_start(
                            out=uT[tt * TT + q * 128:tt * TT + (q + 1) * 128, c0:c0 + ncol],
                            in_=stg[:, sg, 0:ncol]),
                            reads=[('stg', sg)], writes=[('uT', tt, q, g5)], stream='stg%d' % sg)
                        yield

        def chunk_gen(c):
            G['issue_casts'](1)
            tt, q = c // 4, c % 4
            t0 = c * 128
            yb = tt % 2
            if SSD:
                S.op('sp', lambda e, t0=t0: e.dma_start(
                    out=usb[:], in_=uF[0:16, :, UPAD + t0 - 3:UPAD + t0 + 128].rearrange("j p t -> p j t")),
                    reads=[('uF', j, tt) for j in range(16)] + ([('uF', j, tt - 1) for j in range(16)] if tt > 0 else [('uFpad', j) for j in range(16)]),
                    writes=['usb'], stream='usb')
                yield
                S.op('sp', lambda e, t0=t0: e.dma_start(out=ztm[:], in_=uT[t0:t0 + 128, :]),
                     reads=[('uT', tt, q, g) for g in range(5)], writes=['ztm'], stream='ztm')
                yield
                for k in range(4):
                    if k == 0:
                        S.op('pool', lambda e: e.tensor_tensor(out=cacc[:], in0=usb[:, :, 0:128],
                                                                in1=bc(cwt[:, :, 0:1], [128, 16, 128]), op=ALU.mult),
                             reads=['usb', 'cwt'], writes=['cacc'])
                    else:
                        S.op('pool', lambda e, k=k: e.tensor_tensor(out=cfm[:], in0=usb[:, :, k:k + 128],
                                                                     in1=bc(cwt[:, :, k:k + 1], [128, 16, 128]), op=ALU.mult),
                             reads=['usb', 'cwt'], writes=['cfm'])
                        S.op(V, lambda e: e.tensor_tensor(out=cacc[:], in0=cacc[:], in1=cfm[:], op=ALU.add),
                             reads=['cfm'], writes=['cacc'])
                yield
                S.op(V, lambda e: e.tensor_tensor(out=cacc[:], in0=cacc[:], in1=bc(cbt[:].unsqueeze(2), [128, 16, 128]), op=ALU.add),
                     reads=['cbt'], writes=['cacc'])
                yield
                S.op('act', lambda e: e.activation(out=cfm[:, 0:8, :], in_=cacc[:, 0:8, :], func=AF.Silu),
                     reads=['cacc'], writes=['cfm'])
                yield
                S.op('act', lambda e: e.activation(out=BCb[:], in_=cacc[:, 8:16, :], func=AF.Silu),
                     reads=['cacc'], writes=['BCb'])
                yield
                S.op(V, lambda e: e.tensor_tensor(out=dt_t[:], in0=ztm[:, 1024:1040], in1=dtp[:, 0, :], op=ALU.add),
                     reads=['ztm', 'dtp'], writes=['dt_t'])
                yield
                S.op('act', lambda e: e.activation(out=dt_t[:], in_=dt_t[:], func=AF.Exp), writes=['dt_t'])
                yield
                S.op('act', lambda e: e.activation(out=dt_t[:], in_=dt_t[:], func=AF.Ln, bias=1.0), writes=['dt_t'])
                yield
                S.op(V, lambda e: e.tensor_tensor(out=a_t[:], in0=dt_t[:], in1=At[:], op=ALU.mult),
                     reads=['dt_t', 'At'], writes=['a_t'])
                yield
                S.op('act', lambda e: e.activation(out=zs[:], in_=ztm[:, 0:1024], func=AF.Silu), reads=['ztm'], writes=['zs'])
                yield
                for half in range(2):
                    bank = half

                    def tr(e, half=half, bank=bank):
                        for i in range(4):
                            inst = e.transpose(PB[bank][:, i * 128:(i + 1) * 128], cfm[:, half * 4 + i, :], idf[:])
                        return inst
                    S.op('pe', tr, reads=['cfm', 'idf'], writes=[pk(bank)])
                    evac_copy(xs32[:, half * 512:(half + 1) * 512], PB[bank][:], [], [pk(bank), ('xs32', half)])
                yield

                def trb(e):
                    pbf = PB[2][:].bitcast(BF16)
                    for i in range(4):
                        inst = e.transpose(pbf[:, i * 128:(i + 1) * 128], BCb[:, i, :], idb[:])
                    return inst
                yield
                S.op('pe', trb, reads=['BCb', 'idb'], writes=[pk(2)])
                yield
                evac_copy(Btm[:], PB[2][:].bitcast(BF16)[:, 0:512], [], [pk(2), 'Btm'])
                yield
                S.op('pool', lambda e: e.tensor_tensor(out=Rm[:], in0=bc(a_t[:].unsqueeze(2), [128, 16, 128]),
                                                        in1=bc(mUi[:].unsqueeze(1), [128, 16, 128]), op=ALU.mult),
                     reads=['a_t', 'mUi'], writes=['Rm'])
                yield
                for i in range(4):
                    bank = 2 + i
                    S.op('pe', lambda e, i=i, bank=bank: e.matmul(PB[bank][:], mLs[:], Rm[:, i * 4:(i + 1) * 4, :], start=True, stop=True),
                         reads=['Rm', 'mLs'], writes=[pk(bank)])
                    S.op('act', lambda e, i=i, bank=bank: e.activation(out=Em[:, i * 4:(i + 1) * 4, :], in_=PB[bank][:], func=AF.Exp),
                         writes=[pk(bank), ('Em', i)])
                yield

                def mm_cs(e):
                    e.matmul(PB[1][:, 0:16], mUi[:], a_t[:], start=True, stop=True)
                    return e.matmul(PB[1][:, 16:32], ones_f[:], a_t[:], start=True, stop=True)
                yield
                S.op('pe', mm_cs, reads=['a_t', 'mUi', 'ones_f'], writes=[pk(1)])
                yield
                S.op('act', lambda e: e.activation(out=ea[:], in_=PB[1][:, 0:16], func=AF.Exp), writes=[pk(1), 'ea'])
                yield
                S.op('act', lambda e: e.activation(out=cd[:], in_=PB[1][:, 16:32], func=AF.Exp), writes=[pk(1), 'cd'])
                yield
                def mm_sc(e):
                    for g in range(4):
                        inst = e.matmul(PB[0][:, g * 128:(g + 1) * 128], BCb[:, g, :], BCb[:, 4 + g, :], start=True, stop=True)
                    return inst
                yield
                S.op('pe', mm_sc, reads=['BCb'], writes=[pk(0)])
                yield
                S.op(V, lambda e: e.tensor_tensor(out=scm[:], in0=PB[0][:].rearrange("p (g l) -> p g l", g=4),
                                                  in1=bc(mUi[:].unsqueeze(1), [128, 4, 128]), op=ALU.mult),
                     reads=['mUi'], writes=[pk(0), 'scm'])
                yield
                for g in range(4):
                    S.op('pool' if g % 2 else V, lambda e, g=g: e.tensor_tensor(
                        out=MT[:, g * 4:(g + 1) * 4, :], in0=Em[:, g * 4:(g + 1) * 4, :],
                        in1=bc(scm[:, g:g + 1, :], [128, 4, 128]), op=ALU.mult),
                        reads=[('Em', g), 'scm'], writes=[('MT', g)])
                yield
                S.op(V, lambda e: e.tensor_tensor(out=xdt[:].rearrange("p (h d) -> p h d", h=16),
                                                  in0=xs32[:].rearrange("p (h d) -> p h d", h=16),
                                                  in1=bc(dt_t[:].unsqueeze(2), [128, 16, 64]), op=ALU.mult),
                     reads=[('xs32', 0), ('xs32', 1), 'dt_t'], writes=['xdt'])
                yield
                S.op(V, lambda e: e.tensor_tensor(out=dtd[:].unsqueeze(2), in0=dt_t[:].unsqueeze(2), in1=Em[:, :, 127:128], op=ALU.mult),
                     reads=['dt_t'] + [('Em', i) for i in range(4)], writes=['dtd'])
                yield
                S.op(V, lambda e: e.tensor_tensor(out=xdtd[:].rearrange("p (h d) -> p h d", h=16),
                                                  in0=xs32[:].rearrange("p (h d) -> p h d", h=16),
                                                  in1=bc(dtd[:].unsqueeze(2), [128, 16, 64]), op=ALU.mult),
                     reads=[('xs32', 0), ('xs32', 1), 'dtd'], writes=['xdtd'])
                yield
                for half in range(2):
                    def mm_yd(e, half=half):
                        for hh in range(8):
                            h = half * 8 + hh
                            inst = e.matmul(PB[1 + half][:, hh * 64:(hh + 1) * 64], MT[:, h, :], xdt[:, h * 64:(h + 1) * 64],
                                            start=True, stop=True)
                        return inst
                    S.op('pe', mm_yd, reads=[('MT', g) for g in range(4)] + ['xdt'], writes=[pk(1 + half)])

                    def mm_yo(e, half=half):
                        for gg in range(2):
                            g = half * 2 + gg
                            inst = e.matmul(PB[3 + half][:, gg * 256:(gg + 1) * 256], BCb[:, 4 + g, :],
                                            ssm_stb[:, g * 256:(g + 1) * 256], start=True, stop=True)
                        return inst
                    S.op('pe', mm_yo, reads=['BCb', 'ssm_stb'], writes=[pk(3 + half)])

                    def mm_cs2(e, half=half):
                        for gg in range(2):
                            g = half * 2 + gg
                            inst = e.matmul(PB[(5 + half) % 6][:, gg * 256:(gg + 1) * 256], Btm[:, g * 128:(g + 1) * 128],
                                            xdtd[:, g * 256:(g + 1) * 256], start=True, stop=True)
                        return inst
                    S.op('pe', mm_cs2, reads=['Btm', 'xdtd'], writes=[pk((5 + half) % 6)])
                yield
                for half in range(2):
                    sl = slice(half * 512, (half + 1) * 512)
                    hs = slice(half * 8, (half + 1) * 8)
                    S.op(V, lambda e, half=half, sl=sl, hs=hs: e.tensor_tensor(
                        out=t1[:, sl].rearrange("p (h d) -> p h d", h=8), in0=PB[3 + half][:].rearrange("p (h d) -> p h d", h=8),
                        in1=bc(ea[:, hs].unsqueeze(2), [128, 8, 64]), op=ALU.mult),
                        reads=['ea'], writes=[pk(3 + half), ('t1', half)])
                    S.op('pool', lambda e, sl=sl: e.tensor_tensor(out=t2[:, sl], in0=xs32[:, sl], in1=tmb[:, 1, sl], op=ALU.mult),
                         reads=[('xs32', half), 'tmb'], writes=[('t2', half)])
                    S.op(V, lambda e, sl=sl: e.tensor_tensor(out=t1[:, sl], in0=t1[:, sl], in1=t2[:, sl], op=ALU.add),
                         reads=[('t2', half)], writes=[('t1', half)])
                    S.op(V, lambda e, half=half, sl=sl: e.tensor_tensor(out=t1[:, sl], in0=PB[1 + half][:], in1=t1[:, sl], op=ALU.add),
                         writes=[pk(1 + half), ('t1', half)])
                    S.op(V, lambda e, sl=sl, hs=hs: e.tensor_tensor(
                        out=ssm_st[:, sl].rearrange("p (h d) -> p h d", h=8), in0=ssm_st[:, sl].rearrange("p (h d) -> p h d", h=8),
                        in1=bc(cd[:, hs].unsqueeze(2), [128, 8, 64]), op=ALU.mult),
                        reads=['cd', 'ssm_stb'], writes=[('ssm_st', half)])
                    S.op(V, lambda e, half=half, sl=sl: e.tensor_tensor(out=ssm_st[:, sl], in0=PB[(5 + half) % 6][:], in1=ssm_st[:, sl], op=ALU.add),
                         writes=[pk((5 + half) % 6), ('ssm_st', half)])
                yield
                S.op('act', lambda e: e.activation(out=ssm_stb[:], in_=ssm_st[:], func=AF.Copy),
                     reads=[('ssm_st', 0), ('ssm_st', 1)], writes=['ssm_stb'])
                yield
                S.op(V, lambda e: e.tensor_tensor(out=t1[:], in0=t1[:], in1=zs[:], op=ALU.mult),
                     reads=['zs'], writes=[('t1', 0), ('t1', 1)])
                yield
                S.op('act', lambda e: e.activation(out=vsq[:], in_=t1[:], func=AF.Square), reads=[('t1', 0), ('t1', 1)], writes=['vsq'])
                yield
                S.op(V, lambda e: e.tensor_reduce(out=ss4[:], in_=vsq[:].rearrange("p (g d) -> p g d", g=4), axis=AX.X, op=ALU.add),
                     reads=['vsq'], writes=['ss4'])
                yield
                S.op(V, lambda e: e.tensor_scalar(out=ss4[:], in0=ss4[:], scalar1=1.0 / 256.0, scalar2=1e-5, op0=ALU.mult, op1=ALU.add),
                     writes=['ss4'])
                yield
                S.op('act', lambda e: e.activation(out=ss4[:], in_=ss4[:], func=AF.Sqrt), writes=['ss4'])
                yield
                S.op(V, lambda e: e.reciprocal(out=rs4[:], in_=ss4[:]), reads=['ss4'], writes=['rs4'])
                yield
                S.op(V, lambda e: e.tensor_tensor(out=t1[:].rearrange("p (g d) -> p g d", g=4), in0=t1[:].rearrange("p (g d) -> p g d", g=4),
                                                  in1=bc(rs4[:].unsqueeze(2), [128, 4, 256]), op=ALU.mult),
                     reads=['rs4', 'vsq'], writes=[('t1', 0), ('t1', 1)])
                yield
                S.op(V, lambda e: e.tensor_tensor(out=ytm[:], in0=t1[:], in1=tmb[:, 0, :], op=ALU.mult),
                     reads=[('t1', 0), ('t1', 1), 'tmb'], writes=['ytm'])
                yield

            else:
                if c == 0:
                    for _ in p1a_gen(0):
                        pass
                nxt = p1a_gen(c + 1) if c + 1 < NCH else iter(())
                for half in range(2):
                    bank = 4 + half

                    def mm_g(e, half=half, bank=bank):
                        for kc in range(2):
                            inst = e.matmul(PB[bank][:], lob[:, 2 + kc, :], wgb[:, kc, half * 512:(half + 1) * 512],
                                            start=(kc == 0), stop=(kc == 1))
                        return inst
                    S.op('pe', mm_g, reads=[('lob', 2), 'wgb'], writes=[pk(bank)])
                    evac_copy(gtm[:, half * 512:(half + 1) * 512], PB[bank][:], [], [pk(bank), ('gtm', half)])
                S.op('act', lambda e: e.activation(out=w2[:], in_=Lr[:], func=AF.Exp, scale=-CDEC), reads=LrK, writes=['w2'])
                S.op(V, lambda e: e.tensor_copy(out=gC[:].unsqueeze(2), in_=w2[:, :, 127:128]), reads=['w2'], writes=['gC'])
                S.op(V, lambda e: e.tensor_tensor(out=KR[:, :, 128:256], in0=sr[:, 0:8, :], in1=w2[:], op=ALU.mult),
                     reads=['sr', 'w2'], writes=[('KR', 1)])
                S.op('act', lambda e: e.activation(out=w3[:], in_=Lr[:], func=AF.Exp, scale=CDEC), reads=LrK, writes=['w3'])
                S.op(V, lambda e: e.tensor_tensor(out=BhT[:], in0=bb[:], in1=w3[:], op=ALU.mult), reads=['bb', 'w3'], writes=['BhT'])
                S.op('pool', lambda e: e.tensor_tensor(out=KhT[:], in0=kpr[:], in1=w3[:], op=ALU.mult), reads=['kpr', 'w3'], writes=['KhT'])
                S.op(V, lambda e: e.tensor_tensor(out=w1[:], in0=Lr[:], in1=dsig[:], op=ALU.subtract), reads=LrK + ['dsig', 'kap'], writes=['w1'])
                S.op('act', lambda e: e.activation(out=w1[:], in_=w1[:], func=AF.Exp, scale=-CDEC), writes=['w1'])
                S.op(V, lambda e: e.tensor_tensor(out=KR[:, :, 0:128], in0=kap[:], in1=w1[:], op=ALU.mult), reads=['kap', 'w1'], writes=[('KR', 0)])
                S.op(V, lambda e: e.tensor_tensor(out=w2[:], in0=Lr[:], in1=bc(Lr[:, :, 127:128], [128, 8, 128]), op=ALU.subtract),
                     reads=LrK + ['gC', ('KR', 1)], writes=['w2'])
                S.op('act', lambda e: e.activation(out=w2[:], in_=w2[:], func=AF.Exp, scale=CDEC), writes=['w2'])
                S.op(V, lambda e: e.tensor_tensor(out=BGT[:], in0=bb[:], in1=w2[:], op=ALU.mult), reads=['bb', 'w2'], writes=['BGT'])
                S.op('pool', lambda e: e.tensor_tensor(out=KGT[:], in0=kpr[:], in1=w2[:], op=ALU.mult), reads=['kpr', 'w2'], writes=['KGT'])
                S.op('pool', lambda e: e.tensor_tensor(out=w3[:], in0=sr[:, 0:8, :], in1=kpr[:], op=ALU.mult),
                     reads=['sr', 'kpr', 'BhT', 'KhT'], writes=['w3'])
                S.op('pool', lambda e: e.tensor_tensor(out=rkrb[:], in0=w3[:], in1=bc(chp[:, 4, :].unsqueeze(2), [128, 8, 128]), op=ALU.mult),
                     reads=['w3', 'chp'], writes=['rkrb'])
                for half in range(2):
                    bank = half

                    def trv(e, half=half, bank=bank):
                        for i in range(4):
                            inst = e.transpose(PB[bank][:, i * 128:(i + 1) * 128], sr[:, 16 + half * 4 + i, :], idf[:])
                        return inst
                    S.op('pe', trv, reads=['sr', 'idf'], writes=[pk(bank)])
                    S.op('act', lambda e, half=half, bank=bank: e.activation(out=Vtm[:, half * 512:(half + 1) * 512], in_=PB[bank][:], func=AF.Copy),
                         writes=[pk(bank), ('Vtm', half)])
                    S.op(V, lambda e, half=half, bank=bank: e.tensor_copy(out=Vb[:, half * 512:(half + 1) * 512], in_=PB[bank][:]),
                         writes=[pk(bank), ('Vb', half)])
                VbK = [('Vb', 0), ('Vb', 1)]
                for wi, (src, dst, key) in enumerate([(BGT, BGtm, 'BG'), (KGT, KGtm, 'KG')]):
                    bank = 2 + wi

                    def trg(e, src=src, bank=bank):
                        pbf = PB[bank][:].bitcast(BF16)
                        for i in range(8):
                            inst = e.transpose(pbf[:, i * 128:(i + 1) * 128], src[:, i, :], idb[:])
                        return inst
                    S.op('pe', trg, reads=[key + 'T', 'idb'], writes=[pk(bank)])
                    evac_copy(dst[:], PB[bank][:].bitcast(BF16), [], [pk(bank), key + 'tm'])
                def mm_bs(e):
                    for hp in range(8):
                        inst = e.matmul(PB[4][:, hp * 2:(hp + 1) * 2], rkrb[:, hp, :], sel2[:], start=True, stop=True)
                    return inst
                S.op('pe', mm_bs, reads=['rkrb', 'sel2'], writes=[pk(4)])
                S.op(V, lambda e: e.tensor_copy(out=bs16[:], in_=PB[4][:, 0:16]), writes=[pk(4), 'bs16'])
                for zi, (srcT, dstT, skey, dkey) in enumerate([(BhT, BhTz, 'BhT', 'BhTz'), (KhT, KhTz, 'KhT', 'KhTz'), (None, KtTz, ('KR', 0), 'KtTz')]):
                    for par in range(2):
                        src_ap = KR[:, :, 0:128] if srcT is None else srcT[:]
                        dst_ap = dstT[:].rearrange("p (a two) t -> p a two t", two=2)[:, :, par, :]
                        if (zi + par) % 2 == 0:
                            S.op('act', lambda e, src_ap=src_ap, dst_ap=dst_ap, par=par: e.activation(out=dst_ap, in_=src_ap, func=AF.Copy, scale=pm[:, par:par + 1]),
                                 reads=[skey, 'pm'], writes=[(dkey, par)])
                        else:
                            S.op(V, lambda e, src_ap=src_ap, dst_ap=dst_ap, par=par: e.tensor_scalar(out=dst_ap, in0=src_ap, scalar1=pm[:, par:par + 1], scalar2=None, op0=ALU.mult),
                                 reads=[skey, 'pm'], writes=[(dkey, par)])
                ZK = [(k_, p_) for k_ in ('BhTz', 'KhTz', 'KtTz') for p_ in range(2)]
                for g4 in range(4):
                    hs = range(g4 * 4, g4 * 4 + 4)
                    bA = [5, 6]
                    bB = [7, 0]
                    bA2 = 1

                    def mm_in(e, hs=hs):
                        for i, h in enumerate(hs):
                            hp = h // 2
                            e.matmul(PB[bA[i // 2]][:, (i % 2) * 256:(i % 2 + 1) * 256], BhTz[:, h, :], KR[:, hp, :], start=True, stop=True)
                            e.matmul(PB[bB[i // 2]][:, (i % 2) * 256:(i % 2 + 1) * 256], KhTz[:, h, :], KR[:, hp, :], start=True, stop=True)
                            inst = e.matmul(PB[bA2][:, i * 128:(i + 1) * 128], KtTz[:, h, :], BhT[:, hp, :], start=True, stop=True)
                        return inst
                    S.op('pe', mm_in, reads=['BhT', 'KhT', ('KR', 0), ('KR', 1)] + ZK, writes=[pk(5), pk(6), pk(7), pk(0), pk(1)])
                    for i2 in range(2):
                        hh = slice(g4 * 4 + i2 * 2, g4 * 4 + i2 * 2 + 2)
                        pa = PB[bA[i2]][:].rearrange("p (h w t) -> p h w t", h=2, w=2)
                        pb_ = PB[bB[i2]][:].rearrange("p (h w t) -> p h w t", h=2, w=2)
                        S.op(V, lambda e, hh=hh, pa=pa: e.scalar_tensor_tensor(
                            out=X0[:, hh, :], in0=pa[:, :, 0, :], scalar=-1.0, in1=bc(mUs[:].unsqueeze(1), [128, 2, 128]),
                            op0=ALU.mult, op1=ALU.mult), reads=['mUs'], writes=[pk(bA[i2]), ('X0', g4, i2)])
                        S.op(V, lambda e, hh=hh, pa=pa: e.tensor_tensor(
                            out=PTm[:, hh, :], in0=pa[:, :, 1, :], in1=bc(mUi[:].unsqueeze(1), [128, 2, 128]), op=ALU.mult),
                            reads=['mUi'], writes=[pk(bA[i2]), ('PT', g4, i2)])
                        S.op(V, lambda e, hh=hh, pb_=pb_: e.tensor_tensor(
                            out=BmT[:, hh, :], in0=pb_[:, :, 0, :], in1=bc(mUs[:].unsqueeze(1), [128, 2, 128]), op=ALU.mult),
                            reads=['mUs'], writes=[pk(bB[i2]), ('BmT', g4, i2)])
                        S.op(V, lambda e, hh=hh, pb_=pb_: e.tensor_tensor(
                            out=QTm[:, hh, :], in0=pb_[:, :, 1, :], in1=bc(mUi[:].unsqueeze(1), [128, 2, 128]), op=ALU.mult),
                            reads=['mUi'], writes=[pk(bB[i2]), ('QT', g4, i2)])
                    h4 = slice(g4 * 4, g4 * 4 + 4)
                    S.op(V, lambda e, h4=h4: e.scalar_tensor_tensor(
                        out=XT0[:, h4, :], in0=PB[bA2][:].rearrange("p (h t) -> p h t", h=4), scalar=-1.0,
                        in1=bc(mLs[:].unsqueeze(1), [128, 4, 128]), op0=ALU.mult, op1=ALU.mult),
                        reads=['mLs'], writes=[pk(bA2), ('XT0', g4)])
                    S.op('pool', lambda e, h4=h4: e.tensor_tensor(out=Gm[:, h4, :], in0=X0[:, h4, :], in1=bc(idb[:].unsqueeze(1), [128, 4, 128]), op=ALU.add),
                         reads=[('X0', g4, 0), ('X0', g4, 1), 'idb'], writes=[('G', g4)])
                Xs, XTs = [X0, X1], [XT0, XT1]
                for r in range(1, 8):
                    for _ in range(2):
                        next(nxt, None)
                    src, dst = (r - 1) % 2, r % 2
                    for g4 in range(4):
                        h4 = slice(g4 * 4, g4 * 4 + 4)
                        kx_src = [('X0', g4, 0), ('X0', g4, 1)] if r == 1 else [('X', src, g4)]
                        kxt_src = [('XT0', g4)] if r == 1 else [('XT', src, g4)]
                        banks = [(g4 * 2) % 8, (g4 * 2 + 1) % 8, 0]
                        b0, b1 = (2 + (r * 4 + g4) * 3) % 8, (3 + (r * 4 + g4) * 3) % 8
                        b2 = (4 + (r * 4 + g4) * 3) % 8
                        if r <= 5:
                            def mmx(e, h4=h4, src=src, b0=b0):
                                for i, h in enumerate(range(h4.start, h4.stop)):
                                    inst = e.matmul(PB[b0][:, i * 128:(i + 1) * 128], XTs[src][:, h, :], Xs[src][:, h, :], start=True, stop=True)
                                return inst
                            S.op('pe', mmx, reads=kx_src + kxt_src, writes=[pk(b0)])
                            evac_copy(Xs[dst][:, h4, :], PB[b0][:].rearrange("p (h t) -> p h t", h=4), [], [pk(b0), ('X', dst, g4)])
                        if r <= 6:
                            def mmxt(e, h4=h4, src=src, b1=b1):
                                for i, h in enumerate(range(h4.start, h4.stop)):
                                    inst = e.matmul(PB[b1][:, i * 128:(i + 1) * 128], Xs[src][:, h, :], XTs[src][:, h, :], start=True, stop=True)
                                return inst
                            S.op('pe', mmxt, reads=kx_src + kxt_src, writes=[pk(b1)])
                            evac_copy(XTs[dst][:, h4, :], PB[b1][:].rearrange("p (h t) -> p h t", h=4), [], [pk(b1), ('XT', dst, g4)])
                        if r >= 2:
                            def mmg(e, h4=h4, src=src, b2=b2):
                                for i, h in enumerate(range(h4.start, h4.stop)):
                                    inst = e.matmul(PB[b2][:, i * 128:(i + 1) * 128], XTs[src][:, h, :], Gm[:, h, :], start=True, stop=True)
                                return inst
                            S.op('pe', mmg, reads=kxt_src + [('G', g4)], writes=[pk(b2)])
                            S.op(V, lambda e, h4=h4, b2=b2: e.tensor_tensor(out=Gm[:, h4, :], in0=PB[b2][:].rearrange("p (h t) -> p h t", h=4),
                                                                            in1=Gm[:, h4, :], op=ALU.add),
                                 writes=[pk(b2), ('G', g4)])
                for _ in nxt:
                    pass
                GK = [('G', g4) for g4 in range(4)]
                BmK = [('BmT', g4, i2) for g4 in range(4) for i2 in range(2)]
                PK_ = [('PT', g4, i2) for g4 in range(4) for i2 in range(2)]
                QK = [('QT', g4, i2) for g4 in range(4) for i2 in range(2)]
                for half in range(2):
                    bank = half

                    def mmz(e, half=half, bank=bank):
                        for hpl in range(4):
                            hp = half * 4 + hpl
                            e.matmul(PB[bank][:, hpl * 128:(hpl + 1) * 128], KR[:, hp, 0:128], rstb[:, hp, :], start=True, stop=False)
                            for par in range(2):
                                h = 2 * hp + par
                                inst = e.matmul(PB[bank][:, hpl * 128 + par * 64:hpl * 128 + (par + 1) * 64], BmT[:, h, :], Vb[:, h * 64:(h + 1) * 64],
                                                start=False, stop=(par == 1))
                        return inst
                    S.op('pe', mmz, reads=[('KR', 0), 'rstb'] + BmK + VbK, writes=[pk(bank)])
                    evac_copy(Zb[:, half * 512:(half + 1) * 512], PB[bank][:], [], [pk(bank), ('Zb', half)])
                for half in range(2):
                    bank = 2 + half

                    def mmu(e, half=half, bank=bank):
                        for hh in range(8):
                            h = half * 8 + hh
                            inst = e.matmul(PB[bank][:, hh * 64:(hh + 1) * 64], Gm[:, h, :], Zb[:, h * 64:(h + 1) * 64], start=True, stop=True)
                        return inst
                    S.op('pe', mmu, reads=GK + [('Zb', half)], writes=[pk(bank)])
                    S.op('act', lambda e, half=half, bank=bank: e.activation(out=Ub[:, half * 512:(half + 1) * 512], in_=PB[bank][:], func=AF.Copy, scale=-1.0),
                         writes=[pk(bank), ('Ub', half)])
                UbK = [('Ub', 0), ('Ub', 1)]
                for half in range(2):
                    bank = 4 + half

                    def mmy(e, half=half, bank=bank):
                        for hpl in range(4):
                            hp = half * 4 + hpl
                            e.matmul(PB[bank][:, hpl * 128:(hpl + 1) * 128], KR[:, hp, 128:256], rstb[:, hp, :], start=True, stop=False)
                            for par in range(2):
                                h = 2 * hp + par
                                cs = slice(hpl * 128 + par * 64, hpl * 128 + (par + 1) * 64)
                                e.matmul(PB[bank][:, cs], PTm[:, h, :], Ub[:, h * 64:(h + 1) * 64], start=False, stop=False)
                                inst = e.matmul(PB[bank][:, cs], QTm[:, h, :], Vb[:, h * 64:(h + 1) * 64], start=False, stop=(par == 1))
                        return inst
                    S.op('pe', mmy, reads=[('KR', 1), 'rstb'] + PK_ + QK + UbK + VbK, writes=[pk(bank)])
                    evac_copy(yr[:, half * 512:(half + 1) * 512], PB[bank][:], [], [pk(bank), ('yr', half)])

                def mms(e):
                    for hp in range(8):
                        o_ap = PB[6 + hp // 4][:, (hp % 4) * 128:(hp % 4 + 1) * 128]
                        e.matmul(o_ap, BGtm[:, hp * 128:(hp + 1) * 128], Ub[:, hp * 128:(hp + 1) * 128], start=True, stop=False)
                        inst = e.matmul(o_ap, KGtm[:, hp * 128:(hp + 1) * 128], Vb[:, hp * 128:(hp + 1) * 128], start=False, stop=True)
                    return inst
                S.op('pe', mms, reads=['BGtm', 'KGtm'] + UbK + VbK, writes=[pk(6), pk(7)])
                S.op(V, lambda e: e.tensor_tensor(out=rst[:], in0=rst[:], in1=bc(gC[:].unsqueeze(2), [128, 8, 128]), op=ALU.mult),
                     reads=['gC', 'rstb'], writes=['rst'])
                for half in range(2):
                    S.op(V, lambda e, half=half: e.tensor_tensor(out=rst[:, half * 4:(half + 1) * 4, :], in0=PB[6 + half][:].rearrange("p (a i) -> p a i", a=4),
                                                                 in1=rst[:, half * 4:(half + 1) * 4, :], op=ALU.add),
                         writes=[pk(6 + half), 'rst'])
                S.op(V, lambda e: e.tensor_tensor(out=rst[:], in0=rst[:], in1=bc(blkf[:].unsqueeze(1), [128, 8, 128]), op=ALU.mult),
                     reads=['blkf'], writes=['rst'])
                S.op('act', lambda e: e.activation(out=rstb[:], in_=rst[:], func=AF.Copy), reads=['rst'], writes=['rstb'])
                yrK = [('yr', 0), ('yr', 1)]
                yr3 = yr[:].rearrange("p (h d) -> p h d", h=16)
                S.op(V, lambda e: e.tensor_reduce(out=s16a[:], in_=yr3, axis=AX.X, op=ALU.add), reads=yrK, writes=['s16a'])
                S.op('act', lambda e: e.activation(out=t2[:], in_=yr[:], func=AF.Square), reads=yrK, writes=[('t2', 0), ('t2', 1)])
                S.op(V, lambda e: e.tensor_reduce(out=s16b[:], in_=t2[:].rearrange("p (h d) -> p h d", h=16), axis=AX.X, op=ALU.add),
                     reads=[('t2', 0), ('t2', 1)], writes=['s16b'])
                S.op(V, lambda e: e.tensor_scalar(out=s16a[:], in0=s16a[:], scalar1=1.0 / 64.0, scalar2=None, op0=ALU.mult), writes=['s16a'])
                S.op(V, lambda e: e.tensor_tensor(out=s16c[:], in0=s16a[:], in1=s16a[:], op=ALU.mult), reads=['s16a'], writes=['s16c'])
                S.op(V, lambda e: e.scalar_tensor_tensor(out=s16b[:], in0=s16b[:], scalar=1.0 / 64.0, in1=s16c[:], op0=ALU.mult, op1=ALU.subtract),
                     reads=['s16c'], writes=['s16b'])
                S.op(V, lambda e: e.tensor_scalar(out=s16b[:], in0=s16b[:], scalar1=64e-5, scalar2=None, op0=ALU.add), writes=['s16b'])
                S.op('act', lambda e: e.activation(out=s16b[:], in_=s16b[:], func=AF.Sqrt), writes=['s16b'])
                S.op(V, lambda e: e.reciprocal(out=s16d[:], in_=s16b[:]), reads=['s16b'], writes=['s16d'])
                S.op(V, lambda e: e.tensor_tensor(out=yr3, in0=yr3, in1=bc(s16a[:].unsqueeze(2), [128, 16, 64]), op=ALU.subtract),
                     reads=['s16a', ('t2', 0), ('t2', 1)], writes=yrK)
                S.op(V, lambda e: e.tensor_tensor(out=yr3, in0=yr3, in1=bc(s16d[:].unsqueeze(2), [128, 16, 64]), op=ALU.mult),
                     reads=['s16d'], writes=yrK)
                S.op(V, lambda e: e.tensor_tensor(out=yr[:], in0=yr[:], in1=tmb[:, 0, :], op=ALU.mult), reads=['tmb'], writes=yrK)
                S.op(V, lambda e: e.tensor_tensor(out=yr[:], in0=yr[:], in1=tmb[:, 1, :], op=ALU.add), reads=['tmb'], writes=yrK)
                S.op('pool', lambda e: e.tensor_tensor(out=t2[:].rearrange("p (h d) -> p h d", h=16), in0=Vtm[:].rearrange("p (h d) -> p h d", h=16),
                                                        in1=bc(bs16[:].unsqueeze(2), [128, 16, 64]), op=ALU.mult),
                     reads=[('Vtm', 0), ('Vtm', 1), 'bs16', 's16b'], writes=[('t2', 0), ('t2', 1)])
                S.op(V, lambda e: e.tensor_tensor(out=yr[:], in0=yr[:], in1=t2[:], op=ALU.add), reads=[('t2', 0), ('t2', 1)], writes=yrK)
                S.op(V, lambda e: e.tensor_tensor(out=ytm[:], in0=yr[:], in1=gtm[:], op=ALU.mult),
                     reads=yrK + [('gtm', 0), ('gtm', 1)], writes=['ytm'])
            def try_(e):
                pbf = PB[YB][:].bitcast(BF16)
                for i in range(8):
                    inst = e.transpose(pbf[:, i * 128:(i + 1) * 128], ytm[:, i * 128:(i + 1) * 128], idb[:])
                return inst
            S.op('pe', try_, reads=['ytm', 'idb'], writes=[pk(YB)])
            evac_copy(yT[:, :, q * 128:(q + 1) * 128], PB[YB][:].bitcast(BF16).rearrange("p (a t) -> p a t", a=8), [], [pk(YB), ('yT', q)])
            if q == 3:
                agin_bf = agin[tt].ap().bitcast(BF16)
                S.op('act', lambda e, agin_bf=agin_bf: e.dma_start(
                    out=agin_bf[which * 1024:(which + 1) * 1024, :].rearrange("(a p) t -> p a t", p=128), in_=yT[:]),
                    reads=[('yT', q_) for q_ in range(4)], writes=[('agin', tt, which)], stream='agst')
                if not SSD:
                    S.op('pool', lambda e, tt=tt: e.collective_compute(
                        "AllGather", ALU.bypass, replica_groups=[[0, 1], [2, 3], [4, 5], [6, 7]],
                        ins=[agin[tt].ap().opt()], outs=[agout[tt].ap().opt()]),
                        reads=[('agin', tt, 0), ('agin', tt, 1)], writes=[('agout', tt)], stream='cc', inc=1)

        if SSD:
            for _ in a1_tile(0):
                pass
        for c in range(NCH):
            if SSD:
                if c % 4 == 0:
                    a1g = a1_tile(c // 4 + 1) if c // 4 + 1 < NT else iter(())
                k = 0
                for _ in chunk_gen(c):
                    k += 1
                    if k % 4 == 0:
                        next(a1g, None)
                if c % 4 == 3:
                    for _ in a1g:
                        pass
            else:
                for _ in chunk_gen(c):
                    pass
        S.flush(final_streams=['ld', 'st', 'cc', 'ld2'])


def phase_b(nc, S, PB, pk, sb, G):
    agout, xTr, pT_d, lnp_d, outT, sel_d = G['agout'], G['xTr'], G['pT'], G['lnp'], G['outT'], G['sel']
    w_out_bf, w_up_bf, w_down_bf, w_gate_bf, w_ple_bf = G['w_out_bf'], G['w_up_bf'], G['w_down_bf'], G['w_gate_bf'], G['w_ple_bf']
    w_down_v = w_down_bf.rearrange("r (s c) -> (r s) c", s=2)
    w_ple_v = w_ple_bf.rearrange("r (s c) -> (r s) c", s=16)
    from contextlib import ExitStack
    V = 'dve'
    NS = 5
    with ExitStack() as es:
        def T(name, shape, dt=F32):
            return es.enter_context(sb("b_" + name, shape, dt))
        hb = T("hb", [128, 32, TT], BF16)
        acc = T("acc", [128, 32, TT])
        hid = T("hid", [128, 16, TT], BF16)
        ring = T("ring", [128, NS, 4096], BF16)
        lnp = T("lnp", [128, 6, 32]); sel = T("sel", [128, 2])
        wp = T("wp", [128, 32, 256], BF16); lnpa = T("lnpa", [128, 6, 32])
        ptb = T("ptb", [128, 2, TT], BF16); ptf = T("ptf", [128, 2, TT])
        sq = T("sq", [128, TT]); sq2 = T("sq2", [128, TT]); red = T("red", [128, 2, TT]); redp = T("redp", [128, TT])
        mean = T("mean", [128, TT]); rstd = T("rstd", [128, TT]); ones_f = T("ones", [128, 128])
        pe_sb = T("pe_sb", [128, TT]); sg_sb = T("sg_sb", [128, TT])
        S.op('sp', lambda e: e.dma_start(out=lnp[:], in_=lnp_d), writes=['lnp'], stream='ldc')
        S.op('sp', lambda e: e.dma_start(out=sel[:], in_=sel_d), writes=['sel'], stream='ldc')
        S.op(V, lambda e: e.tensor_scalar(out=lnpa[:], in0=lnp[:], scalar1=ALPHA, scalar2=None, op0=ALU.mult), reads=['lnp'], writes=['lnpa'])
        S.op(V, lambda e: e.memset(ones_f[:], 1.0), writes=['ones_b'])
        S.op('sp', lambda e: e.dma_start(out=wp[:], in_=w_ple_v.rearrange("(a p) c -> p a c", p=128)), reads=[('wplebf', 0)], writes=['wp'], stream='ldc')
        rs = [0]
        ps = [0]
        ACC = [('acc', i) for i in range(32)]
        HB = [('hb', i) for i in range(32)]

        def wload(src_ap, key_reads, n=4096):
            slot = rs[0] % NS
            rs[0] += 1
            S.op('sp', lambda e: e.dma_start(out=ring[:, slot, 0:n], in_=src_ap), reads=key_reads, writes=[('bring', slot)], stream='bring%d' % slot)
            return slot

        def nbank():
            b = ps[0] % 8
            ps[0] += 1
            return b

        def stat_accum(i):
            sqb, sqk = (sq, 'sq') if i % 2 == 0 else (sq2, 'sq2')
            S.op('act', lambda e: e.activation(out=sqb[:], in_=acc[:, i, :], func=AF.Square), reads=[('acc', i)], writes=[sqk])
            if i == 0:
                S.op(V, lambda e: e.tensor_copy(out=red[:, 0, :], in_=acc[:, i, :]), reads=[('acc', i)], writes=[('red', 0)])
                S.op(V, lambda e: e.tensor_copy(out=red[:, 1, :], in_=sqb[:]), reads=[sqk], writes=[('red', 1)])
            else:
                S.op(V, lambda e: e.tensor_tensor(out=red[:, 0, :], in0=red[:, 0, :], in1=acc[:, i, :], op=ALU.add), reads=[('acc', i)], writes=[('red', 0)])
                S.op(V, lambda e: e.tensor_tensor(out=red[:, 1, :], in0=red[:, 1, :], in1=sqb[:], op=ALU.add), reads=[sqk], writes=[('red', 1)])

        def layer_norm(li, last):
            b = nbank()
            S.op('pe', lambda e: e.matmul(PB[b][:], ones_f[:], red[:, 0, :], start=True, stop=True), reads=['ones_b', ('red', 0)], writes=[pk(b)])
            S.op(V, lambda e: e.tensor_scalar(out=mean[:], in0=PB[b][:], scalar1=1.0 / D, scalar2=None, op0=ALU.mult), writes=[pk(b), 'mean'])
            b2 = nbank()
            S.op('pe', lambda e: e.matmul(PB[b2][:], ones_f[:], red[:, 1, :], start=True, stop=True), reads=['ones_b', ('red', 1)], writes=[pk(b2)])
            S.op(V, lambda e: e.tensor_tensor(out=redp[:], in0=mean[:], in1=mean[:], op=ALU.mult), reads=['mean'], writes=['redp'])
            S.op(V, lambda e: e.scalar_tensor_tensor(out=rstd[:], in0=PB[b2][:], scalar=1.0 / D, in1=redp[:], op0=ALU.mult, op1=ALU.subtract),
                 reads=['redp'], writes=[pk(b2), 'rstd'])
            S.op(V, lambda e: e.tensor_scalar(out=rstd[:], in0=rstd[:], scalar1=1e-5, scalar2=None, op0=ALU.add), writes=['rstd'])
            S.op('act', lambda e: e.activation(out=rstd[:], in_=rstd[:], func=AF.Sqrt), writes=['rstd'])
            S.op(V, lambda e: e.reciprocal(out=rstd[:], in_=rstd[:]), writes=['rstd'])
            for i in range(32):
                S.op(V, lambda e, i=i: e.tensor_tensor(out=acc[:, i, :], in0=acc[:, i, :], in1=mean[:], op=ALU.subtract),
                     reads=['mean'], writes=[('acc', i)])
                S.op(V, lambda e, i=i: e.tensor_tensor(out=acc[:, i, :], in0=acc[:, i, :], in1=rstd[:], op=ALU.mult),
                     reads=['rstd'], writes=[('acc', i)])
                if last:
                    S.op('act', lambda e, i=i: e.activation(out=acc[:, i, :], in_=acc[:, i, :], func=AF.Identity,
                                                            scale=lnp[:, 2 * li, i:i + 1], bias=lnp[:, 2 * li + 1, i:i + 1]),
                         reads=['lnp'], writes=[('acc', i)])
                else:
                    if i % 2 == 0:
                        S.op('act', lambda e, i=i: e.activation(out=hb[:, i, :], in_=acc[:, i, :], func=AF.Identity,
                                                                scale=lnp[:, 2 * li, i:i + 1], bias=lnp[:, 2 * li + 1, i:i + 1]),
                             reads=['lnp', ('acc', i)], writes=[('hb', i)])
                    else:
                        S.op('pool', lambda e, i=i: e.tensor_scalar(out=hb[:, i, :], in0=acc[:, i, :], scalar1=lnp[:, 2 * li, i:i + 1],
                                                                    scalar2=lnp[:, 2 * li + 1, i:i + 1], op0=ALU.mult, op1=ALU.add),
                             reads=['lnp', ('acc', i)], writes=[('hb', i)])
                    S.op('act', lambda e, i=i: e.activation(out=acc[:, i, :], in_=acc[:, i, :], func=AF.Identity,
                                                            scale=lnpa[:, 2 * li, i:i + 1], bias=lnpa[:, 2 * li + 1, i:i + 1]),
                         reads=['lnpa', ('hb', i)], writes=[('acc', i)])

        if DEBUG_STAGE == 4:
            dbgB = nc.dram_tensor("dbgB", [5, D, TT], F32, kind="ExternalOutput").ap()

        def ckpt(k, tl):
            if DEBUG_STAGE == 4 and tl == 0:
                S.op('sp', lambda e: e.dma_start(out=dbgB[k, :, :].rearrange("(a p) t -> p a t", p=128), in_=acc[:]),
                     reads=ACC, writes=[('dbgB', k)], stream='dbg')

        for tl in range(4):
            if DEBUG_STAGE == 4 and tl == 1:
                break
            ya = agout[tl].ap().bitcast(BF16)
            yb_ = agout[tl + 4].ap().bitcast(BF16)
            S.op('sp', lambda e, ya=ya: e.dma_start(out=hb[:], in_=ya.rearrange("(a p) t -> p a t", p=128)),
                 reads=[('agout', tl)], writes=HB, stream='hbld')
            S.op('sp', lambda e, tl=tl: e.dma_start(out=ptf[:], in_=pT_d[:, :, tl * TT:(tl + 1) * TT]), writes=['ptf'], stream='ptld')
            S.op(V, lambda e: e.tensor_copy(out=ptb[:], in_=ptf[:]), reads=['ptf'], writes=['ptb'])
            for i in range(4):
                slot = wload(yb_[i * 1024:(i + 1) * 1024, :].rearrange("(a p) t -> p a t", p=128), [('agout', tl + 4)])
                hk = [('hb', 8 * i + k) for k in range(8)]
                S.op(V, lambda e, i=i: e.tensor_scalar(out=hb[:, 8 * i:8 * i + 8, :], in0=hb[:, 8 * i:8 * i + 8, :], scalar1=sel[:, 0:1], scalar2=None, op0=ALU.mult),
                     reads=['sel'], writes=hk)
                S.op(V, lambda e, i=i, slot=slot: e.scalar_tensor_tensor(
                    out=hb[:, 8 * i:8 * i + 8, :], in0=ring[:, slot, :].rearrange("p (a t) -> p a t", a=8), scalar=sel[:, 1:2],
                    in1=hb[:, 8 * i:8 * i + 8, :], op0=ALU.mult, op1=ALU.add), reads=['sel', ('bring', slot)], writes=hk)
            S.op('sp', lambda e, tl=tl: e.dma_start(out=acc[:], in_=xTr[:, tl * TT:(tl + 1) * TT].rearrange("(a p) t -> p a t", p=128)),
                 writes=ACC, stream='accld')
            for i in range(32):
                slot = wload(w_out_bf[i * 128:(i + 1) * 128, :], [('woutbf', i // 8)])
                b = nbank()

                def mm(e, slot=slot, b=b):
                    for kc in range(32):
                        inst = e.matmul(PB[b][:], ring[:, slot, kc * 128:(kc + 1) * 128], hb[:, kc, :], start=(kc == 0), stop=(kc == 31))
                    return inst
                S.op('pe', mm, reads=[('bring', slot)] + HB, writes=[pk(b)])
                S.op(V, lambda e, i=i, b=b: e.scalar_tensor_tensor(out=acc[:, i, :], in0=acc[:, i, :], scalar=ALPHA, in1=PB[b][:], op0=ALU.mult, op1=ALU.add),
                     writes=[pk(b), ('acc', i)])
                stat_accum(i)
            ckpt(0, tl)
            layer_norm(0, False)
            ckpt(1, tl)
            for fb in range(8):
                for j in range(16):
                    blk = fb * 16 + j
                    if tl == 0:
                        G['issue_casts'](1)
                    slot = wload(w_up_bf[blk * 128:(blk + 1) * 128, :], [('wupbf', blk // 8)])
                    b = nbank()

                    def mmu(e, slot=slot, b=b):
                        for kc in range(32):
                            inst = e.matmul(PB[b][:], ring[:, slot, kc * 128:(kc + 1) * 128], hb[:, kc, :], start=(kc == 0), stop=(kc == 31))
                        return inst
                    S.op('pe', mmu, reads=[('bring', slot)] + HB, writes=[pk(b)])
                    S.op('act', lambda e, b=b: e.activation(out=sg_sb[:], in_=PB[b][:], func=AF.Square), writes=[pk(b), 'sg_sb'])
                    S.op(V, lambda e, b=b, j=j: e.scalar_tensor_tensor(out=hid[:, j, :], in0=PB[b][:], scalar=0.0, in1=sg_sb[:], op0=ALU.is_gt, op1=ALU.mult),
                         reads=['sg_sb'], writes=[pk(b), ('hid', j)])
                for i in range(32):
                    r0 = (fb * 32 + i) * 128
                    slot = wload(w_down_v[r0:r0 + 128, :], [('wdownbf', r0 // 2 // 1024)], n=2048)
                    b = nbank()

                    def mmd(e, slot=slot, b=b):
                        for kc in range(16):
                            inst = e.matmul(PB[b][:], ring[:, slot, kc * 128:(kc + 1) * 128], hid[:, kc, :], start=(kc == 0), stop=(kc == 15))
                        return inst
                    S.op('pe', mmd, reads=[('bring', slot)] + [('hid', j) for j in range(16)], writes=[pk(b)])
                    S.op(V, lambda e, i=i, b=b: e.tensor_tensor(out=acc[:, i, :], in0=PB[b][:], in1=acc[:, i, :], op=ALU.add), writes=[pk(b), ('acc', i)])
                    if fb == 7:
                        stat_accum(i)
            ckpt(2, tl)
            layer_norm(1, False)
            ckpt(3, tl)
            for i in range(32):
                slot = wload(w_gate_bf[i * 128:(i + 1) * 128, :], [('wgatebf', i // 8)])
                b = nbank()

                def mmg(e, slot=slot, b=b):
                    for kc in range(32):
                        inst = e.matmul(PB[b][:], ring[:, slot, kc * 128:(kc + 1) * 128], hb[:, kc, :], start=(kc == 0), stop=(kc == 31))
                    return inst
                S.op('pe', mmg, reads=[('bring', slot)] + HB, writes=[pk(b)])
                S.op('act', lambda e, b=b: e.activation(out=sg_sb[:], in_=PB[b][:], func=AF.Sigmoid), writes=[pk(b), 'sg_sb'])
                b2 = nbank()

                def mmp(e, b2=b2, i=i):
                    for kc in range(2):
                        inst = e.matmul(PB[b2][:], wp[:, i, kc * 128:(kc + 1) * 128], ptb[:, kc, :], start=(kc == 0), stop=(kc == 1))
                    return inst
                S.op('pe', mmp, reads=['wp', 'ptb'], writes=[pk(b2)])
                S.op(V, lambda e, b2=b2: e.tensor_tensor(out=pe_sb[:], in0=PB[b2][:], in1=sg_sb[:], op=ALU.mult), reads=['sg_sb'], writes=[pk(b2), 'pe_sb'])
                S.op(V, lambda e, i=i: e.tensor_tensor(out=acc[:, i, :], in0=acc[:, i, :], in1=pe_sb[:], op=ALU.add), reads=['pe_sb'], writes=[('acc', i)])
                stat_accum(i)
            ckpt(4, tl)
            layer_norm(2, True)
            S.op('act', lambda e, tl=tl: e.dma_start(out=outT[:, tl * TT:(tl + 1) * TT].rearrange("(a p) t -> p a t", p=128), in_=acc[:]),
                 reads=ACC, writes=[('out', tl)], stream='outst')
        S.flush(final_streams=['ld', 'st', 'cc', 'cast'])


_NC = None


def _prep(inputs):
    f = np.float32
    g = lambda k: np.asarray(inputs[k], dtype=f)[0]
    x = np.asarray(inputs["x"], dtype=f)
    p = np.asarray(inputs["p"], dtype=f)[0]
    w_in = g("w_in")
    D_SSM = 2048
    OFF_R = 6176

    def blkfmt(w):
        K, C = w.shape
        nb, kc = C // 128, K // 128
        return np.ascontiguousarray(w.reshape(kc, 128, nb, 128).transpose(2, 1, 0, 3)).reshape(nb * 128 * kc * 128 // 4096, 4096)

    w_up = blkfmt(g("w_up"))
    wd = g("w_down")
    w_down = np.ascontiguousarray(wd.reshape(8, 16, 128, 32, 128).transpose(0, 3, 2, 1, 4)).reshape(-1, 4096)
    w_gate = blkfmt(g("w_ple_gate"))
    w_ple = blkfmt(g("w_ple"))
    wo = g("w_out")
    lnp = np.stack([g(k).reshape(32, 128).T for k in ["ln1_g", "ln1_b", "ln2_g", "ln2_b", "ln3_g", "ln3_b"]], axis=1)
    lnp = np.ascontiguousarray(lnp)
    per_half = []
    for hf in range(2):
        cs = slice(hf * 1024, (hf + 1) * 1024)
        cols = []
        cols += list(range(D_SSM + hf * 1024, D_SSM + (hf + 1) * 1024))
        cols += list(range(2 * D_SSM + hf * 512, 2 * D_SSM + (hf + 1) * 512))
        cols += list(range(2 * D_SSM + 1024 + hf * 512, 2 * D_SSM + 1024 + (hf + 1) * 512))
        conv_cols = [c - D_SSM for c in cols]
        for part in range(3):
            cols += list(range(OFF_R + part * 2048 + hf * 1024, OFF_R + part * 2048 + (hf + 1) * 1024))
        lo = OFF_R + 3 * 2048
        wcols = np.zeros((4096, NBLK * 128), f)
        wcols[:, :40 * 128] = w_in[:, cols]
        wcols[:, 40 * 128:40 * 128 + 96] = w_in[:, lo:lo + 96]
        wcols[:, 41 * 128:41 * 128 + 96] = w_in[:, lo + 96:lo + 192]
        wcols[:, 42 * 128:44 * 128] = w_in[:, lo + 192:lo + 448]
        wcols[:, 44 * 128:52 * 128] = w_in[:, hf * 1024:(hf + 1) * 1024]
        dt0 = D_SSM + 4096 + hf * 16
        wcols[:, 52 * 128:52 * 128 + 16] = w_in[:, dt0:dt0 + 16]
        w_in_c = blkfmt(wcols)
        cw = np.ascontiguousarray(g("conv_w")[:, conv_cols].T.reshape(16, 128, 4).transpose(1, 0, 2))
        cb = np.ascontiguousarray(g("conv_b")[conv_cols].reshape(16, 128).T)
        mu_full = g("rwkv_mu")
        mu_cols = np.zeros((28, 128), f)
        for part in range(3):
            mu_cols[part * 8:(part + 1) * 8] = mu_full[part * 2048 + hf * 1024:part * 2048 + (hf + 1) * 1024].reshape(8, 128)
        mu_cols[24, :96] = mu_full[6144:6240]
        mu_cols[25, :96] = mu_full[6240:6336]
        mu_cols[26:28] = mu_full[6336:6592].reshape(2, 128)
        mu_c = np.ascontiguousarray(mu_cols.T)
        chp = np.stack([g(k).reshape(-1)[cs].reshape(8, 128).T for k in ["w0", "a0", "k_k", "k_a", "r_k"]], axis=1)
        chp = np.ascontiguousarray(chp)
        Dfull = np.repeat(g("D_skip")[hf * 16:(hf + 1) * 16], 64)
        tmb = np.stack([g("ssm_norm_g")[cs], Dfull, g("gn_g")[cs], g("gn_b")[cs]], axis=0)
        tmb = np.ascontiguousarray(np.broadcast_to(tmb[None], (128, 4, 1024)))
        dtp = np.stack([g("dt_bias")[hf * 16:(hf + 1) * 16], g("A_log")[hf * 16:(hf + 1) * 16]], axis=0)
        dtp = np.ascontiguousarray(np.broadcast_to(dtp[None], (128, 2, 16)))
        wdb = np.ascontiguousarray(g("w_decay_b")[:, cs])
        wab = np.ascontiguousarray(g("w_aaa_b")[:, cs])
        wgb = np.ascontiguousarray(g("w_gate_b")[:, cs].reshape(2, 128, 1024).transpose(1, 0, 2))
        sel = np.zeros((128, 2), f)
        sel[:, hf] = 1.0
        per_half.append(dict(w_in=w_in_c, cw=cw, cb=cb, mu=mu_c, chp=chp, tmb=tmb, dtp=dtp, wdb=wdb, wab=wab, wgb=wgb, sel=sel))
    perm = np.concatenate([np.arange(0, 1024), np.arange(2048, 3072), np.arange(1024, 2048), np.arange(3072, 4096)])
    w_out = blkfmt(wo[perm])
    shared = dict(w_out=w_out, w_up=w_up, w_down=w_down, w_gate=w_gate, w_ple=w_ple, lnp=lnp)
    in_maps = []
    for c in range(8):
        b, hf = c // 2, c % 2
        xT = np.ascontiguousarray(x[b].T)
        m = dict(shared)
        m.update(per_half[hf])
        m["xT"] = xT
        m["xTr"] = np.ascontiguousarray(xT[:, hf * 2048:(hf + 1) * 2048])
        m["pT"] = np.ascontiguousarray(p[b].T[:, hf * 2048:(hf + 1) * 2048].reshape(2, 128, 2048).transpose(1, 0, 2))
        in_maps.append(m)
    return in_maps


def kernel(**inputs):
    global _NC
    in_maps = _prep(inputs)
    if _NC is None:
        _NC = build_nc()
    res = run_bass_kernel_spmd(_NC, in_maps, core_ids=list(range(8)))
    out = np.empty((4, SEQ, D), np.float32)
    for c in range(8):
        b, hf = c // 2, c % 2
        out[b, hf * 2048:(hf + 1) * 2048, :] = np.asarray(res.results[c]["outT"]).T
    return out
```
